# Optimizing a Trainium2 kernel written in Bass

```python
import math
import jax, jax.numpy as jnp
from jax import lax
import numpy as np

D_MODEL = 1024
BATCH = 4
SEQ = 4096
DEPTH = 4

N_MIXERS = 3
HEAD_DIM = 64
BLOCK = 128
D_FF = 4 * D_MODEL
EPS = 1e-6

A_HEADS = D_MODEL // HEAD_DIM
A_GROUPS = ((128, 1, 6), (512, 4, 5), (2048, 16, 5))
B_HEADS = D_MODEL // (2 * HEAD_DIM)
C_HEADS = D_MODEL // HEAD_DIM
C_KV_HEADS = 4
C_GROUP = C_HEADS // C_KV_HEADS
GRID_W = 64
ROPE_THETA = 10000.0
AXIS_ROPE_DIM = HEAD_DIM // 2
NUM_BUCKETS = 32
REL_MAX_DISTANCE = 1024
REL_BIAS_HEADS = A_HEADS

kernel_name = "hybrid_dilated_diff_axial_gqa_encoder"


def rms_norm(x, g):
    xf = x.astype(jnp.float32)
    y = xf * lax.rsqrt(jnp.mean(xf * xf, axis=-1, keepdims=True) + EPS)
    return (y * g.astype(jnp.float32)).astype(x.dtype)


def t5_bucket(rel):
    nb = NUM_BUCKETS // 2
    max_exact = nb // 2
    side = jnp.where(rel > 0, nb, 0)
    n = jnp.abs(rel)
    nf = jnp.maximum(n, 1).astype(jnp.float32)
    large = max_exact + (jnp.log(nf / max_exact) / math.log(REL_MAX_DISTANCE / max_exact)
                         * (nb - max_exact)).astype(jnp.int32)
    large = jnp.minimum(large, nb - 1)
    return side + jnp.where(n < max_exact, n, large)


def dilated_window_attention(xn, w_qkv, w_o, rel_bias):
    B, S, _ = xn.shape
    nblk = S // BLOCK
    starts = jnp.arange(nblk, dtype=jnp.int32) * BLOCK
    qkv = xn @ w_qkv
    q, k, v = jnp.split(qkv, 3, axis=-1)
    q = q.reshape(B, S, A_HEADS, HEAD_DIM) * (HEAD_DIM ** -0.5)
    k = k.reshape(B, S, A_HEADS, HEAD_DIM)
    v = v.reshape(B, S, A_HEADS, HEAD_DIM)
    outs, glses = [], []
    h0 = 0
    for (win, dil, nh) in A_GROUPS:
        qg, kg, vg = q[:, :, h0:h0 + nh], k[:, :, h0:h0 + nh], v[:, :, h0:h0 + nh]
        n_side = win // (2 * dil)
        offs = jnp.arange(-n_side, n_side + 1, dtype=jnp.int32) * dil
        bias = rel_bias[t5_bucket(offs), h0:h0 + nh].astype(jnp.float32).T
        qb = qg.reshape(B, nblk, BLOCK, nh, HEAD_DIM).transpose(1, 0, 2, 3, 4)

        def block_fn(args, kg=kg, vg=vg, offs=offs, bias=bias):
            q_blk, start = args
            pos = start + jnp.arange(BLOCK, dtype=jnp.int32)[:, None] + offs[None, :]
            valid = (pos >= 0) & (pos < S)
            idx = jnp.clip(pos, 0, S - 1)
            k_sel = kg[:, idx]
            v_sel = vg[:, idx]
            logits = jnp.einsum('bqhd,bqkhd->bhqk', q_blk, k_sel).astype(jnp.float32)
            logits = logits + bias[None, :, None, :]
            logits = jnp.where(valid[None, None], logits, -jnp.inf)
            lse = jax.nn.logsumexp(logits, axis=-1)
            p = jnp.exp(logits - lse[..., None])
            o = jnp.einsum('bhqk,bqkhd->bqhd', p.astype(vg.dtype), v_sel)
            return o, lse

        o, lse = lax.map(block_fn, (qb, starts))
        o = o.transpose(1, 0, 2, 3, 4).reshape(B, S, nh, HEAD_DIM)
        lse = lse.transpose(1, 0, 3, 2).reshape(B, S, nh)
        outs.append(o)
        glses.append(jax.nn.logsumexp(lse, axis=-1) - math.log(nh))
        h0 += nh
    alpha = jax.nn.softmax(jnp.stack(glses, axis=-1), axis=-1)
    n_groups = len(A_GROUPS)
    o = jnp.concatenate(
        [outs[g] * (n_groups * alpha[..., g])[..., None, None].astype(outs[g].dtype)
         for g in range(n_groups)], axis=2)
    return o.reshape(B, S, A_HEADS * HEAD_DIM) @ w_o


def differential_attention(xn, w_qkv, lam, subln_g, w_o, rel_bias, lambda_init):
    B, S, _ = xn.shape
    nblk = S // BLOCK
    starts = jnp.arange(nblk, dtype=jnp.int32) * BLOCK
    qkv = xn @ w_qkv
    q, k, v = jnp.split(qkv, 3, axis=-1)
    q = q.reshape(B, S, B_HEADS, 2, HEAD_DIM) * (HEAD_DIM ** -0.5)
    k = k.reshape(B, S, B_HEADS, 2, HEAD_DIM)
    v = v.reshape(B, S, B_HEADS, 2 * HEAD_DIM)
    lamf = lam.astype(jnp.float32)
    lam_full = (jnp.exp(jnp.sum(lamf[0] * lamf[1])) - jnp.exp(jnp.sum(lamf[2] * lamf[3]))
                + lambda_init)
    keys = jnp.arange(S, dtype=jnp.int32)
    qb = q.reshape(B, nblk, BLOCK, B_HEADS, 2, HEAD_DIM).transpose(1, 0, 2, 3, 4, 5)

    def block_fn(args):
        q_blk, start = args
        rel = keys[None, :] - (start + jnp.arange(BLOCK, dtype=jnp.int32))[:, None]
        bias = rel_bias[t5_bucket(rel)].astype(jnp.float32)
        bias = bias.reshape(BLOCK, S, B_HEADS, 2).transpose(2, 3, 0, 1)
        logits = jnp.einsum('bqhjd,bkhjd->bhjqk', q_blk, k).astype(jnp.float32) + bias[None]
        p = jax.nn.softmax(logits, axis=-1)
        a = p[:, :, 0] - lam_full * p[:, :, 1]
        return jnp.einsum('bhqk,bkhe->bqhe', a.astype(v.dtype), v)

    o = lax.map(block_fn, (qb, starts))
    o = o.transpose(1, 0, 2, 3, 4).reshape(B, S, B_HEADS, 2 * HEAD_DIM)
    o = rms_norm(o, subln_g) * (1.0 - lambda_init)
    return o.reshape(B, S, B_HEADS * 2 * HEAD_DIM) @ w_o


def axial_rope_tables(S):
    n_rows = S // GRID_W
    row = jnp.repeat(jnp.arange(n_rows, dtype=jnp.int32), GRID_W).astype(jnp.float32)
    col = jnp.tile(jnp.arange(GRID_W, dtype=jnp.int32), n_rows).astype(jnp.float32)
    half = AXIS_ROPE_DIM // 2
    inv = ROPE_THETA ** (-jnp.arange(half, dtype=jnp.float32) / half)
    ang = jnp.concatenate([row[:, None] * inv, col[:, None] * inv], axis=-1)
    return jnp.cos(ang), jnp.sin(ang)


def apply_axial_rope(x, cos, sin):
    B, S, H, _ = x.shape
    half = AXIS_ROPE_DIM // 2
    xs = x.astype(jnp.float32).reshape(B, S, H, 2, 2, half)
    x1, x2 = xs[..., 0, :], xs[..., 1, :]
    c = cos.reshape(S, 1, 2, half)
    s = sin.reshape(S, 1, 2, half)
    out = jnp.stack([x1 * c - x2 * s, x2 * c + x1 * s], axis=-2)
    return out.reshape(B, S, H, HEAD_DIM).astype(x.dtype)


def axial_gqa_attention(xn, w_qkv, q_norm_g, k_norm_g, w_o, cos, sin):
    B, S, _ = xn.shape
    nblk = S // BLOCK
    qkv = xn @ w_qkv
    q = qkv[..., :C_HEADS * HEAD_DIM].reshape(B, S, C_HEADS, HEAD_DIM)
    k = qkv[..., C_HEADS * HEAD_DIM:(C_HEADS + C_KV_HEADS) * HEAD_DIM].reshape(B, S, C_KV_HEADS, HEAD_DIM)
    v = qkv[..., (C_HEADS + C_KV_HEADS) * HEAD_DIM:].reshape(B, S, C_KV_HEADS, HEAD_DIM)
    q = apply_axial_rope(rms_norm(q, q_norm_g), cos, sin) * (HEAD_DIM ** -0.5)
    k = apply_axial_rope(rms_norm(k, k_norm_g), cos, sin)
    qb = q.reshape(B, nblk, BLOCK, C_KV_HEADS, C_GROUP, HEAD_DIM).transpose(1, 0, 2, 3, 4, 5)

    def block_fn(q_blk):
        logits = jnp.einsum('bqkgd,bskd->bkgqs', q_blk, k).astype(jnp.float32)
        p = jax.nn.softmax(logits, axis=-1)
        return jnp.einsum('bkgqs,bskd->bqkgd', p.astype(v.dtype), v)

    o = lax.map(block_fn, qb)
    o = o.transpose(1, 0, 2, 3, 4, 5).reshape(B, S, C_HEADS * HEAD_DIM)
    return o @ w_o


def sq_relu_mlp(xn, w_in, w_out):
    h = jax.nn.relu(xn @ w_in)
    return (h * h) @ w_out


def lambda_init_fn(layer_idx):
    return 0.8 - 0.6 * math.exp(-0.3 * layer_idx)


def setup_inputs(seed: int = 0) -> dict:
    key = jax.random.key(seed)
    ks = jax.random.split(key, 24)
    n_a = len(range(0, DEPTH, N_MIXERS))
    n_b = len(range(1, DEPTH, N_MIXERS))
    n_c = len(range(2, DEPTH, N_MIXERS))

    def nrm(k, shape, scale):
        return jax.random.normal(k, shape, jnp.float32) * scale

    d_attn = A_HEADS * HEAD_DIM
    c_qkv = (C_HEADS + 2 * C_KV_HEADS) * HEAD_DIM
    return {
        "x": nrm(ks[0], (BATCH, SEQ, D_MODEL), 1.0),
        "rel_bias": nrm(ks[1], (NUM_BUCKETS, REL_BIAS_HEADS), 0.3),
        "norm_mix_g": 1.0 + nrm(ks[2], (DEPTH, D_MODEL), 0.02),
        "norm_mlp_g": 1.0 + nrm(ks[3], (DEPTH, D_MODEL), 0.02),
        "norm_final_g": 1.0 + nrm(ks[4], (D_MODEL,), 0.02),
        "a_w_qkv": nrm(ks[5], (n_a, D_MODEL, 3 * d_attn), D_MODEL ** -0.5),
        "a_w_o": nrm(ks[6], (n_a, d_attn, D_MODEL), d_attn ** -0.5),
        "b_w_qkv": nrm(ks[7], (n_b, D_MODEL, 3 * d_attn), D_MODEL ** -0.5),
        "b_lambda": nrm(ks[8], (n_b, 4, HEAD_DIM), 0.1),
        "b_subln_g": 1.0 + nrm(ks[9], (n_b, 2 * HEAD_DIM), 0.02),
        "b_w_o": nrm(ks[10], (n_b, d_attn, D_MODEL), d_attn ** -0.5),
        "c_w_qkv": nrm(ks[11], (n_c, D_MODEL, c_qkv), D_MODEL ** -0.5),
        "c_q_norm_g": 1.0 + nrm(ks[12], (n_c, HEAD_DIM), 0.02),
        "c_k_norm_g": 1.0 + nrm(ks[13], (n_c, HEAD_DIM), 0.02),
        "c_w_o": nrm(ks[14], (n_c, C_HEADS * HEAD_DIM, D_MODEL), (C_HEADS * HEAD_DIM) ** -0.5),
        "mlp_w_in": nrm(ks[15], (DEPTH, D_MODEL, D_FF), D_MODEL ** -0.5),
        "mlp_w_out": nrm(ks[16], (DEPTH, D_FF, D_MODEL), D_FF ** -0.5),
    }


def reference(x, rel_bias, norm_mix_g, norm_mlp_g, norm_final_g, a_w_qkv, a_w_o,
              b_w_qkv, b_lambda, b_subln_g, b_w_o, c_w_qkv, c_q_norm_g, c_k_norm_g,
              c_w_o, mlp_w_in, mlp_w_out):
    S = x.shape[1]
    cos, sin = axial_rope_tables(S)
    h = x
    for i in range(DEPTH):
        kind, j = i % N_MIXERS, i // N_MIXERS
        hn = rms_norm(h, norm_mix_g[i])
        if kind == 0:
            mix = dilated_window_attention(hn, a_w_qkv[j], a_w_o[j], rel_bias)
        elif kind == 1:
            mix = differential_attention(hn, b_w_qkv[j], b_lambda[j], b_subln_g[j], b_w_o[j],
                                         rel_bias, lambda_init_fn(i))
        else:
            mix = axial_gqa_attention(hn, c_w_qkv[j], c_q_norm_g[j], c_k_norm_g[j], c_w_o[j],
                                      cos, sin)
        h = h + mix
        h = h + sq_relu_mlp(rms_norm(h, norm_mlp_g[i]), mlp_w_in[i], mlp_w_out[i])
    return rms_norm(h, norm_final_g)
```

```python
import math
from contextlib import ExitStack

import numpy as np
import concourse.bass as bass
import concourse.mybir as mybir
from concourse.bass_utils import run_bass_kernel_spmd

F32 = mybir.dt.float32
BF16 = mybir.dt.bfloat16
AF = mybir.ActivationFunctionType
ALU = mybir.AluOpType

D = 1024
NCH = 8
DFF = 4096
NFC = 32
EPS = 1e-6
TT = 512


import os as _os
PTT = "dve" if _os.environ.get("K_NOPTT", "0") == "1" else "pool"
PMS = "pool" if _os.environ.get("K_POOLMS", "0") == "1" else "dve"
_UID = [0]


def _uid(name):
    _UID[0] += 1
    return f"{name}_u{_UID[0]}"


class Buf:
    def __init__(self, name):
        self.name = name
        self.w = {}
        self.r = {}


class Sched:
    ENGS = ("pe", "act", "dve", "pool", "sp")

    def __init__(self, nc, stack, n_dma_sems=60):
        self.nc = nc
        self.sem = {e: stack.enter_context(nc.semaphore(f"sem_{e}")) for e in self.ENGS}
        self.cnt = {e: 0 for e in self.ENGS}
        self.dsem = [stack.enter_context(nc.semaphore(f"sem_dma{i}")) for i in range(n_dma_sems)]
        self.dcnt = [0] * n_dma_sems
        self.dnext = 0
        self.n_sw = 16
        self.dnext_sw = 0
        self.waited = {}
        self.q = {e: [] for e in self.ENGS}
        self.semobj = {}
        self.nblocks = 0

    def _key(self, sem):
        k = id(sem)
        self.semobj[k] = sem
        return k

    def _wait(self, e, toks):
        for k, val in toks.items():
            if self.waited.get((e, k), 0) >= val:
                continue
            self.waited[(e, k)] = val
            sem = self.semobj[k]
            self.q[e].append(lambda eng, sem=sem, val=val: eng.wait_ge(sem, val))

    def _deps(self, e, reads, writes, pwrites):
        toks = {}

        def add(d):
            for k, v in d.items():
                if v > toks.get(k, 0):
                    toks[k] = v
        for b in reads:
            add(b.w)
        for b in writes:
            add(b.w)
            add(b.r)
        for b in pwrites:
            add(b.r)
        if e == "pe":
            toks.pop(self._key(self.sem[e]), None)
        return toks

    def _commit(self, tok, reads, writes, pwrites=()):
        k = self._key(tok[0])
        for b in reads:
            if b.r.get(k, 0) < tok[1]:
                b.r[k] = tok[1]
        for b in writes:
            b.w = {k: tok[1]}
            b.r = {}
        for b in pwrites:
            if b.w.get(k, 0) < tok[1]:
                b.w[k] = tok[1]

    def op(self, e, fn, reads=(), writes=(), pwrites=()):
        self._wait(e, self._deps(e, reads, writes, pwrites))
        self.cnt[e] += 1
        sem = self.sem[e]
        self.q[e].append(lambda eng, fn=fn, sem=sem: fn(eng).then_inc(sem, 1))
        self._commit((sem, self.cnt[e]), reads, writes, pwrites)

    def mm(self, mms, reads=(), writes=(), pwrites=(), start=True, stop=True):
        e = "pe"
        self._wait(e, self._deps(e, reads, writes, pwrites))
        self.cnt[e] += 1
        sem = self.sem[e]
        n = len(mms)

        def run(eng, mms=mms, sem=sem, n=n, start=start, stop=stop):
            for i, (o, l, r) in enumerate(mms):
                ins = eng.matmul(o, l, r, start=(start and i == 0), stop=(stop and i == n - 1))
            ins.then_inc(sem, 1)
        self.q[e].append(run)
        self._commit((sem, self.cnt[e]), reads, writes, pwrites)

    def dma(self, e, out, in_, reads=(), writes=(), pwrites=()):
        toks = self._deps(e, reads, writes, pwrites)
        if e == "pool":
            i = self.dnext_sw
            self.dnext_sw = (self.dnext_sw + 1) % self.n_sw
        else:
            i = self.n_sw + self.dnext
            self.dnext = (self.dnext + 1) % (len(self.dsem) - self.n_sw)
        sem = self.dsem[i]
        k = self._key(sem)
        if self.dcnt[i] > toks.get(k, 0):
            toks[k] = self.dcnt[i]
        self._wait(e, toks)
        self.dcnt[i] += 16
        self.q[e].append(lambda eng, out=out, in_=in_, sem=sem:
                         eng.dma_start(out=out, in_=in_).then_inc(sem, 16))
        self._commit((sem, self.dcnt[i]), reads, writes, pwrites)

    def flush(self, wait_bufs=(), wait_all=False):
        toks = {}
        for i, sem in enumerate(self.dsem):
            if self.dcnt[i] > 0 and (wait_all or i >= self.n_sw):
                toks[self._key(sem)] = self.dcnt[i]
        for b in wait_bufs:
            for k, v in b.w.items():
                if v > toks.get(k, 0):
                    toks[k] = v
        if toks:
            self._wait("sp", toks)
        q = self.q
        self.q = {e: [] for e in self.ENGS}
        self.nblocks += 1
        with self.nc.Block() as block:
            @block.tensor
            def _(eng):
                for f in q["pe"]:
                    f(eng)

            @block.scalar
            def _(eng):
                for f in q["act"]:
                    f(eng)

            @block.vector
            def _(eng):
                for f in q["dve"]:
                    f(eng)

            @block.gpsimd
            def _(eng):
                for f in q["pool"]:
                    f(eng)

            @block.sync
            def _(eng):
                for f in q["sp"]:
                    f(eng)


def emit_norm(S, T0, hx, hxb, sq, sqb, hn, hnb, gcol, ones, ps, psb, rstd, rstdb, gcolb, onesb):
    S.op("act", lambda e: e.activation(out=sq[:, :, :], in_=hx[:, :, :], func=AF.Square),
         reads=[hxb], writes=[sqb])
    S.mm([(ps[:, :], ones[:, :], sq[:, k, :]) for k in range(NCH)], reads=[sqb, onesb], writes=[psb])
    S.op("act", lambda e: e.activation(out=rstd[:, :], in_=ps[:, :], func=AF.Sqrt,
                                       scale=1.0 / D, bias=EPS),
         reads=[psb], writes=[rstdb])
    S.op("dve", lambda e: e.reciprocal(out=rstd[:, :], in_=rstd[:, :]),
         reads=[rstdb], writes=[rstdb])
    for k in range(NCH):
        S.op("dve", lambda e, k=k: e.scalar_tensor_tensor(
            out=hn[:, k, :], in0=hx[:, k, :], scalar=gcol[:, k:k + 1], in1=rstd[:, :],
            op0=ALU.mult, op1=ALU.mult), reads=[hxb, rstdb, gcolb], pwrites=[hnb])


def phase_mlp(nc, S, T, hT, hTb, w_in_bf, w_out_bf, wb, gnorm_dram):
    ntile = T // TT
    with ExitStack() as st:
        sb = lambda name, shape, dt: st.enter_context(nc.sbuf_tensor(_uid(name), shape, dt))
        win = sb("win", [128, NCH, DFF], BF16)
        wout = sb("wout", [128, NFC, D], BF16)
        hx = [sb(f"hx{i}", [128, NCH, TT], F32) for i in range(2)]
        hn = sb("hn", [128, NCH, TT], BF16)
        u = sb("u", [128, NFC, TT], BF16)
        r = [sb(f"r{i}", [128, TT], BF16) for i in range(2)]
        rstd = sb("rstd", [128, TT], F32)
        gcol = sb("gcol", [128, NCH], F32)
        ones = sb("ones", [128, 128], BF16)
        ps = [st.enter_context(nc.psum_tensor(_uid(f"ps{i}"), [128, TT], F32)) for i in range(8)]
        B = lambda n: Buf(n)
        winb, woutb, hnb, ub, rstdb, gcolb, onesb = (B("win"), B("wout"), B("hn"), B("u"),
                                                     B("rstd"), B("gcol"), B("ones"))
        hxb = [B("hx0"), B("hx1")]
        rb = [B("r0"), B("r1")]
        psb = [B(f"ps{i}") for i in range(8)]
        ucb = [B(f"u{c}") for c in range(NFC)]

        S.op(PMS, lambda e: e.memset(ones[:, :], 1.0), writes=[onesb])
        S.dma("sp", gcol[:, :], gnorm_dram, writes=[gcolb])
        def load(t):
            S.dma("sp", hx[t % 2][:, :, :],
                  hT[:, t * TT:(t + 1) * TT].rearrange("(k p) n -> p k n", p=128),
                  reads=[hTb[t]], writes=[hxb[t % 2]])

        load(0)
        for k in range(NCH):
            S.dma("sp", win[:, k, :], w_in_bf[k * 128:(k + 1) * 128, :], reads=[wb[0][k]], pwrites=[winb])
        for c4 in range(0, NFC, 4):
            S.dma("sp", wout[:, c4:c4 + 4, :],
                  w_out_bf[c4 * 128:(c4 + 4) * 128, :].rearrange("(c p) n -> p c n", p=128),
                  reads=[wb[1][c4 // 4]], pwrites=[woutb])

        for t in range(ntile):
            if t + 1 < ntile:
                load(t + 1)
            x = hx[t % 2]
            xb = hxb[t % 2]
            emit_norm(S, t, x, xb, u[:, 0:NCH, :], ub, hn, hnb, gcol, ones, ps[7], psb[7],
                      rstd, rstdb, gcolb, onesb)
            for c in range(NFC):
                p = ps[c % 4]
                pb = psb[c % 4]
                S.mm([(p[:, :], win[:, k, c * 128:(c + 1) * 128], hn[:, k, :]) for k in range(NCH)],
                     reads=[hnb, winb], writes=[pb])
                rr, rrb = r[c % 2], rb[c % 2]
                S.op("act", lambda e, p=p, rr=rr: e.activation(out=rr[:, :], in_=p[:, :], func=AF.Relu),
                     reads=[pb], writes=[rrb])
                S.op("dve", lambda e, p=p, rr=rr, c=c: e.scalar_tensor_tensor(
                    out=u[:, c, :], in0=p[:, :], scalar=0.0, in1=rr[:, :],
                    op0=ALU.max, op1=ALU.mult), reads=[pb, rrb], writes=[ucb[c]])
            for o in range(NCH):
                p = ps[4 + o % 3]
                pb = psb[4 + o % 3]
                S.mm([(p[:, :], wout[:, c, o * 128:(o + 1) * 128], u[:, c, :]) for c in range(NFC)],
                     reads=ucb + [woutb], writes=[pb])
                S.op("dve", lambda e, p=p, x=x, o=o: e.tensor_tensor(
                    out=x[:, o, :], in0=p[:, :], in1=x[:, o, :], op=ALU.add),
                    reads=[pb, xb], pwrites=[xb])
            S._commit((S.sem["pe"], S.cnt["pe"]), ucb + [ub], [])
            S.dma("sp", hT[:, t * TT:(t + 1) * TT].rearrange("(k p) n -> p k n", p=128),
                  x[:, :, :], reads=[xb], writes=[hTb[t]])
        S.flush()


PAD = 1024
GW = 2266
GC = 1069
NEAR_LO, NEAR_HI = -686, 1070


def phase_qkv_ab(nc, S, T, hT, hTb, w_bf, wb, g_dram, QT, QTb, KT, KTb, V, Vb, w_f32=None, h_in=None, h_inb=None):
    ntile = T // TT
    with ExitStack() as st:
        sb = lambda name, shape, dt: st.enter_context(nc.sbuf_tensor(_uid(name), shape, dt))
        w = sb("wqkv", [128, NCH, 3072], BF16)
        hx = [sb(f"hx{i}", [128, NCH, TT], F32) for i in range(2)]
        sq = sb("sq", [128, NCH, TT], BF16)
        hn = sb("hn", [128, NCH, TT], BF16)
        rstd = sb("rstd", [128, TT], F32)
        gcol = sb("gcol", [128, NCH], F32)
        ones = sb("ones", [128, 128], BF16)
        qk = [sb(f"qk{i}", [128, 16, TT], BF16) for i in range(2)]
        vs = [sb(f"vs{i}", [128, 4, D], BF16) for i in range(2)]
        ps = [st.enter_context(nc.psum_tensor(_uid(f"ps{i}"), [128, TT], F32)) for i in range(8)]
        B = Buf
        wbuf, sqb, hnb, rstdb, gcolb, onesb = B("w"), B("sq"), B("hn"), B("rstd"), B("gcol"), B("ones")
        hxb = [B("hx0"), B("hx1")]
        qkb = [B("qk0"), B("qk1")]
        vsb = [B("vs0"), B("vs1")]
        psb = [B(f"ps{i}") for i in range(8)]
        S.op(PMS, lambda e: e.memset(ones[:, :], 1.0), writes=[onesb])
        S.dma("sp", gcol[:, :], g_dram, writes=[gcolb])
        def load(t):
            S.dma("sp", hx[t % 2][:, :, :],
                  (hT if h_in is None else h_in)[:, t * TT:(t + 1) * TT].rearrange("(k p) n -> p k n", p=128),
                  reads=[(hTb if h_in is None else h_inb)[t]], writes=[hxb[t % 2]])
        load(0)
        if w_f32 is None:
            for k in range(NCH):
                S.dma("sp", w[:, k, :], w_bf[k * 128:(k + 1) * 128, :], reads=[wb[k]], pwrites=[wbuf])
        else:
            wst = [sb(f"wst{i}", [128, 3072], F32) for i in range(2)]
            wstb = [B("wst0"), B("wst1")]
            for k in range(NCH):
                S.dma("sp", wst[k % 2][:, :], w_f32[k * 128:(k + 1) * 128, :], writes=[wstb[k % 2]])
                S.op("act", lambda e, k=k: e.activation(out=w[:, k, :], in_=wst[k % 2][:, :], func=AF.Copy),
                     reads=[wstb[k % 2]], pwrites=[wbuf])

        for t in range(ntile):
            if t + 1 < ntile:
                load(t + 1)
            x, xb = hx[t % 2], hxb[t % 2]
            emit_norm(S, t, x, xb, sq, sqb, hn, hnb, gcol, ones, ps[7], psb[7], rstd, rstdb, gcolb, onesb)
            q_, q_b = qk[t % 2], qkb[t % 2]
            for c in range(16):
                p, pb = ps[c % 4], psb[c % 4]
                S.mm([(p[:, :], w[:, k, c * 128:(c + 1) * 128], hn[:, k, :]) for k in range(NCH)],
                     reads=[hnb, wbuf], writes=[pb])
                if c < 8:
                    S.op("act", lambda e, p=p, q_=q_, c=c: e.activation(
                        out=q_[:, c, :], in_=p[:, :], func=AF.Copy, scale=0.125),
                        reads=[pb], pwrites=[q_b])
                else:
                    S.op("dve", lambda e, p=p, q_=q_, c=c: e.tensor_copy(out=q_[:, c, :], in_=p[:, :]),
                         reads=[pb], pwrites=[q_b])
            S.dma("sp", QT[:, t * TT:(t + 1) * TT].rearrange("(c p) n -> p c n", p=128),
                  q_[:, 0:8, :], reads=[q_b], pwrites=[QTb])
            S.dma("sp", KT[:, PAD + t * TT:PAD + (t + 1) * TT].rearrange("(c p) n -> p c n", p=128),
                  q_[:, 8:16, :], reads=[q_b], pwrites=[KTb])
            v_, v_b = vs[t % 2], vsb[t % 2]
            for tb in range(4):
                for half in range(2):
                    i = tb * 2 + half
                    p, pb = ps[4 + i % 3], psb[4 + i % 3]
                    S.mm([(p[:, :], hn[:, k, tb * 128:(tb + 1) * 128],
                           w[:, k, 2048 + half * 512:2048 + (half + 1) * 512]) for k in range(NCH)],
                         reads=[hnb, wbuf], writes=[pb])
                    if i % 2 == 0:
                        S.op("act", lambda e, p=p, v_=v_, tb=tb, half=half: e.activation(
                            out=v_[:, tb, half * 512:(half + 1) * 512], in_=p[:, :], func=AF.Copy),
                            reads=[pb], pwrites=[v_b])
                    else:
                        S.op("dve", lambda e, p=p, v_=v_, tb=tb, half=half: e.tensor_copy(
                            out=v_[:, tb, half * 512:(half + 1) * 512], in_=p[:, :]),
                            reads=[pb], pwrites=[v_b])
            S.dma("sp", V[PAD + t * TT:PAD + (t + 1) * TT, :].rearrange("(b p) f -> p b f", p=128),
                  v_[:, :, :], reads=[v_b], pwrites=[Vb])
            S._commit((S.sem["pe"], S.cnt["pe"]), [hnb, sqb], [])
        S.flush()


def phase_wo(nc, S, T, OT, OTb, w_bf, wb, hT, hTb, h_in=None, h_inb=None):
    ntile = T // TT
    with ExitStack() as st:
        sb = lambda name, shape, dt: st.enter_context(nc.sbuf_tensor(_uid(name), shape, dt))
        w = sb("wo", [128, NCH, D], BF16)
        hx = [sb(f"hx{i}", [128, NCH, TT], F32) for i in range(2)]
        ot = [sb(f"ot{i}", [128, NCH, TT], BF16) for i in range(2)]
        ps = [st.enter_context(nc.psum_tensor(_uid(f"ps{i}"), [128, TT], F32)) for i in range(8)]
        B = Buf
        wbuf = B("w")
        hxb = [B("hx0"), B("hx1")]
        otb = [B("ot0"), B("ot1")]
        psb = [B(f"ps{i}") for i in range(8)]
        def load(t):
            S.dma("sp", hx[t % 2][:, :, :],
                  (hT if h_in is None else h_in)[:, t * TT:(t + 1) * TT].rearrange("(k p) n -> p k n", p=128),
                  reads=[(hTb if h_in is None else h_inb)[t]], writes=[hxb[t % 2]])
            S.dma("sp", ot[t % 2][:, :, :],
                  OT[:, t * TT:(t + 1) * TT].rearrange("(k p) n -> p k n", p=128),
                  reads=[OTb], writes=[otb[t % 2]])
        load(0)
        for k in range(NCH):
            S.dma("sp", w[:, k, :], w_bf[k * 128:(k + 1) * 128, :], reads=[wb[k]], pwrites=[wbuf])

        for t in range(ntile):
            if t + 1 < ntile:
                load(t + 1)
            x, xb = hx[t % 2], hxb[t % 2]
            o_, o_b = ot[t % 2], otb[t % 2]
            for o in range(NCH):
                p, pb = ps[o % 8], psb[o % 8]
                S.mm([(p[:, :], w[:, k, o * 128:(o + 1) * 128], o_[:, k, :]) for k in range(NCH)],
                     reads=[o_b, wbuf], writes=[pb])
                S.op("dve", lambda e, p=p, x=x, o=o: e.tensor_tensor(
                    out=x[:, o, :], in0=p[:, :], in1=x[:, o, :], op=ALU.add),
                    reads=[pb, xb], pwrites=[xb])
            S.dma("sp", hT[:, t * TT:(t + 1) * TT].rearrange("(k p) n -> p k n", p=128),
                  x[:, :, :], reads=[xb], writes=[hTb[t]])
        S.flush()


def phase_final(nc, S, T, hT, hTb, g_dram, outT, outb):
    ntile = T // TT
    with ExitStack() as st:
        sb = lambda name, shape, dt: st.enter_context(nc.sbuf_tensor(_uid(name), shape, dt))
        hx = [sb(f"hx{i}", [128, NCH, TT], F32) for i in range(2)]
        ho = [sb(f"ho{i}", [128, NCH, TT], F32) for i in range(2)]
        sq = sb("sq", [128, NCH, TT], BF16)
        rstd = sb("rstd", [128, TT], F32)
        gcol = sb("gcol", [128, NCH], F32)
        ones = sb("ones", [128, 128], BF16)
        ps = [st.enter_context(nc.psum_tensor(_uid(f"ps{i}"), [128, TT], F32)) for i in range(2)]
        B = Buf
        sqb, rstdb, gcolb, onesb = B("sq"), B("rstd"), B("gcol"), B("ones")
        hxb = [B("hx0"), B("hx1")]
        hob = [B("ho0"), B("ho1")]
        psb = [B("ps0"), B("ps1")]
        S.op(PMS, lambda e: e.memset(ones[:, :], 1.0), writes=[onesb])
        S.dma("sp", gcol[:, :], g_dram, writes=[gcolb])

        def load(t):
            S.dma("sp", hx[t % 2][:, :, :],
                  hT[:, t * TT:(t + 1) * TT].rearrange("(k p) n -> p k n", p=128),
                  reads=[hTb[t]], writes=[hxb[t % 2]])
        load(0)
        for t in range(ntile):
            if t + 1 < ntile:
                load(t + 1)
            emit_norm(S, t, hx[t % 2], hxb[t % 2], sq, sqb, ho[t % 2], hob[t % 2], gcol, ones,
                      ps[t % 2], psb[t % 2], rstd, rstdb, gcolb, onesb)
            S.dma("sp", outT[:, t * TT:(t + 1) * TT].rearrange("(k p) n -> p k n", p=128),
                  ho[t % 2][:, :, :], reads=[hob[t % 2]], pwrites=[outb])
        S.flush(wait_bufs=[outb], wait_all=True)


def phase_attn_b(nc, S, T, QT, QTb, KT, KTb, V, Vb, OT, OTb, gtab, bconst_d, lam_d, gsub_d, lam_init):
    NKB, NQT = T // 128, T // TT
    with ExitStack() as st:
        sb = lambda name, shape, dt: st.enter_context(nc.sbuf_tensor(_uid(name), shape, dt))
        qt_ = [sb(f"qt{i}", [128, T], BF16) for i in range(2)]
        kt_ = [sb(f"kt{i}", [128, T], BF16) for i in range(2)]
        vv = [sb(f"vv{i}", [128, NKB, 128], BF16) for i in range(2)]
        gt = [sb(f"gt{i}", [128, 2, GW], F32) for i in range(2)]
        sbi = [sb(f"sbi{i}", [128, TT], F32) for i in range(4)]
        pt = [sb(f"pt{i}", [128, TT], BF16) for i in range(8)]
        ones = sb("ones", [128, 128], BF16)
        bconst = sb("bconst", [128, 32], F32)
        lam = sb("lam", [128, 256], F32)
        ltmp = sb("ltmp", [128, 64], F32)
        lsc = sb("lsc", [128, 8], F32)
        gsub = sb("gsub", [128, 2], F32)
        ef = [sb(f"ef{i}", [128, TT], F32) for i in range(5)]
        osq = sb("osq", [128, TT], BF16)
        ost = [sb(f"ost{i}", [128, TT], BF16) for i in range(2)]
        zs = [sb(f"zs{i}", [128, TT], F32) for i in range(2)]
        pvs = [sb(f"pvs{i}", [128, TT], F32) for i in range(2)]
        zsb = [Buf("zs0"), Buf("zs1")]
        pvsb = [Buf("pvs0"), Buf("pvs1")]
        deferred = []
        ps = [st.enter_context(nc.psum_tensor(_uid(f"ps{i}"), [128, TT], F32)) for i in range(8)]
        B = Buf
        qtb = [B("qt0"), B("qt1")]
        ktb = [B("kt0"), B("kt1")]
        vvb = [B("vv0"), B("vv1")]
        gtb = [B("gt0"), B("gt1")]
        sbib = [B(f"sbi{i}") for i in range(4)]
        ptb = [B(f"pt{i}") for i in range(8)]
        efb = [B(f"ef{i}") for i in range(5)]
        onesb, bcb, lamb, ltb, lscb, gsubb, osqb = (B("ones"), B("bc"), B("lam"), B("lt"), B("lsc"),
                                                    B("gsub"), B("osq"))
        ostb = [B("ost0"), B("ost1")]
        psb = [B(f"ps{i}") for i in range(8)]
        psS = [[ps[0], ps[1]], [ps[2], ps[3]]]
        psSb = [[psb[0], psb[1]], [psb[2], psb[3]]]
        PV, PVb = [ps[4], ps[5]], [psb[4], psb[5]]
        Z, Zb = [ps[6], ps[7]], [psb[6], psb[7]]

        S.op(PMS, lambda e: e.memset(ones[:, :], 1.0), writes=[onesb])
        S.dma("sp", bconst[:, :], bconst_d, writes=[bcb])
        S.dma("sp", lam[:, :], lam_d, writes=[lamb])
        S.dma("sp", gsub[:, 0:1], gsub_d, writes=[gsubb])
        S.op("dve", lambda e: e.scalar_tensor_tensor(out=ltmp[:, :], in0=lam[:, 0:64], scalar=1.0,
                                                     in1=lam[:, 64:128], op0=ALU.mult, op1=ALU.mult,
                                                     accum_out=lsc[:, 0:1]),
             reads=[lamb], writes=[ltb, lscb])
        S.op("dve", lambda e: e.scalar_tensor_tensor(out=ltmp[:, :], in0=lam[:, 128:192], scalar=1.0,
                                                     in1=lam[:, 192:256], op0=ALU.mult, op1=ALU.mult,
                                                     accum_out=lsc[:, 1:2]),
             reads=[lamb, lscb], writes=[ltb, lscb])
        S.op("act", lambda e: e.activation(out=lsc[:, 2:4], in_=lsc[:, 0:2], func=AF.Exp),
             reads=[lscb], writes=[lscb])
        S.op("dve", lambda e: e.tensor_tensor(out=lsc[:, 4:5], in0=lsc[:, 3:4], in1=lsc[:, 2:3],
                                              op=ALU.subtract), reads=[lscb], writes=[lscb])
        S.op("dve", lambda e: e.tensor_scalar(out=lsc[:, 5:6], in0=lsc[:, 4:5], scalar1=-float(lam_init),
                                              scalar2=None, op0=ALU.add), reads=[lscb], writes=[lscb])
        S.op("dve", lambda e: e.tensor_scalar(out=gsub[:, 1:2], in0=gsub[:, 0:1],
                                              scalar1=float(1.0 - lam_init), scalar2=None, op0=ALU.mult),
             reads=[gsubb], writes=[gsubb])
        neglam = lsc[:, 5:6]

        def load(h):
            i = h % 2
            S.dma("sp", qt_[i][:, :], QT[h * 128:(h + 1) * 128, :], reads=[QTb], writes=[qtb[i]])
            S.dma("sp", kt_[i][:, :], KT[h * 128:(h + 1) * 128, PAD:PAD + T], reads=[KTb], writes=[ktb[i]])
            S.dma("sp", vv[i][:, :, :],
                  V[PAD:PAD + T, h * 128:(h + 1) * 128].rearrange("(b p) f -> p b f", p=128),
                  reads=[Vb], writes=[vvb[i]])
            for j in range(2):
                S.dma("sp", gt[i][:, j, :], gtab[2 * h + j, :, :], pwrites=[gtb[i]],
                      reads=[])
        load(0)
        npt = 0
        nsb = 0
        for h in range(8):
            if h + 1 < 8:
                load(h + 1)
            i = h % 2
            q_, k_, v_, g_ = qt_[i], kt_[i], vv[i], gt[i]
            for qt in range(NQT):
                q0 = qt * TT

                def issue_S(kb):
                    for j in range(2):
                        S.mm([(psS[j][kb % 2][:, :], k_[j * 64:(j + 1) * 64, kb * 128:(kb + 1) * 128],
                               q_[j * 64:(j + 1) * 64, q0:q0 + TT])],
                             reads=[qtb[i], ktb[i]], writes=[psSb[j][kb % 2]])
                issue_S(0)
                for kb in range(NKB):
                    if kb + 1 < NKB:
                        issue_S(kb + 1)
                    d = kb * 128 - q0
                    pts = []
                    for j in range(2):
                        sp_, spb = psS[j][kb % 2], psSb[j][kb % 2]
                        p_, p_b = pt[npt % 8], ptb[npt % 8]
                        npt += 1
                        if NEAR_LO < d < NEAR_HI:
                            m0 = GC - d
                            s_, s_b = sbi[nsb % 4], sbib[nsb % 4]
                            nsb += 1
                            S.op("dve", lambda e, s_=s_, sp_=sp_, g_=g_, j=j, m0=m0: e.tensor_tensor(
                                out=s_[:, :], in0=sp_[:, :], in1=g_[:, j, m0:m0 + TT], op=ALU.add),
                                reads=[spb, gtb[i]], writes=[s_b])
                            S.op("act", lambda e, p_=p_, s_=s_: e.activation(
                                out=p_[:, :], in_=s_[:, :], func=AF.Exp), reads=[s_b], writes=[p_b])
                        else:
                            col = (16 if d > 0 else 0) + 2 * h + j
                            S.op("act", lambda e, p_=p_, sp_=sp_, col=col: e.activation(
                                out=p_[:, :], in_=sp_[:, :], func=AF.Exp, bias=bconst[:, col:col + 1]),
                                reads=[spb, bcb], writes=[p_b])
                        pts.append((p_, p_b))
                    for j in range(2):
                        p_, p_b = pts[j]
                        S.mm([(PV[j][:, :], v_[:, kb, :], p_[:, :])], reads=[p_b, vvb[i]],
                             writes=[PVb[j]] if kb == 0 else [], pwrites=[] if kb == 0 else [PVb[j]],
                             start=(kb == 0), stop=(kb == NKB - 1))
                    for j in range(2):
                        p_, p_b = pts[j]
                        S.mm([(Z[j][:, :], ones[:, :], p_[:, :])], reads=[p_b, onesb],
                             writes=[Zb[j]] if kb == 0 else [], pwrites=[] if kb == 0 else [Zb[j]],
                             start=(kb == 0), stop=(kb == NKB - 1))
                    if deferred and kb >= 1:
                        deferred.pop(0)(kb)
                while deferred:
                    deferred.pop(0)(NKB - 1)
                S.op("act", lambda e: e.activation(out=zs[0][:, :], in_=Z[0][:, :], func=AF.Copy),
                     reads=[Zb[0]], writes=[zsb[0]])
                S.op("dve", lambda e: e.tensor_copy(out=pvs[0][:, :], in_=PV[0][:, :]),
                     reads=[PVb[0]], writes=[pvsb[0]])
                S.op("act", lambda e: e.activation(out=zs[1][:, :], in_=Z[1][:, :], func=AF.Copy),
                     reads=[Zb[1]], writes=[zsb[1]])
                S.op("dve", lambda e: e.tensor_copy(out=pvs[1][:, :], in_=PV[1][:, :]),
                     reads=[PVb[1]], writes=[pvsb[1]])
                r0, r1, o0, t1, o = ef
                os_, os_b = ost[qt % 2], ostb[qt % 2]
                dst = OT[h * 128:(h + 1) * 128, q0:q0 + TT]

                def mk(os_=os_, os_b=os_b, dst=dst):
                    ops = []
                    ops.append(lambda kb: S.op("dve", lambda e: e.reciprocal(out=r0[:, :], in_=zs[0][:, :]),
                                               reads=[zsb[0]], writes=[efb[0]]))
                    ops.append(lambda kb: S.op("dve", lambda e: e.reciprocal(out=r1[:, :], in_=zs[1][:, :]),
                                               reads=[zsb[1]], writes=[efb[1]]))
                    ops.append(lambda kb: S.op("dve", lambda e: e.tensor_tensor(
                        out=o0[:, :], in0=pvs[0][:, :], in1=r0[:, :], op=ALU.mult),
                        reads=[pvsb[0], efb[0]], writes=[efb[2]]))
                    ops.append(lambda kb: S.op("dve", lambda e: e.tensor_tensor(
                        out=t1[:, :], in0=pvs[1][:, :], in1=r1[:, :], op=ALU.mult),
                        reads=[pvsb[1], efb[1]], writes=[efb[3]]))
                    ops.append(lambda kb: S.op("dve", lambda e: e.scalar_tensor_tensor(
                        out=o[:, :], in0=t1[:, :], scalar=neglam, in1=o0[:, :], op0=ALU.mult, op1=ALU.add),
                        reads=[efb[2], efb[3], lscb], writes=[efb[4]]))
                    ops.append(lambda kb: S.op("act", lambda e: e.activation(
                        out=osq[:, :], in_=o[:, :], func=AF.Square), reads=[efb[4]], writes=[osqb]))

                    def ones_mm(kb):
                        pm, pmb = psS[0][kb % 2], psSb[0][kb % 2]
                        S.mm([(pm[:, :], ones[:, :], osq[:, :])], reads=[osqb, onesb], writes=[pmb])
                        S.op("act", lambda e, pm=pm: e.activation(out=r0[:, :], in_=pm[:, :], func=AF.Sqrt,
                                                                  scale=1.0 / 128, bias=EPS),
                             reads=[pmb], writes=[efb[0]])
                    ops.append(ones_mm)
                    ops.append(lambda kb: S.op("dve", lambda e: e.reciprocal(out=r0[:, :], in_=r0[:, :]),
                                               reads=[efb[0]], writes=[efb[0]]))
                    ops.append(lambda kb: S.op("dve", lambda e: e.scalar_tensor_tensor(
                        out=os_[:, :], in0=o[:, :], scalar=gsub[:, 1:2], in1=r0[:, :],
                        op0=ALU.mult, op1=ALU.mult), reads=[efb[4], efb[0], gsubb], writes=[os_b]))
                    ops.append(lambda kb: S.dma("sp", dst, os_[:, :], reads=[os_b], pwrites=[OTb]))
                    return ops
                deferred = mk()
        while deferred:
            deferred.pop(0)(NKB - 1)
        S.flush()


def phase_qkv_c(nc, S, T, hT, hTb, w_bf, wb, g_dram, QT, QTb, KT, KTb, V, Vb,
                ctab_d, stab_d, pmat_d, qkg_d, h_in=None, h_inb=None):
    ntile = T // TT
    with ExitStack() as st:
        sb = lambda name, shape, dt: st.enter_context(nc.sbuf_tensor(_uid(name), shape, dt))
        w = sb("wqkv", [128, NCH, 1536], BF16)
        hx = [sb(f"hx{i}", [128, NCH, TT], F32) for i in range(2)]
        sq = sb("sq", [128, NCH, TT], BF16)
        hn = sb("hn", [128, NCH, TT], BF16)
        rstd = sb("rstd", [128, TT], F32)
        gcol = sb("gcol", [128, NCH], F32)
        ones = sb("ones", [128, 128], BF16)
        oblk = sb("oblk", [128, 128], BF16)
        pm32 = sb("pm32", [128, 128], F32)
        pmat = sb("pmat", [128, 128], BF16)
        qkg = sb("qkg", [128, 2], F32)
        ctab = sb("ctab", [128, T], F32)
        stab = sb("stab", [128, T], F32)
        sq2 = [sb(f"sq2{i}", [128, TT], BF16) for i in range(2)]
        qg = [sb(f"qg{i}", [128, TT], BF16) for i in range(2)]
        qgf = [sb(f"qgf{i}", [128, TT], F32) for i in range(2)]
        qgfb = [Buf("qgf0"), Buf("qgf1")]
        rs = [sb(f"rs{i}", [128, TT], F32) for i in range(2)]
        t1 = [sb(f"t1{i}", [128, TT], F32) for i in range(2)]
        t2 = [sb(f"t2{i}", [128, TT], F32) for i in range(2)]
        qk = [sb(f"qk{i}", [128, 10, TT], BF16) for i in range(2)]
        vs = [sb(f"vs{i}", [128, 4, 256], BF16) for i in range(2)]
        ps = [st.enter_context(nc.psum_tensor(_uid(f"ps{i}"), [128, TT], F32)) for i in range(8)]
        B = Buf
        wbuf, sqb, hnb, rstdb, gcolb, onesb, oblkb, pm32b, pmatb, qkgb, ctb, stb = (
            B("w"), B("sq"), B("hn"), B("rstd"), B("gcol"), B("ones"), B("oblk"), B("pm32"), B("pmat"),
            B("qkg"), B("ct"), B("st"))
        hxb = [B("hx0"), B("hx1")]
        sq2b = [B("a"), B("b")]
        qgb = [B("a"), B("b")]
        rsb = [B("a"), B("b")]
        t1b = [B("a"), B("b")]
        t2b = [B("a"), B("b")]
        qkb = [B("qk0"), B("qk1")]
        vsb = [B("vs0"), B("vs1")]
        psb = [B(f"ps{i}") for i in range(8)]
        S.op(PMS, lambda e: e.memset(ones[:, :], 1.0), writes=[onesb])
        S.op(PMS, lambda e: e.memset(oblk[:, :], 0.0), writes=[oblkb])
        if _os.environ.get("K_C2", "0") != "1":
            S.op(PMS, lambda e: e.memset(oblk[0:64, 0:64], 1.0), writes=[oblkb])
            S.op(PMS, lambda e: e.memset(oblk[64:128, 64:128], 1.0), writes=[oblkb])
        S.dma("sp", gcol[:, :], g_dram, writes=[gcolb])
        if _os.environ.get("K_C3", "0") == "2":
            S.dma("sp", pm32[:, :], pmat_d, writes=[pm32b])
            S.op("dve", lambda e: e.memset(pmat[:, :], 1.0), writes=[pmatb])
        elif _os.environ.get("K_C3", "0") != "1":
            S.dma("sp", pm32[:, :], pmat_d, writes=[pm32b])
            S.op("act", lambda e: e.activation(out=pmat[:, :], in_=pm32[:, :], func=AF.Copy),
                 reads=[pm32b], writes=[pmatb])
        else:
            S.op("dve", lambda e: e.memset(pmat[:, :], 1.0), writes=[pmatb])
        if _os.environ.get("K_C4", "0") != "1":
            S.dma("sp", qkg[:, :], qkg_d, writes=[qkgb])
        if _os.environ.get("K_C5", "0") != "1":
            S.dma("sp", ctab[:, :], ctab_d, writes=[ctb])
            S.dma("sp", stab[:, :], stab_d, writes=[stb])
        def load(t):
            S.dma("sp", hx[t % 2][:, :, :],
                  (hT if h_in is None else h_in)[:, t * TT:(t + 1) * TT].rearrange("(k p) n -> p k n", p=128),
                  reads=[(hTb if h_in is None else h_inb)[t]], writes=[hxb[t % 2]])
        load(0)
        for k in range(NCH):
            S.dma("sp", w[:, k, :], w_bf[k * 128:(k + 1) * 128, :], reads=[wb[k]], pwrites=[wbuf])

        n = 0
        for t in range(ntile):
            if t + 1 < ntile:
                load(t + 1)
            x, xb = hx[t % 2], hxb[t % 2]
            emit_norm(S, t, x, xb, sq, sqb, hn, hnb, gcol, ones, ps[7], psb[7], rstd, rstdb, gcolb, onesb)
            q_, q_b = qk[t % 2], qkb[t % 2]
            t0 = t * TT
            for c in range(10):
                isq = c < 8
                pA, pAb = ps[(2 * c) % 4], psb[(2 * c) % 4]
                pB, pBb = ps[(2 * c) % 4 + 1], psb[(2 * c) % 4 + 1]
                pC, pCb = ps[4 + c % 2], psb[4 + c % 2]
                i = n % 2
                n += 1
                gq = qkg[:, 0:1] if isq else qkg[:, 1:2]
                _cn = int(_os.environ.get('K_CN', '99'))
                if _cn > 0:
                    S.mm([(pA[:, :], w[:, k, c * 128:(c + 1) * 128], hn[:, k, :]) for k in range(NCH)],
                         reads=[hnb, wbuf], writes=[pAb])
                if _cn > 1:
                    S.op("act", lambda e, pA=pA, i=i: e.activation(out=sq2[i][:, :], in_=pA[:, :], func=AF.Square),
                         reads=[pAb], writes=[sq2b[i]])
                if _cn > 2:
                    S.op("act", lambda e, pA=pA, i=i, gq=gq: e.activation(
                        out=qgf[i][:, :], in_=pA[:, :], func=AF.Copy, scale=gq),
                        reads=[pAb, qkgb], writes=[qgfb[i]])
                    S.op("pool", lambda e, i=i: e.tensor_copy(out=qg[i][:, :], in_=qgf[i][:, :]),
                         reads=[qgfb[i]], writes=[qgb[i]])
                if _cn > 3:
                    S.mm([(pB[:, :], oblk[:, :], sq2[i][:, :])], reads=[sq2b[i], oblkb], writes=[pBb])
                if _cn > 4:
                    S.mm([(pC[:, :], pmat[:, :], qg[i][:, :])], reads=[qgb[i], pmatb], writes=[pCb])
                if _cn > 5:
                    S.op("act", lambda e, pB=pB, i=i: e.activation(out=rs[i][:, :], in_=pB[:, :], func=AF.Sqrt,
                                                                  scale=1.0 / 64, bias=EPS),
                         reads=[pBb], writes=[rsb[i]])
                if _cn > 6:
                    S.op("dve", lambda e, i=i: e.reciprocal(out=rs[i][:, :], in_=rs[i][:, :]),
                         reads=[rsb[i]], writes=[rsb[i]])
                if _cn > 7:
                    S.op("pool", lambda e, i=i, t0=t0: e.tensor_tensor(
                        out=t1[i][:, :], in0=qgf[i][:, :], in1=ctab[:, t0:t0 + TT], op=ALU.mult),
                        reads=[qgfb[i], ctb], writes=[t1b[i]])
                if _cn > 8:
                    S.op("dve", lambda e, pC=pC, i=i, t0=t0: e.tensor_tensor(
                        out=t2[i][:, :], in0=pC[:, :], in1=stab[:, t0:t0 + TT], op=ALU.mult),
                        reads=[pCb, stb], writes=[t2b[i]])
                if _cn > 9:
                    S.op(PTT, lambda e, i=i: e.tensor_tensor(out=t1[i][:, :], in0=t1[i][:, :], in1=t2[i][:, :],
                                                               op=ALU.add),
                         reads=[t1b[i], t2b[i]], writes=[t1b[i]])
                if _cn > 10:
                    S.op("dve", lambda e, i=i, q_=q_, c=c, isq=isq: e.scalar_tensor_tensor(
                        out=q_[:, c, :], in0=t1[i][:, :], scalar=(0.125 if isq else 1.0), in1=rs[i][:, :],
                        op0=ALU.mult, op1=ALU.mult), reads=[t1b[i], rsb[i]], pwrites=[q_b])
            if _os.environ.get("K_C6", "0") != "1":
                S.dma("sp", QT[:, t0:t0 + TT].rearrange("(c p) n -> p c n", p=128),
                      q_[:, 0:8, :], reads=[q_b], pwrites=[QTb])
                S.dma("sp", KT[0:256, PAD + t0:PAD + t0 + TT].rearrange("(c p) n -> p c n", p=128),
                      q_[:, 8:10, :], reads=[q_b], pwrites=[KTb])
            v_, v_b = vs[t % 2], vsb[t % 2]
            for tb in range(4 if _os.environ.get("K_C1", "0") != "1" else 0):
                p, pb = ps[6], psb[6]
                S.mm([(p[:, 0:256], hn[:, k, tb * 128:(tb + 1) * 128], w[:, k, 1280:1536])
                      for k in range(NCH)], reads=[hnb, wbuf], writes=[pb])
                S.op("act", lambda e, p=p, v_=v_, tb=tb: e.activation(
                    out=v_[:, tb, :], in_=p[:, 0:256], func=AF.Copy), reads=[pb], pwrites=[v_b])
            if _os.environ.get("K_C6", "0") != "1":
                S.dma("sp", V[PAD + t0:PAD + t0 + TT, 0:256].rearrange("(b p) f -> p b f", p=128),
                      v_[:, :, :], reads=[v_b], pwrites=[Vb])
            S._commit((S.sem["pe"], S.cnt["pe"]), [hnb, sqb], [])
        S.flush()


def phase_attn_c(nc, S, T, QT, QTb, KT, KTb, V, Vb, OT, OTb):
    NKB, NQT = T // 128, T // TT
    with ExitStack() as st:
        sb = lambda name, shape, dt: st.enter_context(nc.sbuf_tensor(_uid(name), shape, dt))
        qt_ = [sb(f"qt{i}", [128, T], BF16) for i in range(2)]
        k2 = [sb(f"k2{i}", [128, T], BF16) for i in range(2)]
        ve = [sb(f"ve{i}", [128, NKB, 128], BF16) for i in range(2)]
        vo = [sb(f"vo{i}", [128, NKB, 128], BF16) for i in range(2)]
        pt = [sb(f"pt{i}", [128, 2, TT], BF16) for i in range(4)]
        zz = [sb(f"zz{i}", [128, TT], F32) for i in range(2)]
        zzs = [sb(f"zzs{i}", [128, TT], F32) for i in range(2)]
        rzs = [sb(f"rzs{i}", [128, TT], F32) for i in range(2)]
        ost = [sb(f"ost{i}", [128, TT], BF16) for i in range(2)]
        sS = [st.enter_context(nc.psum_tensor(_uid(f"sS{i}"), [128, 2, TT], F32)) for i in range(2)]
        E = [st.enter_context(nc.psum_tensor(_uid(f"E{i}"), [128, TT], F32)) for i in range(2)]
        O = [st.enter_context(nc.psum_tensor(_uid(f"O{i}"), [128, TT], F32)) for i in range(2)]
        B = Buf
        qtb = [B("qt0"), B("qt1")]
        k2b = [B("k0"), B("k1")]
        veb = [B("a"), B("b")]
        vob = [B("a"), B("b")]
        ptb = [B(f"pt{i}") for i in range(4)]
        zzb = [B("a"), B("b")]
        zzsb = [B("a"), B("b")]
        rzsb = [B("a"), B("b")]
        ostb = [B("a"), B("b")]
        sSb = [B("s0"), B("s1")]
        Eb = [B("e0"), B("e1")]
        Ob = [B("o0"), B("o1")]
        for i in range(2):
            S.op(PMS, lambda e, i=i: e.memset(ve[i][:, :, 64:128], 1.0), writes=[veb[i]])
            S.op(PMS, lambda e, i=i: e.memset(vo[i][:, :, 0:64], 1.0), writes=[vob[i]])

        def load(c):
            i = c % 2
            g = c // 2
            S.dma("sp", qt_[i][:, :], QT[c * 128:(c + 1) * 128, :], reads=[QTb], writes=[qtb[i]])
            S.dma("sp", k2[i][0:64, :], KT[g * 64:(g + 1) * 64, PAD:PAD + T], reads=[KTb], writes=[k2b[i]])
            S.dma("sp", k2[i][64:128, :], KT[g * 64:(g + 1) * 64, PAD:PAD + T], reads=[KTb], pwrites=[k2b[i]])
            vsrc = V[PAD:PAD + T, g * 64:(g + 1) * 64].rearrange("(b p) f -> p b f", p=128)
            S.dma("sp", ve[i][:, :, 0:64], vsrc, reads=[Vb, veb[i]], pwrites=[veb[i]])
            S.dma("sp", vo[i][:, :, 64:128], vsrc, reads=[Vb, vob[i]], pwrites=[vob[i]])
        load(0)
        npt = 0
        nq = 0
        for c in range(8):
            if c + 1 < 8:
                load(c + 1)
            i = c % 2
            q_, k_ = qt_[i], k2[i]
            for qt in range(NQT):
                q0 = qt * TT
                a = nq % 2
                nq += 1

                def issue_S(kb):
                    x = kb % 2
                    for j in range(2):
                        S.mm([(sS[x][:, j, :], k_[j * 64:(j + 1) * 64, kb * 128:(kb + 1) * 128],
                               q_[j * 64:(j + 1) * 64, q0:q0 + TT])],
                             reads=[qtb[i], k2b[i]],
                             writes=[sSb[x]] if j == 0 else [], pwrites=[] if j == 0 else [sSb[x]])
                issue_S(0)
                for kb in range(NKB):
                    if kb + 1 < NKB:
                        issue_S(kb + 1)
                    x = kb % 2
                    n = npt % 4
                    npt += 1
                    S.op("act", lambda e, n=n, x=x: e.activation(
                        out=pt[n][:, :, :], in_=sS[x][:, :, :], func=AF.Exp), reads=[sSb[x]], writes=[ptb[n]])
                    first, last = (kb == 0), (kb == NKB - 1)
                    S.mm([(E[a][:, :], ve[i][:, kb, :], pt[n][:, 0, :])], reads=[ptb[n], veb[i]],
                         writes=[Eb[a]] if first else [], pwrites=[] if first else [Eb[a]],
                         start=first, stop=last)
                    S.mm([(O[a][:, :], vo[i][:, kb, :], pt[n][:, 1, :])], reads=[ptb[n], vob[i]],
                         writes=[Ob[a]] if first else [], pwrites=[] if first else [Ob[a]],
                         start=first, stop=last)
                S.op("dve", lambda e, a=a: e.tensor_copy(out=zz[a][0:64, :], in_=O[a][0:64, :]),
                     reads=[Ob[a]], writes=[zzb[a]])
                S.op("dve", lambda e, a=a: e.tensor_copy(out=zz[a][64:128, :], in_=E[a][64:128, :]),
                     reads=[Eb[a]], pwrites=[zzb[a]])
                S.dma("sp", zzs[a][0:64, :], zz[a][64:128, :], reads=[zzb[a]], writes=[zzsb[a]])
                S.dma("sp", zzs[a][64:128, :], zz[a][0:64, :], reads=[zzb[a]], pwrites=[zzsb[a]])
                S.op("dve", lambda e, a=a: e.reciprocal(out=rzs[a][:, :], in_=zzs[a][:, :]),
                     reads=[zzsb[a]], writes=[rzsb[a]])
                S.op("dve", lambda e, a=a: e.tensor_tensor(out=ost[a][0:64, :], in0=E[a][0:64, :],
                                                          in1=rzs[a][0:64, :], op=ALU.mult),
                     reads=[Eb[a], rzsb[a]], writes=[ostb[a]])
                S.op("dve", lambda e, a=a: e.tensor_tensor(out=ost[a][64:128, :], in0=O[a][64:128, :],
                                                          in1=rzs[a][64:128, :], op=ALU.mult),
                     reads=[Ob[a], rzsb[a]], pwrites=[ostb[a]])
                S.dma("sp", OT[c * 128:(c + 1) * 128, q0:q0 + TT], ost[a][:, :], reads=[ostb[a]], pwrites=[OTb])
        S.flush()


A_DIL = [1] * 6 + [4] * 5 + [16] * 5
A_GRP = [0] * 6 + [1] * 5 + [2] * 5
A_NH = [6, 5, 5]


def phase_attn_a(nc, S, T, QT, QTb, KT, KTb, V, Vb, PVun, PVb_d, Zbc, Zb_d, bm_d):
    with ExitStack() as st:
        sb = lambda name, shape, dt: st.enter_context(nc.sbuf_tensor(_uid(name), shape, dt))
        NBMAX = T // 128 + 16
        qa = [sb(f"qa{i}", [128, T], BF16) for i in range(2)]
        ka = [sb(f"ka{i}", [128, T + 2 * PAD], BF16) for i in range(2)]
        va = [[sb(f"va{i}{j}", [128, NBMAX, 128], BF16) for j in range(2)] for i in range(2)]
        bm = [sb(f"bm{i}", [128, 2, 4, 128], F32) for i in range(2)]
        pvb = [sb(f"pvb{i}", [128, T], F32) for i in range(2)]
        zb = [sb(f"zb{i}", [128, T], F32) for i in range(2)]
        sbi = [sb(f"sbi{i}", [128, 128], F32) for i in range(4)]
        pt = [sb(f"pt{i}", [128, 128], BF16) for i in range(8)]
        olo = sb("olo", [128, 128], BF16)
        ohi = sb("ohi", [128, 128], BF16)
        ps = [st.enter_context(nc.psum_tensor(_uid(f"ps{i}"), [128, TT], F32)) for i in range(8)]
        B = Buf
        qab = [B("a"), B("b")]
        kab = [B("a"), B("b")]
        vab = [[B("a"), B("b")], [B("c"), B("d")]]
        bmb = [B("a"), B("b")]
        pvbb = [B("a"), B("b")]
        zbb = [B("a"), B("b")]
        sbib = [B("s") for _ in range(4)]
        ptb = [B("p") for _ in range(8)]
        olob, ohib = B("olo"), B("ohi")
        psb = [B(f"ps{i}") for i in range(8)]
        S.op(PMS, lambda e: e.memset(olo[:, :], 0.0), writes=[olob])
        S.op(PMS, lambda e: e.memset(olo[:, 0:64], 1.0), writes=[olob])
        S.op(PMS, lambda e: e.memset(ohi[:, :], 0.0), writes=[ohib])
        S.op(PMS, lambda e: e.memset(ohi[:, 64:128], 1.0), writes=[ohib])
        for i in range(2):
            for j in range(2):
                S.op(PMS, lambda e, i=i, j=j: e.memset(va[i][j][:, :, :], 0.0), writes=[vab[i][j]])
        on = [olo, ohi]
        onb = [olob, ohib]

        def load(c):
            i = c % 2
            S.dma("sp", qa[i][:, :], QT[c * 128:(c + 1) * 128, :], reads=[QTb], writes=[qab[i]])
            S.dma("sp", ka[i][:, :], KT[c * 128:(c + 1) * 128, :], reads=[KTb], writes=[kab[i]])
            S.dma("sp", bm[i][:, :, :, :], bm_d[c, :, :].rearrange("p (s v j) -> p s v j", s=2, v=4),
                  writes=[bmb[i]])
            for s_ in range(2):
                h = 2 * c + s_
                d = A_DIL[h]
                nbl = T // (128 * d) + 1
                for r in range(d):
                    start = PAD + r - 64 * d
                    src = V[start:start + d * (128 * nbl - 1) + 1:d, h * 64:(h + 1) * 64].rearrange(
                        "(m i) f -> i m f", i=128)
                    S.dma("sp", va[i][s_][:, r * nbl:(r + 1) * nbl, s_ * 64:(s_ + 1) * 64], src,
                          reads=[Vb, vab[i][s_]], pwrites=[vab[i][s_]])
        load(0)
        nn = 0
        for c in range(8):
            if c + 1 < 8:
                load(c + 1)
            i = c % 2
            groups = []
            for s_ in range(2):
                h = 2 * c + s_
                d = A_DIL[h]
                nq = (T // d) // 128
                for r in range(d):
                    for b in range(nq):
                        groups.append((s_, d, nq, r, b))

            def stage1(g):
                nonlocal nn
                s_, d, nq, r, b = g
                nbl = nq + 1
                rows = slice(s_ * 64, (s_ + 1) * 64)
                qsl = slice(r + 128 * d * b, r + 128 * d * b + 127 * d + 1, d)
                pts = []
                for mi in range(2):
                    m = b + mi
                    k0 = PAD + r - 64 * d + 128 * d * m
                    ksl = slice(k0, k0 + 127 * d + 1, d)
                    if mi == 0:
                        var = 1 if b == 0 else 0
                    else:
                        var = 3 if b == nq - 1 else 2
                    n4, n8 = nn % 4, nn % 8
                    nn += 1
                    sp_, spb = ps[n4], psb[n4]
                    S.mm([(sp_[:, 0:128], ka[i][rows, ksl], qa[i][rows, qsl])],
                         reads=[kab[i], qab[i]], writes=[spb])
                    S.op("dve", lambda e, n4=n4, sp_=sp_, s_=s_, var=var, i=i: e.tensor_tensor(
                        out=sbi[n4][:, :], in0=sp_[:, 0:128], in1=bm[i][:, s_, var, :], op=ALU.add),
                        reads=[spb, bmb[i]], writes=[sbib[n4]])
                    S.op("act", lambda e, n4=n4, n8=n8: e.activation(
                        out=pt[n8][:, :], in_=sbi[n4][:, :], func=AF.Exp),
                        reads=[sbib[n4]], writes=[ptb[n8]])
                    pts.append((n8, r * nbl + m))
                return pts, (nn // 2) % 2

            def stage2(g, st1):
                s_, d, nq, r, b = g
                pts, a = st1
                rows = slice(s_ * 64, (s_ + 1) * 64)
                qsl = slice(r + 128 * d * b, r + 128 * d * b + 127 * d + 1, d)
                pv_, pv_b = ps[4 + a], psb[4 + a]
                z_, z_b = ps[6 + a], psb[6 + a]
                S.mm([(pv_[:, 0:128], va[i][s_][:, blk, :], pt[n8][:, :]) for (n8, blk) in pts],
                     reads=[ptb[n8] for (n8, _) in pts] + [vab[i][s_]], writes=[pv_b])
                S.mm([(z_[:, 0:128], on[s_][:, :], pt[n8][:, :]) for (n8, blk) in pts],
                     reads=[ptb[n8] for (n8, _) in pts] + [onb[s_]], writes=[z_b])
                S.op("act", lambda e, pv_=pv_, rows=rows, qsl=qsl, i=i: e.activation(
                    out=pvb[i][rows, qsl], in_=pv_[rows, 0:128], func=AF.Copy),
                    reads=[pv_b], pwrites=[pvbb[i]])
                S.op("dve", lambda e, z_=z_, rows=rows, qsl=qsl, i=i: e.tensor_copy(
                    out=zb[i][rows, qsl], in_=z_[rows, 0:128]),
                    reads=[z_b], pwrites=[zbb[i]])

            st = stage1(groups[0])
            for gi, g in enumerate(groups):
                nxt = stage1(groups[gi + 1]) if gi + 1 < len(groups) else None
                stage2(g, st)
                st = nxt
            S.dma("sp", PVun[c * 128:(c + 1) * 128, :], pvb[i][:, :], reads=[pvbb[i]], pwrites=[PVb_d])
            S.dma("sp", Zbc[2 * c:2 * c + 1, :], zb[i][0:1, :], reads=[zbb[i]], pwrites=[Zb_d])
            S.dma("sp", Zbc[2 * c + 1:2 * c + 2, :], zb[i][64:65, :], reads=[zbb[i]], pwrites=[Zb_d])
        S.flush()


def phase_comb_a(nc, S, T, PVun, PVb_d, Zc, Zb_d, OT, OTb, asel_d, bsel_d, esel_d):
    ntile = T // TT
    with ExitStack() as st:
        sb = lambda name, shape, dt: st.enter_context(nc.sbuf_tensor(_uid(name), shape, dt))
        zt = [sb(f"zt{i}", [16, TT], F32) for i in range(2)]
        pv = [sb(f"pv{i}", [128, NCH, TT], F32) for i in range(2)]
        asel = sb("asel", [16, 3], F32)
        bsel = sb("bsel", [3, 16], F32)
        esel = sb("esel", [16, NCH * 128], F32)
        o33 = sb("o33", [3, 3], F32)
        sg = sb("sg", [3, TT], F32)
        rt = sb("rt", [3, TT], F32)
        al = sb("al", [3, TT], F32)
        rz = sb("rz", [16, TT], F32)
        f16 = sb("f16", [16, TT], F32)
        ost = [sb(f"ost{i}", [128, NCH, TT], BF16) for i in range(2)]
        ps = [st.enter_context(nc.psum_tensor(_uid(f"ps{i}"), [128, TT], F32)) for i in range(8)]
        B = Buf
        ztb = [B("a"), B("b")]
        pvb = [B("a"), B("b")]
        aselb, bselb, eselb, o33b, sgb, rtb, alb, rzb, f16b = (B("a"), B("b"), B("e"), B("c"), B("d"), B("e"),
                                                               B("f"), B("g"), B("h"))
        ostb = [B("a"), B("b")]
        psb = [B(f"ps{i}") for i in range(8)]
        S.dma("sp", asel[:, :], asel_d, writes=[aselb])
        S.dma("sp", bsel[:, :], bsel_d, writes=[bselb])
        S.dma("sp", esel[:, :], esel_d, writes=[eselb])
        S.op(PMS, lambda e: e.memset(o33[:, :], 1.0), writes=[o33b])

        def load(t):
            S.dma("sp", zt[t % 2][:, :], Zc[:, t * TT:(t + 1) * TT], reads=[Zb_d], writes=[ztb[t % 2]])
            S.dma("sp", pv[t % 2][:, :, :], PVun[:, t * TT:(t + 1) * TT].rearrange("(k p) n -> p k n", p=128),
                  reads=[PVb_d], writes=[pvb[t % 2]])
        load(0)
        for t in range(ntile):
            if t + 1 < ntile:
                load(t + 1)
            z_, z_b = zt[t % 2], ztb[t % 2]
            p_, p_b = pv[t % 2], pvb[t % 2]
            S.mm([(ps[0][0:3, :], asel[:, :], z_[:, :])], reads=[z_b, aselb], writes=[psb[0]])
            S.op("dve", lambda e: e.tensor_copy(out=sg[:, :], in_=ps[0][0:3, :]), reads=[psb[0]], writes=[sgb])
            S.mm([(ps[1][0:3, :], o33[:, :], sg[:, :])], reads=[sgb, o33b], writes=[psb[1]])
            S.op("dve", lambda e: e.reciprocal(out=rt[:, :], in_=ps[1][0:3, :]), reads=[psb[1]], writes=[rtb])
            S.op("dve", lambda e: e.scalar_tensor_tensor(out=al[:, :], in0=sg[:, :], scalar=3.0, in1=rt[:, :],
                                                         op0=ALU.mult, op1=ALU.mult),
                 reads=[sgb, rtb], writes=[alb])
            S.mm([(ps[2][0:16, :], bsel[:, :], al[:, :])], reads=[alb, bselb], writes=[psb[2]])
            S.op("dve", lambda e, z_=z_: e.reciprocal(out=rz[:, :], in_=z_[:, :]), reads=[z_b], writes=[rzb])
            S.op("dve", lambda e: e.tensor_tensor(out=f16[:, :], in0=ps[2][0:16, :], in1=rz[:, :], op=ALU.mult),
                 reads=[psb[2], rzb], writes=[f16b])
            o_, o_b = ost[t % 2], ostb[t % 2]
            for c in range(NCH):
                pa, pab = ps[3 + c % 4], psb[3 + c % 4]
                S.mm([(pa[:, :], esel[:, c * 128:(c + 1) * 128], f16[:, :])], reads=[f16b, eselb], writes=[pab])
                S.op("dve", lambda e, pa=pa, p_=p_, c=c, o_=o_: e.tensor_tensor(
                    out=o_[:, c, :], in0=pa[:, :], in1=p_[:, c, :], op=ALU.mult),
                    reads=[pab, p_b], pwrites=[o_b])
            S.dma("sp", OT[:, t * TT:(t + 1) * TT].rearrange("(c p) n -> p c n", p=128), o_[:, :, :],
                  reads=[o_b], pwrites=[OTb])
        S.flush()


def lambda_init_fn(layer_idx):
    return 0.8 - 0.6 * math.exp(-0.3 * layer_idx)


def build(T, layers=(0, 1, 2, 3), final=True):
    nc = bass.Bass("TRN2", target_bir_lowering=False)
    ntile = T // TT
    import os
    limit = int(os.environ.get("K_STOP", "1000"))
    count = [0]

    def RUN(fn, *a):
        count[0] += 1
        if count[0] <= limit:
            fn(*a)
    with ExitStack() as st:
        S = Sched(nc, st)

        def ext(name, shape, dtype=F32):
            return nc.dram_tensor(name, list(shape), dtype, kind="ExternalInput").ap()

        def internal(name, shape, dtype):
            return nc.dram_tensor(name, list(shape), dtype, kind="Internal").ap()

        xT = ext("xT", [D, T])
        outT = nc.dram_tensor("outT", [D, T], F32, kind="ExternalOutput").ap()
        kinds = {li % 3 for li in layers}
        W = {"mlp_w_in": ext("mlp_w_in", [4, D, DFF]), "mlp_w_out": ext("mlp_w_out", [4, DFF, D])}
        if 0 in kinds:
            W["a_w_qkv"] = ext("a_w_qkv", [2, D, 3072])
            W["a_w_o"] = ext("a_w_o", [2, D, D])
        if 1 in kinds:
            W["b_w_qkv"] = ext("b_w_qkv", [1, D, 3072])
            W["b_w_o"] = ext("b_w_o", [1, D, D])
        if 2 in kinds:
            W["c_w_qkv"] = ext("c_w_qkv", [1, D, 1536])
            W["c_w_o"] = ext("c_w_o", [1, D, D])
        gmix = ext("gmix", [128, 32])
        gmlp = ext("gmlp", [128, 32])
        gfin = ext("gfin", [128, 8])
        gtab = ext("gtab", [16, 128, GW])
        bconst = ext("bconst", [128, 32])
        lam = ext("lam", [128, 256])
        gsub = ext("gsub", [128, 1])
        ctab = ext("ctab", [128, T])
        stab = ext("stab", [128, T])
        pmat = ext("pmat", [128, 128])
        qkg = ext("qkg", [128, 2])
        bm = ext("bm", [8, 128, 1024])
        asel = ext("asel", [16, 3])
        bsel = ext("bsel", [3, 16])
        esel = ext("esel", [16, 1024])

        hT = internal("hT", [D, T], F32)
        QT = internal("QT", [D, T], BF16)
        KT = internal("KT", [D, T + 2 * PAD], BF16)
        V = internal("V", [T + 2 * PAD, D], BF16)
        OT = internal("OT", [D, T], BF16)
        PVun = internal("PVun", [D, T], F32)
        Zbc = internal("Zbc", [16, T], F32)
        hTb = [Buf(f"hT{t}") for t in range(ntile)]
        QTb, KTb, Vb, OTb, PVb, Zb, outb = (Buf("QT"), Buf("KT"), Buf("V"), Buf("OT"), Buf("PV"),
                                            Buf("Z"), Buf("out"))

        wbf = {}
        castchain = Buf("castchain")

        def cast(name, idx, rows, cols):
            src = W[name][idx]
            dst = internal(f"{name}{idx}_bf", [rows, cols], BF16)
            sv, dv = src, dst
            if cols > 2048:
                c = 2048 if cols % 2048 == 0 else 1024
                sv = sv.rearrange("a (b c) -> (a b) c", c=c)
                dv = dv.rearrange("a (b c) -> (a b) c", c=c)
            elif cols == 1024:
                sv = sv.rearrange("(a b) c -> a (b c)", b=2)
                dv = dv.rearrange("(a b) c -> a (b c)", b=2)
            b = Buf(name)
            S.dma("pool", dv, sv, writes=[b, castchain])
            wbf[(name, idx)] = (dst, [b] * 8)

        first_f32 = (layers[0] % 3) in (0, 1)
        for n_, li in enumerate(layers):
            kind, j = li % 3, li // 3
            pre = "abc"[kind]
            if n_ == 0 and first_f32:
                wbf[(f"{pre}_w_qkv", j)] = (None, None)
            else:
                cast(f"{pre}_w_qkv", j, D, 1536 if kind == 2 else 3072)
            cast(f"{pre}_w_o", j, D, D)
            cast("mlp_w_in", li, D, DFF)
            cast("mlp_w_out", li, DFF, D)

        with ExitStack() as st2:
            zt = st2.enter_context(nc.sbuf_tensor(_uid("zt"), [128, NCH, PAD], BF16))
            ztb = Buf("zt")
            S.op(PMS, lambda e: e.memset(zt[:, :, :], 0.0), writes=[ztb])
            S.dma("sp", KT[:, 0:PAD].rearrange("(c p) n -> p c n", p=128), zt[:, :, :], reads=[ztb], pwrites=[KTb])
            S.dma("sp", KT[:, PAD + T:].rearrange("(c p) n -> p c n", p=128), zt[:, :, :], reads=[ztb], pwrites=[KTb])
            S.dma("sp", V[0:PAD, :].rearrange("(b p) f -> p b f", p=128), zt[:, :, :], reads=[ztb], pwrites=[Vb])
            S.dma("sp", V[PAD + T:, :].rearrange("(b p) f -> p b f", p=128), zt[:, :, :], reads=[ztb], pwrites=[Vb])
            S.flush(wait_bufs=[KTb, Vb])
        xTb = [Buf(f"xT{t}") for t in range(ntile)]

        for n_, li in enumerate(layers):
            kind, j = li % 3, li // 3
            pre = "abc"[kind]
            wq, wqb = wbf[(f"{pre}_w_qkv", j)]
            wf32 = W[f"{pre}_w_qkv"][j] if (n_ == 0 and first_f32) else None
            hin = (xT, xTb) if n_ == 0 else (None, None)
            wo, wob = wbf[(f"{pre}_w_o", j)]
            wi, wib = wbf[("mlp_w_in", li)]
            wo2, wo2b = wbf[("mlp_w_out", li)]
            g1 = gmix[:, 8 * li:8 * li + 8]
            g2 = gmlp[:, 8 * li:8 * li + 8]
            if kind == 0:
                RUN(phase_qkv_ab, nc, S, T, hT, hTb, wq, wqb, g1, QT, QTb, KT, KTb, V, Vb, wf32, *hin)
                RUN(phase_attn_a, nc, S, T, QT, QTb, KT, KTb, V, Vb, PVun, PVb, Zbc, Zb, bm)
                RUN(phase_comb_a, nc, S, T, PVun, PVb, Zbc, Zb, OT, OTb, asel, bsel, esel)
            elif kind == 1:
                RUN(phase_qkv_ab, nc, S, T, hT, hTb, wq, wqb, g1, QT, QTb, KT, KTb, V, Vb, wf32, *hin)
                RUN(phase_attn_b, nc, S, T, QT, QTb, KT, KTb, V, Vb, OT, OTb, gtab, bconst, lam, gsub,
                             lambda_init_fn(li))
            else:
                RUN(phase_qkv_c, nc, S, T, hT, hTb, wq, wqb, g1, QT, QTb, KT, KTb, V, Vb, ctab, stab, pmat, qkg, *hin)
                RUN(phase_attn_c, nc, S, T, QT, QTb, KT, KTb, V, Vb, OT, OTb)
            RUN(phase_wo, nc, S, T, OT, OTb, wo, wob, hT, hTb, *hin)
            RUN(phase_mlp, nc, S, T, hT, hTb, wi, wo2, [wib, wo2b], g2)
        if final:
            RUN(phase_final, nc, S, T, hT, hTb, gfin, outT, outb)
        else:
            with ExitStack() as st2:
                for t in range(ntile):
                    S.dma("sp", outT[:, t * TT:(t + 1) * TT], hT[:, t * TT:(t + 1) * TT], reads=[hTb[t]],
                          pwrites=[outb])
                S.flush(wait_bufs=[outb], wait_all=True)
    return nc


def t5_bucket_np(rel):
    rel = np.asarray(rel, dtype=np.int64)
    nb = 16
    max_exact = 8
    side = np.where(rel > 0, nb, 0)
    n = np.abs(rel)
    nf = np.maximum(n, 1).astype(np.float32)
    large = max_exact + (np.log(nf / np.float32(max_exact)) / np.float32(math.log(1024 / max_exact))
                         * np.float32(nb - max_exact)).astype(np.int32)
    large = np.minimum(large, nb - 1)
    return (side + np.where(n < max_exact, n, large)).astype(np.int64)


def host_tables(T, inputs):
    f32 = np.float32
    rb = np.asarray(inputs["rel_bias"], f32)
    tabs = {}
    i = np.arange(128)[:, None]
    col = np.arange(GW)[None, :]
    bk = t5_bucket_np(i - col + GC)
    tabs["gtab"] = np.ascontiguousarray(np.transpose(rb[bk], (2, 0, 1)))
    bc = np.concatenate([rb[15], rb[31]])[None, :]
    tabs["bconst"] = np.ascontiguousarray(np.repeat(bc, 128, axis=0))
    tabs["lam"] = np.ascontiguousarray(np.repeat(np.asarray(inputs["b_lambda"], f32).reshape(1, 256), 128, 0))
    tabs["gsub"] = np.ascontiguousarray(np.asarray(inputs["b_subln_g"], f32).reshape(128, 1))
    NEG = f32(-30000.0)
    bm = np.zeros((16, 4, 128, 128), f32)
    ii = np.arange(128)[:, None]
    jj = np.arange(128)[None, :]
    for h in range(16):
        d = A_DIL[h]
        o0 = ii - 64 - jj
        o1 = ii + 64 - jj
        b0 = rb[t5_bucket_np(o0 * d), h]
        b1 = rb[t5_bucket_np(o1 * d), h]
        v0 = ii >= jj
        v1 = ii <= jj
        bm[h, 0] = np.where(v0, b0, NEG)
        bm[h, 1] = np.where(v0 & (ii >= 64), b0, NEG)
        bm[h, 2] = np.where(v1, b1, NEG)
        bm[h, 3] = np.where(v1 & (ii < 64), b1, NEG)
    bm = bm.reshape(8, 2, 4, 128, 128).transpose(0, 3, 1, 2, 4).reshape(8, 128, 1024)
    tabs["bm"] = np.ascontiguousarray(bm)
    asel = np.zeros((16, 3), f32)
    bsel = np.zeros((3, 16), f32)
    esel = np.zeros((16, 1024), f32)
    for h in range(16):
        g = A_GRP[h]
        asel[h, g] = 1.0 / A_NH[g]
        bsel[g, h] = 1.0
        esel[h, h * 64:(h + 1) * 64] = 1.0
    tabs["asel"], tabs["bsel"], tabs["esel"] = asel, bsel, esel
    pos = np.arange(T)
    row = (pos // 64).astype(f32)
    colp = (pos % 64).astype(f32)
    inv = (f32(10000.0) ** (-np.arange(16, dtype=f32) / f32(16))).astype(f32)
    ang = np.concatenate([row[:, None] * inv, colp[:, None] * inv], axis=-1).astype(f32)
    cos, sin = np.cos(ang).astype(f32), np.sin(ang).astype(f32)
    ct = np.zeros((64, T), f32)
    stt = np.zeros((64, T), f32)
    pm = np.zeros((128, 128), f32)
    for a in range(2):
        for jh in range(2):
            for f in range(16):
                dd = a * 32 + jh * 16 + f
                ct[dd] = cos[:, a * 16 + f]
                stt[dd] = (-sin[:, a * 16 + f]) if jh == 0 else sin[:, a * 16 + f]
                other = a * 32 + (1 - jh) * 16 + f
                for hh in range(2):
                    pm[hh * 64 + other, hh * 64 + dd] = 1.0
    tabs["ctab"] = np.ascontiguousarray(np.concatenate([ct, ct], 0))
    tabs["stab"] = np.ascontiguousarray(np.concatenate([stt, stt], 0))
    tabs["pmat"] = pm
    qg = np.asarray(inputs["c_q_norm_g"], f32).reshape(64)
    kg = np.asarray(inputs["c_k_norm_g"], f32).reshape(64)
    tabs["qkg"] = np.ascontiguousarray(np.stack([np.tile(qg, 2), np.tile(kg, 2)], axis=1))
    def gl(a):
        a = np.asarray(a, f32).reshape(-1, 8, 128)
        return np.ascontiguousarray(a.transpose(2, 0, 1).reshape(128, -1))
    tabs["gmix"] = gl(inputs["norm_mix_g"])
    tabs["gmlp"] = gl(inputs["norm_mlp_g"])
    tabs["gfin"] = gl(inputs["norm_final_g"])
    for k in ("a_w_qkv", "a_w_o", "b_w_qkv", "b_w_o", "c_w_qkv", "c_w_o", "mlp_w_in", "mlp_w_out"):
        tabs[k] = np.ascontiguousarray(np.asarray(inputs[k], f32))
    return tabs


_PROGRAM_CACHE = {}


def kernel(**inputs):
    x = np.asarray(inputs["x"], np.float32)
    Bn, T, _ = x.shape
    key = (T,)
    if key not in _PROGRAM_CACHE:
        _PROGRAM_CACHE[key] = build(T)
    nc = _PROGRAM_CACHE[key]
    tabs = host_tables(T, inputs)
    in_maps = []
    for c in range(8):
        m = dict(tabs)
        m["xT"] = np.ascontiguousarray(x[c % Bn].T)
        in_maps.append(m)
    res = run_bass_kernel_spmd(nc, in_maps, core_ids=list(range(8)))
    out = np.stack([np.ascontiguousarray(res.results[b]["outT"].T) for b in range(Bn)], axis=0)
    return out.astype(np.float32)
```

```python
import math
from contextlib import ExitStack

import numpy as np
import concourse.bass as bass
import concourse.mybir as mybir
from concourse.bass_utils import run_bass_kernel_spmd

F32 = mybir.dt.float32
BF16 = mybir.dt.bfloat16
AF = mybir.ActivationFunctionType
ALU = mybir.AluOpType

D = 1024
NCH = 8
DFF = 4096
NFC = 32
EPS = 1e-6
TT = 512


import os as _os
PTT = "dve" if _os.environ.get("K_NOPTT", "0") == "1" else "pool"
PMS = "pool" if _os.environ.get("K_POOLMS", "0") == "1" else "dve"
_UID = [0]


def _uid(name):
    _UID[0] += 1
    return f"{name}_u{_UID[0]}"


class Buf:
    def __init__(self, name):
        self.name = name
        self.w = {}
        self.r = {}


class Sched:
    ENGS = ("pe", "act", "dve", "pool", "sp")

    def __init__(self, nc, stack, n_dma_sems=60):
        self.nc = nc
        self.sem = {e: stack.enter_context(nc.semaphore(f"sem_{e}")) for e in self.ENGS}
        self.cnt = {e: 0 for e in self.ENGS}
        self.dsem = [stack.enter_context(nc.semaphore(f"sem_dma{i}")) for i in range(n_dma_sems)]
        self.dcnt = [0] * n_dma_sems
        self.dnext = 0
        self.n_sw = 16
        self.dnext_sw = 0
        self.waited = {}
        self.q = {e: [] for e in self.ENGS}
        self.semobj = {}
        self.nblocks = 0

    def _key(self, sem):
        k = id(sem)
        self.semobj[k] = sem
        return k

    def _wait(self, e, toks):
        for k, val in toks.items():
            if self.waited.get((e, k), 0) >= val:
                continue
            self.waited[(e, k)] = val
            sem = self.semobj[k]
            self.q[e].append(lambda eng, sem=sem, val=val: eng.wait_ge(sem, val))

    def _deps(self, e, reads, writes, pwrites):
        toks = {}

        def add(d):
            for k, v in d.items():
                if v > toks.get(k, 0):
                    toks[k] = v
        for b in reads:
            add(b.w)
        for b in writes:
            add(b.w)
            add(b.r)
        for b in pwrites:
            add(b.r)
        if e == "pe":
            toks.pop(self._key(self.sem[e]), None)
        return toks

    def _commit(self, tok, reads, writes, pwrites=()):
        k = self._key(tok[0])
        for b in reads:
            if b.r.get(k, 0) < tok[1]:
                b.r[k] = tok[1]
        for b in writes:
            b.w = {k: tok[1]}
            b.r = {}
        for b in pwrites:
            if b.w.get(k, 0) < tok[1]:
                b.w[k] = tok[1]

    def op(self, e, fn, reads=(), writes=(), pwrites=()):
        self._wait(e, self._deps(e, reads, writes, pwrites))
        self.cnt[e] += 1
        sem = self.sem[e]
        self.q[e].append(lambda eng, fn=fn, sem=sem: fn(eng).then_inc(sem, 1))
        self._commit((sem, self.cnt[e]), reads, writes, pwrites)

    def mm(self, mms, reads=(), writes=(), pwrites=(), start=True, stop=True):
        e = "pe"
        self._wait(e, self._deps(e, reads, writes, pwrites))
        self.cnt[e] += 1
        sem = self.sem[e]
        n = len(mms)

        def run(eng, mms=mms, sem=sem, n=n, start=start, stop=stop):
            for i, (o, l, r) in enumerate(mms):
                ins = eng.matmul(o, l, r, start=(start and i == 0), stop=(stop and i == n - 1))
            ins.then_inc(sem, 1)
        self.q[e].append(run)
        self._commit((sem, self.cnt[e]), reads, writes, pwrites)

    def dma(self, e, out, in_, reads=(), writes=(), pwrites=()):
        toks = self._deps(e, reads, writes, pwrites)
        if e == "pool":
            i = self.dnext_sw
            self.dnext_sw = (self.dnext_sw + 1) % self.n_sw
        else:
            i = self.n_sw + self.dnext
            self.dnext = (self.dnext + 1) % (len(self.dsem) - self.n_sw)
        sem = self.dsem[i]
        k = self._key(sem)
        if self.dcnt[i] > toks.get(k, 0):
            toks[k] = self.dcnt[i]
        self._wait(e, toks)
        self.dcnt[i] += 16
        self.q[e].append(lambda eng, out=out, in_=in_, sem=sem:
                         eng.dma_start(out=out, in_=in_).then_inc(sem, 16))
        self._commit((sem, self.dcnt[i]), reads, writes, pwrites)

    def flush(self, wait_bufs=(), wait_all=False):
        toks = {}
        for i, sem in enumerate(self.dsem):
            if self.dcnt[i] > 0 and (wait_all or i >= self.n_sw):
                toks[self._key(sem)] = self.dcnt[i]
        for b in wait_bufs:
            for k, v in b.w.items():
                if v > toks.get(k, 0):
                    toks[k] = v
        if toks:
            self._wait("sp", toks)
        q = self.q
        self.q = {e: [] for e in self.ENGS}
        self.nblocks += 1
        with self.nc.Block() as block:
            @block.tensor
            def _(eng):
                for f in q["pe"]:
                    f(eng)

            @block.scalar
            def _(eng):
                for f in q["act"]:
                    f(eng)

            @block.vector
            def _(eng):
                for f in q["dve"]:
                    f(eng)

            @block.gpsimd
            def _(eng):
                for f in q["pool"]:
                    f(eng)

            @block.sync
            def _(eng):
                for f in q["sp"]:
                    f(eng)


def emit_norm(S, T0, hx, hxb, sq, sqb, hn, hnb, gcol, ones, ps, psb, rstd, rstdb, gcolb, onesb):
    S.op("act", lambda e: e.activation(out=sq[:, :, :], in_=hx[:, :, :], func=AF.Square),
         reads=[hxb], writes=[sqb])
    S.mm([(ps[:, :], ones[:, :], sq[:, k, :]) for k in range(NCH)], reads=[sqb, onesb], writes=[psb])
    S.op("act", lambda e: e.activation(out=rstd[:, :], in_=ps[:, :], func=AF.Sqrt,
                                       scale=1.0 / D, bias=EPS),
         reads=[psb], writes=[rstdb])
    S.op("dve", lambda e: e.reciprocal(out=rstd[:, :], in_=rstd[:, :]),
         reads=[rstdb], writes=[rstdb])
    for k in range(NCH):
        S.op("dve", lambda e, k=k: e.scalar_tensor_tensor(
            out=hn[:, k, :], in0=hx[:, k, :], scalar=gcol[:, k:k + 1], in1=rstd[:, :],
            op0=ALU.mult, op1=ALU.mult), reads=[hxb, rstdb, gcolb], pwrites=[hnb])


def phase_mlp(nc, S, T, hT, hTb, w_in_bf, w_out_bf, wb, gnorm_dram):
    ntile = T // TT
    with ExitStack() as st:
        sb = lambda name, shape, dt: st.enter_context(nc.sbuf_tensor(_uid(name), shape, dt))
        win = sb("win", [128, NCH, DFF], BF16)
        wout = sb("wout", [128, NFC, D], BF16)
        hx = [sb(f"hx{i}", [128, NCH, TT], F32) for i in range(2)]
        hn = sb("hn", [128, NCH, TT], BF16)
        u = sb("u", [128, NFC, TT], BF16)
        r = [sb(f"r{i}", [128, TT], BF16) for i in range(2)]
        rstd = sb("rstd", [128, TT], F32)
        gcol = sb("gcol", [128, NCH], F32)
        ones = sb("ones", [128, 128], BF16)
        ps = [st.enter_context(nc.psum_tensor(_uid(f"ps{i}"), [128, TT], F32)) for i in range(8)]
        B = lambda n: Buf(n)
        winb, woutb, hnb, ub, rstdb, gcolb, onesb = (B("win"), B("wout"), B("hn"), B("u"),
                                                     B("rstd"), B("gcol"), B("ones"))
        hxb = [B("hx0"), B("hx1")]
        rb = [B("r0"), B("r1")]
        psb = [B(f"ps{i}") for i in range(8)]
        ucb = [B(f"u{c}") for c in range(NFC)]

        S.op(PMS, lambda e: e.memset(ones[:, :], 1.0), writes=[onesb])
        S.dma("sp", gcol[:, :], gnorm_dram, writes=[gcolb])
        def load(t):
            S.dma("sp", hx[t % 2][:, :, :],
                  hT[:, t * TT:(t + 1) * TT].rearrange("(k p) n -> p k n", p=128),
                  reads=[hTb[t]], writes=[hxb[t % 2]])

        load(0)
        for k in range(NCH):
            S.dma("sp", win[:, k, :], w_in_bf[k * 128:(k + 1) * 128, :], reads=[wb[0][k]], pwrites=[winb])
        for c4 in range(0, NFC, 4):
            S.dma("sp", wout[:, c4:c4 + 4, :],
                  w_out_bf[c4 * 128:(c4 + 4) * 128, :].rearrange("(c p) n -> p c n", p=128),
                  reads=[wb[1][c4 // 4]], pwrites=[woutb])

        for t in range(ntile):
            if t + 1 < ntile:
                load(t + 1)
            x = hx[t % 2]
            xb = hxb[t % 2]
            emit_norm(S, t, x, xb, u[:, 0:NCH, :], ub, hn, hnb, gcol, ones, ps[7], psb[7],
                      rstd, rstdb, gcolb, onesb)
            for c in range(NFC):
                p = ps[c % 4]
                pb = psb[c % 4]
                S.mm([(p[:, :], win[:, k, c * 128:(c + 1) * 128], hn[:, k, :]) for k in range(NCH)],
                     reads=[hnb, winb], writes=[pb])
                rr, rrb = r[c % 2], rb[c % 2]
                S.op("act", lambda e, p=p, rr=rr: e.activation(out=rr[:, :], in_=p[:, :], func=AF.Relu),
                     reads=[pb], writes=[rrb])
                S.op("dve", lambda e, p=p, rr=rr, c=c: e.scalar_tensor_tensor(
                    out=u[:, c, :], in0=p[:, :], scalar=0.0, in1=rr[:, :],
                    op0=ALU.max, op1=ALU.mult), reads=[pb, rrb], writes=[ucb[c]])
            for o in range(NCH):
                p = ps[4 + o % 3]
                pb = psb[4 + o % 3]
                S.mm([(p[:, :], wout[:, c, o * 128:(o + 1) * 128], u[:, c, :]) for c in range(NFC)],
                     reads=ucb + [woutb], writes=[pb])
                S.op("dve", lambda e, p=p, x=x, o=o: e.tensor_tensor(
                    out=x[:, o, :], in0=p[:, :], in1=x[:, o, :], op=ALU.add),
                    reads=[pb, xb], pwrites=[xb])
            S._commit((S.sem["pe"], S.cnt["pe"]), ucb + [ub], [])
            S.dma("sp", hT[:, t * TT:(t + 1) * TT].rearrange("(k p) n -> p k n", p=128),
                  x[:, :, :], reads=[xb], writes=[hTb[t]])
        S.flush()


PAD = 1024
GW = 2266
GC = 1069
NEAR_LO, NEAR_HI = -686, 1070


def phase_qkv_ab(nc, S, T, hT, hTb, w_bf, wb, g_dram, QT, QTb, KT, KTb, V, Vb, w_f32=None, h_in=None, h_inb=None):
    ntile = T // TT
    with ExitStack() as st:
        sb = lambda name, shape, dt: st.enter_context(nc.sbuf_tensor(_uid(name), shape, dt))
        w = sb("wqkv", [128, NCH, 3072], BF16)
        hx = [sb(f"hx{i}", [128, NCH, TT], F32) for i in range(2)]
        sq = sb("sq", [128, NCH, TT], BF16)
        hn = sb("hn", [128, NCH, TT], BF16)
        rstd = sb("rstd", [128, TT], F32)
        gcol = sb("gcol", [128, NCH], F32)
        ones = sb("ones", [128, 128], BF16)
        qk = [sb(f"qk{i}", [128, 16, TT], BF16) for i in range(2)]
        vs = [sb(f"vs{i}", [128, 4, D], BF16) for i in range(2)]
        ps = [st.enter_context(nc.psum_tensor(_uid(f"ps{i}"), [128, TT], F32)) for i in range(8)]
        B = Buf
        wbuf, sqb, hnb, rstdb, gcolb, onesb = B("w"), B("sq"), B("hn"), B("rstd"), B("gcol"), B("ones")
        hxb = [B("hx0"), B("hx1")]
        qkb = [B("qk0"), B("qk1")]
        vsb = [B("vs0"), B("vs1")]
        psb = [B(f"ps{i}") for i in range(8)]
        S.op(PMS, lambda e: e.memset(ones[:, :], 1.0), writes=[onesb])
        S.dma("sp", gcol[:, :], g_dram, writes=[gcolb])
        def load(t):
            S.dma("sp", hx[t % 2][:, :, :],
                  (hT if h_in is None else h_in)[:, t * TT:(t + 1) * TT].rearrange("(k p) n -> p k n", p=128),
                  reads=[(hTb if h_in is None else h_inb)[t]], writes=[hxb[t % 2]])
        load(0)
        if w_f32 is None:
            for k in range(NCH):
                S.dma("sp", w[:, k, :], w_bf[k * 128:(k + 1) * 128, :], reads=[wb[k]], pwrites=[wbuf])
        else:
            wst = [sb(f"wst{i}", [128, 3072], F32) for i in range(2)]
            wstb = [B("wst0"), B("wst1")]
            for k in range(NCH):
                S.dma("sp", wst[k % 2][:, :], w_f32[k * 128:(k + 1) * 128, :], writes=[wstb[k % 2]])
                S.op("act", lambda e, k=k: e.activation(out=w[:, k, :], in_=wst[k % 2][:, :], func=AF.Copy),
                     reads=[wstb[k % 2]], pwrites=[wbuf])

        for t in range(ntile):
            if t + 1 < ntile:
                load(t + 1)
            x, xb = hx[t % 2], hxb[t % 2]
            emit_norm(S, t, x, xb, sq, sqb, hn, hnb, gcol, ones, ps[7], psb[7], rstd, rstdb, gcolb, onesb)
            q_, q_b = qk[t % 2], qkb[t % 2]
            for c in range(16):
                p, pb = ps[c % 4], psb[c % 4]
                S.mm([(p[:, :], w[:, k, c * 128:(c + 1) * 128], hn[:, k, :]) for k in range(NCH)],
                     reads=[hnb, wbuf], writes=[pb])
                if c < 8:
                    S.op("act", lambda e, p=p, q_=q_, c=c: e.activation(
                        out=q_[:, c, :], in_=p[:, :], func=AF.Copy, scale=0.125),
                        reads=[pb], pwrites=[q_b])
                else:
                    S.op("dve", lambda e, p=p, q_=q_, c=c: e.tensor_copy(out=q_[:, c, :], in_=p[:, :]),
                         reads=[pb], pwrites=[q_b])
            S.dma("sp", QT[:, t * TT:(t + 1) * TT].rearrange("(c p) n -> p c n", p=128),
                  q_[:, 0:8, :], reads=[q_b], pwrites=[QTb])
            S.dma("sp", KT[:, PAD + t * TT:PAD + (t + 1) * TT].rearrange("(c p) n -> p c n", p=128),
                  q_[:, 8:16, :], reads=[q_b], pwrites=[KTb])
            v_, v_b = vs[t % 2], vsb[t % 2]
            for tb in range(4):
                for half in range(2):
                    i = tb * 2 + half
                    p, pb = ps[4 + i % 3], psb[4 + i % 3]
                    S.mm([(p[:, :], hn[:, k, tb * 128:(tb + 1) * 128],
                           w[:, k, 2048 + half * 512:2048 + (half + 1) * 512]) for k in range(NCH)],
                         reads=[hnb, wbuf], writes=[pb])
                    if i % 2 == 0:
                        S.op("act", lambda e, p=p, v_=v_, tb=tb, half=half: e.activation(
                            out=v_[:, tb, half * 512:(half + 1) * 512], in_=p[:, :], func=AF.Copy),
                            reads=[pb], pwrites=[v_b])
                    else:
                        S.op("dve", lambda e, p=p, v_=v_, tb=tb, half=half: e.tensor_copy(
                            out=v_[:, tb, half * 512:(half + 1) * 512], in_=p[:, :]),
                            reads=[pb], pwrites=[v_b])
            S.dma("sp", V[PAD + t * TT:PAD + (t + 1) * TT, :].rearrange("(b p) f -> p b f", p=128),
                  v_[:, :, :], reads=[v_b], pwrites=[Vb])
            S._commit((S.sem["pe"], S.cnt["pe"]), [hnb, sqb], [])
        S.flush()


def phase_wo(nc, S, T, OT, OTb, w_bf, wb, hT, hTb, h_in=None, h_inb=None):
    ntile = T // TT
    with ExitStack() as st:
        sb = lambda name, shape, dt: st.enter_context(nc.sbuf_tensor(_uid(name), shape, dt))
        w = sb("wo", [128, NCH, D], BF16)
        hx = [sb(f"hx{i}", [128, NCH, TT], F32) for i in range(2)]
        ot = [sb(f"ot{i}", [128, NCH, TT], BF16) for i in range(2)]
        ps = [st.enter_context(nc.psum_tensor(_uid(f"ps{i}"), [128, TT], F32)) for i in range(8)]
        B = Buf
        wbuf = B("w")
        hxb = [B("hx0"), B("hx1")]
        otb = [B("ot0"), B("ot1")]
        psb = [B(f"ps{i}") for i in range(8)]
        def load(t):
            S.dma("sp", hx[t % 2][:, :, :],
                  (hT if h_in is None else h_in)[:, t * TT:(t + 1) * TT].rearrange("(k p) n -> p k n", p=128),
                  reads=[(hTb if h_in is None else h_inb)[t]], writes=[hxb[t % 2]])
            S.dma("sp", ot[t % 2][:, :, :],
                  OT[:, t * TT:(t + 1) * TT].rearrange("(k p) n -> p k n", p=128),
                  reads=[OTb], writes=[otb[t % 2]])
        load(0)
        for k in range(NCH):
            S.dma("sp", w[:, k, :], w_bf[k * 128:(k + 1) * 128, :], reads=[wb[k]], pwrites=[wbuf])

        for t in range(ntile):
            if t + 1 < ntile:
                load(t + 1)
            x, xb = hx[t % 2], hxb[t % 2]
            o_, o_b = ot[t % 2], otb[t % 2]
            for o in range(NCH):
                p, pb = ps[o % 8], psb[o % 8]
                S.mm([(p[:, :], w[:, k, o * 128:(o + 1) * 128], o_[:, k, :]) for k in range(NCH)],
                     reads=[o_b, wbuf], writes=[pb])
                S.op("dve", lambda e, p=p, x=x, o=o: e.tensor_tensor(
                    out=x[:, o, :], in0=p[:, :], in1=x[:, o, :], op=ALU.add),
                    reads=[pb, xb], pwrites=[xb])
            S.dma("sp", hT[:, t * TT:(t + 1) * TT].rearrange("(k p) n -> p k n", p=128),
                  x[:, :, :], reads=[xb], writes=[hTb[t]])
        S.flush()


def phase_final(nc, S, T, hT, hTb, g_dram, outT, outb):
    ntile = T // TT
    with ExitStack() as st:
        sb = lambda name, shape, dt: st.enter_context(nc.sbuf_tensor(_uid(name), shape, dt))
        hx = [sb(f"hx{i}", [128, NCH, TT], F32) for i in range(2)]
        ho = [sb(f"ho{i}", [128, NCH, TT], F32) for i in range(2)]
        sq = sb("sq", [128, NCH, TT], BF16)
        rstd = sb("rstd", [128, TT], F32)
        gcol = sb("gcol", [128, NCH], F32)
        ones = sb("ones", [128, 128], BF16)
        ps = [st.enter_context(nc.psum_tensor(_uid(f"ps{i}"), [128, TT], F32)) for i in range(2)]
        B = Buf
        sqb, rstdb, gcolb, onesb = B("sq"), B("rstd"), B("gcol"), B("ones")
        hxb = [B("hx0"), B("hx1")]
        hob = [B("ho0"), B("ho1")]
        psb = [B("ps0"), B("ps1")]
        S.op(PMS, lambda e: e.memset(ones[:, :], 1.0), writes=[onesb])
        S.dma("sp", gcol[:, :], g_dram, writes=[gcolb])

        def load(t):
            S.dma("sp", hx[t % 2][:, :, :],
                  hT[:, t * TT:(t + 1) * TT].rearrange("(k p) n -> p k n", p=128),
                  reads=[hTb[t]], writes=[hxb[t % 2]])
        load(0)
        for t in range(ntile):
            if t + 1 < ntile:
                load(t + 1)
            emit_norm(S, t, hx[t % 2], hxb[t % 2], sq, sqb, ho[t % 2], hob[t % 2], gcol, ones,
                      ps[t % 2], psb[t % 2], rstd, rstdb, gcolb, onesb)
            S.dma("sp", outT[:, t * TT:(t + 1) * TT].rearrange("(k p) n -> p k n", p=128),
                  ho[t % 2][:, :, :], reads=[hob[t % 2]], pwrites=[outb])
        S.flush(wait_bufs=[outb], wait_all=True)


def phase_attn_b(nc, S, T, QT, QTb, KT, KTb, V, Vb, OT, OTb, gtab, bconst_d, lam_d, gsub_d, lam_init):
    NKB, NQT = T // 128, T // TT
    with ExitStack() as st:
        sb = lambda name, shape, dt: st.enter_context(nc.sbuf_tensor(_uid(name), shape, dt))
        qt_ = [sb(f"qt{i}", [128, T], BF16) for i in range(2)]
        kt_ = [sb(f"kt{i}", [128, T], BF16) for i in range(2)]
        vv = [sb(f"vv{i}", [128, NKB, 128], BF16) for i in range(2)]
        gt = [sb(f"gt{i}", [128, 2, GW], F32) for i in range(2)]
        sbi = [sb(f"sbi{i}", [128, TT], F32) for i in range(4)]
        pt = [sb(f"pt{i}", [128, TT], BF16) for i in range(8)]
        ones = sb("ones", [128, 128], BF16)
        bconst = sb("bconst", [128, 32], F32)
        lam = sb("lam", [128, 256], F32)
        ltmp = sb("ltmp", [128, 64], F32)
        lsc = sb("lsc", [128, 8], F32)
        gsub = sb("gsub", [128, 2], F32)
        ef = [sb(f"ef{i}", [128, TT], F32) for i in range(5)]
        osq = sb("osq", [128, TT], BF16)
        ost = [sb(f"ost{i}", [128, TT], BF16) for i in range(2)]
        zs = [sb(f"zs{i}", [128, TT], F32) for i in range(2)]
        pvs = [sb(f"pvs{i}", [128, TT], F32) for i in range(2)]
        zsb = [Buf("zs0"), Buf("zs1")]
        pvsb = [Buf("pvs0"), Buf("pvs1")]
        deferred = []
        ps = [st.enter_context(nc.psum_tensor(_uid(f"ps{i}"), [128, TT], F32)) for i in range(8)]
        B = Buf
        qtb = [B("qt0"), B("qt1")]
        ktb = [B("kt0"), B("kt1")]
        vvb = [B("vv0"), B("vv1")]
        gtb = [B("gt0"), B("gt1")]
        sbib = [B(f"sbi{i}") for i in range(4)]
        ptb = [B(f"pt{i}") for i in range(8)]
        efb = [B(f"ef{i}") for i in range(5)]
        onesb, bcb, lamb, ltb, lscb, gsubb, osqb = (B("ones"), B("bc"), B("lam"), B("lt"), B("lsc"),
                                                    B("gsub"), B("osq"))
        ostb = [B("ost0"), B("ost1")]
        psb = [B(f"ps{i}") for i in range(8)]
        psS = [[ps[0], ps[1]], [ps[2], ps[3]]]
        psSb = [[psb[0], psb[1]], [psb[2], psb[3]]]
        PV, PVb = [ps[4], ps[5]], [psb[4], psb[5]]
        Z, Zb = [ps[6], ps[7]], [psb[6], psb[7]]

        S.op(PMS, lambda e: e.memset(ones[:, :], 1.0), writes=[onesb])
        S.dma("sp", bconst[:, :], bconst_d, writes=[bcb])
        S.dma("sp", lam[:, :], lam_d, writes=[lamb])
        S.dma("sp", gsub[:, 0:1], gsub_d, writes=[gsubb])
        S.op("dve", lambda e: e.scalar_tensor_tensor(out=ltmp[:, :], in0=lam[:, 0:64], scalar=1.0,
                                                     in1=lam[:, 64:128], op0=ALU.mult, op1=ALU.mult,
                                                     accum_out=lsc[:, 0:1]),
             reads=[lamb], writes=[ltb, lscb])
        S.op("dve", lambda e: e.scalar_tensor_tensor(out=ltmp[:, :], in0=lam[:, 128:192], scalar=1.0,
                                                     in1=lam[:, 192:256], op0=ALU.mult, op1=ALU.mult,
                                                     accum_out=lsc[:, 1:2]),
             reads=[lamb, lscb], writes=[ltb, lscb])
        S.op("act", lambda e: e.activation(out=lsc[:, 2:4], in_=lsc[:, 0:2], func=AF.Exp),
             reads=[lscb], writes=[lscb])
        S.op("dve", lambda e: e.tensor_tensor(out=lsc[:, 4:5], in0=lsc[:, 3:4], in1=lsc[:, 2:3],
                                              op=ALU.subtract), reads=[lscb], writes=[lscb])
        S.op("dve", lambda e: e.tensor_scalar(out=lsc[:, 5:6], in0=lsc[:, 4:5], scalar1=-float(lam_init),
                                              scalar2=None, op0=ALU.add), reads=[lscb], writes=[lscb])
        S.op("dve", lambda e: e.tensor_scalar(out=gsub[:, 1:2], in0=gsub[:, 0:1],
                                              scalar1=float(1.0 - lam_init), scalar2=None, op0=ALU.mult),
             reads=[gsubb], writes=[gsubb])
        neglam = lsc[:, 5:6]

        def load(h):
            i = h % 2
            S.dma("sp", qt_[i][:, :], QT[h * 128:(h + 1) * 128, :], reads=[QTb], writes=[qtb[i]])
            S.dma("sp", kt_[i][:, :], KT[h * 128:(h + 1) * 128, PAD:PAD + T], reads=[KTb], writes=[ktb[i]])
            S.dma("sp", vv[i][:, :, :],
                  V[PAD:PAD + T, h * 128:(h + 1) * 128].rearrange("(b p) f -> p b f", p=128),
                  reads=[Vb], writes=[vvb[i]])
            for j in range(2):
                S.dma("sp", gt[i][:, j, :], gtab[2 * h + j, :, :], pwrites=[gtb[i]],
                      reads=[])
        load(0)
        npt = 0
        nsb = 0
        for h in range(8):
            if h + 1 < 8:
                load(h + 1)
            i = h % 2
            q_, k_, v_, g_ = qt_[i], kt_[i], vv[i], gt[i]
            for qt in range(NQT):
                q0 = qt * TT

                def issue_S(kb):
                    for j in range(2):
                        S.mm([(psS[j][kb % 2][:, :], k_[j * 64:(j + 1) * 64, kb * 128:(kb + 1) * 128],
                               q_[j * 64:(j + 1) * 64, q0:q0 + TT])],
                             reads=[qtb[i], ktb[i]], writes=[psSb[j][kb % 2]])
                issue_S(0)
                for kb in range(NKB):
                    if kb + 1 < NKB:
                        issue_S(kb + 1)
                    d = kb * 128 - q0
                    pts = []
                    for j in range(2):
                        sp_, spb = psS[j][kb % 2], psSb[j][kb % 2]
                        p_, p_b = pt[npt % 8], ptb[npt % 8]
                        npt += 1
                        if NEAR_LO < d < NEAR_HI:
                            m0 = GC - d
                            s_, s_b = sbi[nsb % 4], sbib[nsb % 4]
                            nsb += 1
                            S.op("dve", lambda e, s_=s_, sp_=sp_, g_=g_, j=j, m0=m0: e.tensor_tensor(
                                out=s_[:, :], in0=sp_[:, :], in1=g_[:, j, m0:m0 + TT], op=ALU.add),
                                reads=[spb, gtb[i]], writes=[s_b])
                            S.op("act", lambda e, p_=p_, s_=s_: e.activation(
                                out=p_[:, :], in_=s_[:, :], func=AF.Exp), reads=[s_b], writes=[p_b])
                        else:
                            col = (16 if d > 0 else 0) + 2 * h + j
                            S.op("act", lambda e, p_=p_, sp_=sp_, col=col: e.activation(
                                out=p_[:, :], in_=sp_[:, :], func=AF.Exp, bias=bconst[:, col:col + 1]),
                                reads=[spb, bcb], writes=[p_b])
                        pts.append((p_, p_b))
                    for j in range(2):
                        p_, p_b = pts[j]
                        S.mm([(PV[j][:, :], v_[:, kb, :], p_[:, :])], reads=[p_b, vvb[i]],
                             writes=[PVb[j]] if kb == 0 else [], pwrites=[] if kb == 0 else [PVb[j]],
                             start=(kb == 0), stop=(kb == NKB - 1))
                    for j in range(2):
                        p_, p_b = pts[j]
                        S.mm([(Z[j][:, :], ones[:, :], p_[:, :])], reads=[p_b, onesb],
                             writes=[Zb[j]] if kb == 0 else [], pwrites=[] if kb == 0 else [Zb[j]],
                             start=(kb == 0), stop=(kb == NKB - 1))
                    if deferred and kb >= 1:
                        deferred.pop(0)(kb)
                while deferred:
                    deferred.pop(0)(NKB - 1)
                S.op("act", lambda e: e.activation(out=zs[0][:, :], in_=Z[0][:, :], func=AF.Copy),
                     reads=[Zb[0]], writes=[zsb[0]])
                S.op("dve", lambda e: e.tensor_copy(out=pvs[0][:, :], in_=PV[0][:, :]),
                     reads=[PVb[0]], writes=[pvsb[0]])
                S.op("act", lambda e: e.activation(out=zs[1][:, :], in_=Z[1][:, :], func=AF.Copy),
                     reads=[Zb[1]], writes=[zsb[1]])
                S.op("dve", lambda e: e.tensor_copy(out=pvs[1][:, :], in_=PV[1][:, :]),
                     reads=[PVb[1]], writes=[pvsb[1]])
                r0, r1, o0, t1, o = ef
                os_, os_b = ost[qt % 2], ostb[qt % 2]
                dst = OT[h * 128:(h + 1) * 128, q0:q0 + TT]

                def mk(os_=os_, os_b=os_b, dst=dst):
                    ops = []
                    ops.append(lambda kb: S.op("dve", lambda e: e.reciprocal(out=r0[:, :], in_=zs[0][:, :]),
                                               reads=[zsb[0]], writes=[efb[0]]))
                    ops.append(lambda kb: S.op("dve", lambda e: e.reciprocal(out=r1[:, :], in_=zs[1][:, :]),
                                               reads=[zsb[1]], writes=[efb[1]]))
                    ops.append(lambda kb: S.op("dve", lambda e: e.tensor_tensor(
                        out=o0[:, :], in0=pvs[0][:, :], in1=r0[:, :], op=ALU.mult),
                        reads=[pvsb[0], efb[0]], writes=[efb[2]]))
                    ops.append(lambda kb: S.op("dve", lambda e: e.tensor_tensor(
                        out=t1[:, :], in0=pvs[1][:, :], in1=r1[:, :], op=ALU.mult),
                        reads=[pvsb[1], efb[1]], writes=[efb[3]]))
                    ops.append(lambda kb: S.op("dve", lambda e: e.scalar_tensor_tensor(
                        out=o[:, :], in0=t1[:, :], scalar=neglam, in1=o0[:, :], op0=ALU.mult, op1=ALU.add),
                        reads=[efb[2], efb[3], lscb], writes=[efb[4]]))
                    ops.append(lambda kb: S.op("act", lambda e: e.activation(
                        out=osq[:, :], in_=o[:, :], func=AF.Square), reads=[efb[4]], writes=[osqb]))

                    def ones_mm(kb):
                        pm, pmb = psS[0][kb % 2], psSb[0][kb % 2]
                        S.mm([(pm[:, :], ones[:, :], osq[:, :])], reads=[osqb, onesb], writes=[pmb])
                        S.op("act", lambda e, pm=pm: e.activation(out=r0[:, :], in_=pm[:, :], func=AF.Sqrt,
                                                                  scale=1.0 / 128, bias=EPS),
                             reads=[pmb], writes=[efb[0]])
                    ops.append(ones_mm)
                    ops.append(lambda kb: S.op("dve", lambda e: e.reciprocal(out=r0[:, :], in_=r0[:, :]),
                                               reads=[efb[0]], writes=[efb[0]]))
                    ops.append(lambda kb: S.op("dve", lambda e: e.scalar_tensor_tensor(
                        out=os_[:, :], in0=o[:, :], scalar=gsub[:, 1:2], in1=r0[:, :],
                        op0=ALU.mult, op1=ALU.mult), reads=[efb[4], efb[0], gsubb], writes=[os_b]))
                    ops.append(lambda kb: S.dma("sp", dst, os_[:, :], reads=[os_b], pwrites=[OTb]))
                    return ops
                deferred = mk()
        while deferred:
            deferred.pop(0)(NKB - 1)
        S.flush()


def phase_qkv_c(nc, S, T, hT, hTb, w_bf, wb, g_dram, QT, QTb, KT, KTb, V, Vb,
                ctab_d, stab_d, pmat_d, qkg_d, h_in=None, h_inb=None):
    ntile = T // TT
    with ExitStack() as st:
        sb = lambda name, shape, dt: st.enter_context(nc.sbuf_tensor(_uid(name), shape, dt))
        w = sb("wqkv", [128, NCH, 1536], BF16)
        hx = [sb(f"hx{i}", [128, NCH, TT], F32) for i in range(2)]
        sq = sb("sq", [128, NCH, TT], BF16)
        hn = sb("hn", [128, NCH, TT], BF16)
        rstd = sb("rstd", [128, TT], F32)
        gcol = sb("gcol", [128, NCH], F32)
        ones = sb("ones", [128, 128], BF16)
        oblk = sb("oblk", [128, 128], BF16)
        pm32 = sb("pm32", [128, 128], F32)
        pmat = sb("pmat", [128, 128], BF16)
        qkg = sb("qkg", [128, 2], F32)
        ctab = sb("ctab", [128, T], F32)
        stab = sb("stab", [128, T], F32)
        sq2 = [sb(f"sq2{i}", [128, TT], BF16) for i in range(2)]
        qg = [sb(f"qg{i}", [128, TT], BF16) for i in range(2)]
        qgf = [sb(f"qgf{i}", [128, TT], F32) for i in range(2)]
        qgfb = [Buf("qgf0"), Buf("qgf1")]
        rs = [sb(f"rs{i}", [128, TT], F32) for i in range(2)]
        t1 = [sb(f"t1{i}", [128, TT], F32) for i in range(2)]
        t2 = [sb(f"t2{i}", [128, TT], F32) for i in range(2)]
        qk = [sb(f"qk{i}", [128, 10, TT], BF16) for i in range(2)]
        vs = [sb(f"vs{i}", [128, 4, 256], BF16) for i in range(2)]
        ps = [st.enter_context(nc.psum_tensor(_uid(f"ps{i}"), [128, TT], F32)) for i in range(8)]
        B = Buf
        wbuf, sqb, hnb, rstdb, gcolb, onesb, oblkb, pm32b, pmatb, qkgb, ctb, stb = (
            B("w"), B("sq"), B("hn"), B("rstd"), B("gcol"), B("ones"), B("oblk"), B("pm32"), B("pmat"),
            B("qkg"), B("ct"), B("st"))
        hxb = [B("hx0"), B("hx1")]
        sq2b = [B("a"), B("b")]
        qgb = [B("a"), B("b")]
        rsb = [B("a"), B("b")]
        t1b = [B("a"), B("b")]
        t2b = [B("a"), B("b")]
        qkb = [B("qk0"), B("qk1")]
        vsb = [B("vs0"), B("vs1")]
        psb = [B(f"ps{i}") for i in range(8)]
        S.op(PMS, lambda e: e.memset(ones[:, :], 1.0), writes=[onesb])
        S.op(PMS, lambda e: e.memset(oblk[:, :], 0.0), writes=[oblkb])
        if _os.environ.get("K_C2", "0") != "1":
            S.op(PMS, lambda e: e.memset(oblk[0:64, 0:64], 1.0), writes=[oblkb])
            S.op(PMS, lambda e: e.memset(oblk[64:128, 64:128], 1.0), writes=[oblkb])
        S.dma("sp", gcol[:, :], g_dram, writes=[gcolb])
        if _os.environ.get("K_C3", "0") == "2":
            S.dma("sp", pm32[:, :], pmat_d, writes=[pm32b])
            S.op("dve", lambda e: e.memset(pmat[:, :], 1.0), writes=[pmatb])
        elif _os.environ.get("K_C3", "0") != "1":
            S.dma("sp", pm32[:, :], pmat_d, writes=[pm32b])
            S.op("act", lambda e: e.activation(out=pmat[:, :], in_=pm32[:, :], func=AF.Copy),
                 reads=[pm32b], writes=[pmatb])
        else:
            S.op("dve", lambda e: e.memset(pmat[:, :], 1.0), writes=[pmatb])
        if _os.environ.get("K_C4", "0") != "1":
            S.dma("sp", qkg[:, :], qkg_d, writes=[qkgb])
        if _os.environ.get("K_C5", "0") != "1":
            S.dma("sp", ctab[:, :], ctab_d, writes=[ctb])
            S.dma("sp", stab[:, :], stab_d, writes=[stb])
        def load(t):
            S.dma("sp", hx[t % 2][:, :, :],
                  (hT if h_in is None else h_in)[:, t * TT:(t + 1) * TT].rearrange("(k p) n -> p k n", p=128),
                  reads=[(hTb if h_in is None else h_inb)[t]], writes=[hxb[t % 2]])
        load(0)
        for k in range(NCH):
            S.dma("sp", w[:, k, :], w_bf[k * 128:(k + 1) * 128, :], reads=[wb[k]], pwrites=[wbuf])

        n = 0
        for t in range(ntile):
            if t + 1 < ntile:
                load(t + 1)
            x, xb = hx[t % 2], hxb[t % 2]
            emit_norm(S, t, x, xb, sq, sqb, hn, hnb, gcol, ones, ps[7], psb[7], rstd, rstdb, gcolb, onesb)
            q_, q_b = qk[t % 2], qkb[t % 2]
            t0 = t * TT
            for c in range(10):
                isq = c < 8
                pA, pAb = ps[(2 * c) % 4], psb[(2 * c) % 4]
                pB, pBb = ps[(2 * c) % 4 + 1], psb[(2 * c) % 4 + 1]
                pC, pCb = ps[4 + c % 2], psb[4 + c % 2]
                i = n % 2
                n += 1
                gq = qkg[:, 0:1] if isq else qkg[:, 1:2]
                _cn = int(_os.environ.get('K_CN', '99'))
                if _cn > 0:
                    S.mm([(pA[:, :], w[:, k, c * 128:(c + 1) * 128], hn[:, k, :]) for k in range(NCH)],
                         reads=[hnb, wbuf], writes=[pAb])
                if _cn > 1:
                    S.op("act", lambda e, pA=pA, i=i: e.activation(out=sq2[i][:, :], in_=pA[:, :], func=AF.Square),
                         reads=[pAb], writes=[sq2b[i]])
                if _cn > 2:
                    S.op("act", lambda e, pA=pA, i=i, gq=gq: e.activation(
                        out=qgf[i][:, :], in_=pA[:, :], func=AF.Copy, scale=gq),
                        reads=[pAb, qkgb], writes=[qgfb[i]])
                    S.op("pool", lambda e, i=i: e.tensor_copy(out=qg[i][:, :], in_=qgf[i][:, :]),
                         reads=[qgfb[i]], writes=[qgb[i]])
                if _cn > 3:
                    S.mm([(pB[:, :], oblk[:, :], sq2[i][:, :])], reads=[sq2b[i], oblkb], writes=[pBb])
                if _cn > 4:
                    S.mm([(pC[:, :], pmat[:, :], qg[i][:, :])], reads=[qgb[i], pmatb], writes=[pCb])
                if _cn > 5:
                    S.op("act", lambda e, pB=pB, i=i: e.activation(out=rs[i][:, :], in_=pB[:, :], func=AF.Sqrt,
                                                                  scale=1.0 / 64, bias=EPS),
                         reads=[pBb], writes=[rsb[i]])
                if _cn > 6:
                    S.op("dve", lambda e, i=i: e.reciprocal(out=rs[i][:, :], in_=rs[i][:, :]),
                         reads=[rsb[i]], writes=[rsb[i]])
                if _cn > 7:
                    S.op("pool", lambda e, i=i, t0=t0: e.tensor_tensor(
                        out=t1[i][:, :], in0=qgf[i][:, :], in1=ctab[:, t0:t0 + TT], op=ALU.mult),
                        reads=[qgfb[i], ctb], writes=[t1b[i]])
                if _cn > 8:
                    S.op("dve", lambda e, pC=pC, i=i, t0=t0: e.tensor_tensor(
                        out=t2[i][:, :], in0=pC[:, :], in1=stab[:, t0:t0 + TT], op=ALU.mult),
                        reads=[pCb, stb], writes=[t2b[i]])
                if _cn > 9:
                    S.op(PTT, lambda e, i=i: e.tensor_tensor(out=t1[i][:, :], in0=t1[i][:, :], in1=t2[i][:, :],
                                                               op=ALU.add),
                         reads=[t1b[i], t2b[i]], writes=[t1b[i]])
                if _cn > 10:
                    S.op("dve", lambda e, i=i, q_=q_, c=c, isq=isq: e.scalar_tensor_tensor(
                        out=q_[:, c, :], in0=t1[i][:, :], scalar=(0.125 if isq else 1.0), in1=rs[i][:, :],
                        op0=ALU.mult, op1=ALU.mult), reads=[t1b[i], rsb[i]], pwrites=[q_b])
            if _os.environ.get("K_C6", "0") != "1":
                S.dma("sp", QT[:, t0:t0 + TT].rearrange("(c p) n -> p c n", p=128),
                      q_[:, 0:8, :], reads=[q_b], pwrites=[QTb])
                S.dma("sp", KT[0:256, PAD + t0:PAD + t0 + TT].rearrange("(c p) n -> p c n", p=128),
                      q_[:, 8:10, :], reads=[q_b], pwrites=[KTb])
            v_, v_b = vs[t % 2], vsb[t % 2]
            for tb in range(4 if _os.environ.get("K_C1", "0") != "1" else 0):
                p, pb = ps[6], psb[6]
                S.mm([(p[:, 0:256], hn[:, k, tb * 128:(tb + 1) * 128], w[:, k, 1280:1536])
                      for k in range(NCH)], reads=[hnb, wbuf], writes=[pb])
                S.op("act", lambda e, p=p, v_=v_, tb=tb: e.activation(
                    out=v_[:, tb, :], in_=p[:, 0:256], func=AF.Copy), reads=[pb], pwrites=[v_b])
            if _os.environ.get("K_C6", "0") != "1":
                S.dma("sp", V[PAD + t0:PAD + t0 + TT, 0:256].rearrange("(b p) f -> p b f", p=128),
                      v_[:, :, :], reads=[v_b], pwrites=[Vb])
            S._commit((S.sem["pe"], S.cnt["pe"]), [hnb, sqb], [])
        S.flush()


def phase_attn_c(nc, S, T, QT, QTb, KT, KTb, V, Vb, OT, OTb):
    NKB, NQT = T // 128, T // TT
    with ExitStack() as st:
        sb = lambda name, shape, dt: st.enter_context(nc.sbuf_tensor(_uid(name), shape, dt))
        qt_ = [sb(f"qt{i}", [128, T], BF16) for i in range(2)]
        k2 = [sb(f"k2{i}", [128, T], BF16) for i in range(2)]
        ve = [sb(f"ve{i}", [128, NKB, 128], BF16) for i in range(2)]
        vo = [sb(f"vo{i}", [128, NKB, 128], BF16) for i in range(2)]
        pt = [sb(f"pt{i}", [128, 2, TT], BF16) for i in range(4)]
        zz = [sb(f"zz{i}", [128, TT], F32) for i in range(2)]
        zzs = [sb(f"zzs{i}", [128, TT], F32) for i in range(2)]
        rzs = [sb(f"rzs{i}", [128, TT], F32) for i in range(2)]
        ost = [sb(f"ost{i}", [128, TT], BF16) for i in range(2)]
        sS = [st.enter_context(nc.psum_tensor(_uid(f"sS{i}"), [128, 2, TT], F32)) for i in range(2)]
        E = [st.enter_context(nc.psum_tensor(_uid(f"E{i}"), [128, TT], F32)) for i in range(2)]
        O = [st.enter_context(nc.psum_tensor(_uid(f"O{i}"), [128, TT], F32)) for i in range(2)]
        B = Buf
        qtb = [B("qt0"), B("qt1")]
        k2b = [B("k0"), B("k1")]
        veb = [B("a"), B("b")]
        vob = [B("a"), B("b")]
        ptb = [B(f"pt{i}") for i in range(4)]
        zzb = [B("a"), B("b")]
        zzsb = [B("a"), B("b")]
        rzsb = [B("a"), B("b")]
        ostb = [B("a"), B("b")]
        sSb = [B("s0"), B("s1")]
        Eb = [B("e0"), B("e1")]
        Ob = [B("o0"), B("o1")]
        for i in range(2):
            S.op(PMS, lambda e, i=i: e.memset(ve[i][:, :, 64:128], 1.0), writes=[veb[i]])
            S.op(PMS, lambda e, i=i: e.memset(vo[i][:, :, 0:64], 1.0), writes=[vob[i]])

        def load(c):
            i = c % 2
            g = c // 2
            S.dma("sp", qt_[i][:, :], QT[c * 128:(c + 1) * 128, :], reads=[QTb], writes=[qtb[i]])
            S.dma("sp", k2[i][0:64, :], KT[g * 64:(g + 1) * 64, PAD:PAD + T], reads=[KTb], writes=[k2b[i]])
            S.dma("sp", k2[i][64:128, :], KT[g * 64:(g + 1) * 64, PAD:PAD + T], reads=[KTb], pwrites=[k2b[i]])
            vsrc = V[PAD:PAD + T, g * 64:(g + 1) * 64].rearrange("(b p) f -> p b f", p=128)
            S.dma("sp", ve[i][:, :, 0:64], vsrc, reads=[Vb, veb[i]], pwrites=[veb[i]])
            S.dma("sp", vo[i][:, :, 64:128], vsrc, reads=[Vb, vob[i]], pwrites=[vob[i]])
        load(0)
        npt = 0
        nq = 0
        for c in range(8):
            if c + 1 < 8:
                load(c + 1)
            i = c % 2
            q_, k_ = qt_[i], k2[i]
            for qt in range(NQT):
                q0 = qt * TT
                a = nq % 2
                nq += 1

                def issue_S(kb):
                    x = kb % 2
                    for j in range(2):
                        S.mm([(sS[x][:, j, :], k_[j * 64:(j + 1) * 64, kb * 128:(kb + 1) * 128],
                               q_[j * 64:(j + 1) * 64, q0:q0 + TT])],
                             reads=[qtb[i], k2b[i]],
                             writes=[sSb[x]] if j == 0 else [], pwrites=[] if j == 0 else [sSb[x]])
                issue_S(0)
                for kb in range(NKB):
                    if kb + 1 < NKB:
                        issue_S(kb + 1)
                    x = kb % 2
                    n = npt % 4
                    npt += 1
                    S.op("act", lambda e, n=n, x=x: e.activation(
                        out=pt[n][:, :, :], in_=sS[x][:, :, :], func=AF.Exp), reads=[sSb[x]], writes=[ptb[n]])
                    first, last = (kb == 0), (kb == NKB - 1)
                    S.mm([(E[a][:, :], ve[i][:, kb, :], pt[n][:, 0, :])], reads=[ptb[n], veb[i]],
                         writes=[Eb[a]] if first else [], pwrites=[] if first else [Eb[a]],
                         start=first, stop=last)
                    S.mm([(O[a][:, :], vo[i][:, kb, :], pt[n][:, 1, :])], reads=[ptb[n], vob[i]],
                         writes=[Ob[a]] if first else [], pwrites=[] if first else [Ob[a]],
                         start=first, stop=last)
                S.op("dve", lambda e, a=a: e.tensor_copy(out=zz[a][0:64, :], in_=O[a][0:64, :]),
                     reads=[Ob[a]], writes=[zzb[a]])
                S.op("dve", lambda e, a=a: e.tensor_copy(out=zz[a][64:128, :], in_=E[a][64:128, :]),
                     reads=[Eb[a]], pwrites=[zzb[a]])
                S.dma("sp", zzs[a][0:64, :], zz[a][64:128, :], reads=[zzb[a]], writes=[zzsb[a]])
                S.dma("sp", zzs[a][64:128, :], zz[a][0:64, :], reads=[zzb[a]], pwrites=[zzsb[a]])
                S.op("dve", lambda e, a=a: e.reciprocal(out=rzs[a][:, :], in_=zzs[a][:, :]),
                     reads=[zzsb[a]], writes=[rzsb[a]])
                S.op("dve", lambda e, a=a: e.tensor_tensor(out=ost[a][0:64, :], in0=E[a][0:64, :],
                                                          in1=rzs[a][0:64, :], op=ALU.mult),
                     reads=[Eb[a], rzsb[a]], writes=[ostb[a]])
                S.op("dve", lambda e, a=a: e.tensor_tensor(out=ost[a][64:128, :], in0=O[a][64:128, :],
                                                          in1=rzs[a][64:128, :], op=ALU.mult),
                     reads=[Ob[a], rzsb[a]], pwrites=[ostb[a]])
                S.dma("sp", OT[c * 128:(c + 1) * 128, q0:q0 + TT], ost[a][:, :], reads=[ostb[a]], pwrites=[OTb])
        S.flush()


A_DIL = [1] * 6 + [4] * 5 + [16] * 5
A_GRP = [0] * 6 + [1] * 5 + [2] * 5
A_NH = [6, 5, 5]


def phase_attn_a(nc, S, T, QT, QTb, KT, KTb, V, Vb, PVun, PVb_d, Zbc, Zb_d, bm_d):
    with ExitStack() as st:
        sb = lambda name, shape, dt: st.enter_context(nc.sbuf_tensor(_uid(name), shape, dt))
        NBMAX = T // 128 + 16
        qa = [sb(f"qa{i}", [128, T], BF16) for i in range(2)]
        ka = [sb(f"ka{i}", [128, T + 2 * PAD], BF16) for i in range(2)]
        va = [[sb(f"va{i}{j}", [128, NBMAX, 128], BF16) for j in range(2)] for i in range(2)]
        bm = [sb(f"bm{i}", [128, 2, 4, 128], F32) for i in range(2)]
        pvb = [sb(f"pvb{i}", [128, T], F32) for i in range(2)]
        zb = [sb(f"zb{i}", [128, T], F32) for i in range(2)]
        sbi = [sb(f"sbi{i}", [128, 128], F32) for i in range(4)]
        pt = [sb(f"pt{i}", [128, 128], BF16) for i in range(8)]
        olo = sb("olo", [128, 128], BF16)
        ohi = sb("ohi", [128, 128], BF16)
        ps = [st.enter_context(nc.psum_tensor(_uid(f"ps{i}"), [128, TT], F32)) for i in range(8)]
        B = Buf
        qab = [B("a"), B("b")]
        kab = [B("a"), B("b")]
        vab = [[B("a"), B("b")], [B("c"), B("d")]]
        bmb = [B("a"), B("b")]
        pvbb = [B("a"), B("b")]
        zbb = [B("a"), B("b")]
        sbib = [B("s") for _ in range(4)]
        ptb = [B("p") for _ in range(8)]
        olob, ohib = B("olo"), B("ohi")
        psb = [B(f"ps{i}") for i in range(8)]
        S.op(PMS, lambda e: e.memset(olo[:, :], 0.0), writes=[olob])
        S.op(PMS, lambda e: e.memset(olo[:, 0:64], 1.0), writes=[olob])
        S.op(PMS, lambda e: e.memset(ohi[:, :], 0.0), writes=[ohib])
        S.op(PMS, lambda e: e.memset(ohi[:, 64:128], 1.0), writes=[ohib])
        for i in range(2):
            for j in range(2):
                S.op(PMS, lambda e, i=i, j=j: e.memset(va[i][j][:, :, :], 0.0), writes=[vab[i][j]])
        on = [olo, ohi]
        onb = [olob, ohib]

        def load(c):
            i = c % 2
            S.dma("sp", qa[i][:, :], QT[c * 128:(c + 1) * 128, :], reads=[QTb], writes=[qab[i]])
            S.dma("sp", ka[i][:, :], KT[c * 128:(c + 1) * 128, :], reads=[KTb], writes=[kab[i]])
            S.dma("sp", bm[i][:, :, :, :], bm_d[c, :, :].rearrange("p (s v j) -> p s v j", s=2, v=4),
                  writes=[bmb[i]])
            for s_ in range(2):
                h = 2 * c + s_
                d = A_DIL[h]
                nbl = T // (128 * d) + 1
                for r in range(d):
                    start = PAD + r - 64 * d
                    src = V[start:start + d * (128 * nbl - 1) + 1:d, h * 64:(h + 1) * 64].rearrange(
                        "(m i) f -> i m f", i=128)
                    S.dma("sp", va[i][s_][:, r * nbl:(r + 1) * nbl, s_ * 64:(s_ + 1) * 64], src,
                          reads=[Vb, vab[i][s_]], pwrites=[vab[i][s_]])
        load(0)
        nn = 0
        for c in range(8):
            if c + 1 < 8:
                load(c + 1)
            i = c % 2
            groups = []
            for s_ in range(2):
                h = 2 * c + s_
                d = A_DIL[h]
                nq = (T // d) // 128
                for r in range(d):
                    for b in range(nq):
                        groups.append((s_, d, nq, r, b))

            def stage1(g):
                nonlocal nn
                s_, d, nq, r, b = g
                nbl = nq + 1
                rows = slice(s_ * 64, (s_ + 1) * 64)
                qsl = slice(r + 128 * d * b, r + 128 * d * b + 127 * d + 1, d)
                pts = []
                for mi in range(2):
                    m = b + mi
                    k0 = PAD + r - 64 * d + 128 * d * m
                    ksl = slice(k0, k0 + 127 * d + 1, d)
                    if mi == 0:
                        var = 1 if b == 0 else 0
                    else:
                        var = 3 if b == nq - 1 else 2
                    n4, n8 = nn % 4, nn % 8
                    nn += 1
                    sp_, spb = ps[n4], psb[n4]
                    S.mm([(sp_[:, 0:128], ka[i][rows, ksl], qa[i][rows, qsl])],
                         reads=[kab[i], qab[i]], writes=[spb])
                    S.op("dve", lambda e, n4=n4, sp_=sp_, s_=s_, var=var, i=i: e.tensor_tensor(
                        out=sbi[n4][:, :], in0=sp_[:, 0:128], in1=bm[i][:, s_, var, :], op=ALU.add),
                        reads=[spb, bmb[i]], writes=[sbib[n4]])
                    S.op("act", lambda e, n4=n4, n8=n8: e.activation(
                        out=pt[n8][:, :], in_=sbi[n4][:, :], func=AF.Exp),
                        reads=[sbib[n4]], writes=[ptb[n8]])
                    pts.append((n8, r * nbl + m))
                return pts, (nn // 2) % 2

            def stage2(g, st1):
                s_, d, nq, r, b = g
                pts, a = st1
                rows = slice(s_ * 64, (s_ + 1) * 64)
                qsl = slice(r + 128 * d * b, r + 128 * d * b + 127 * d + 1, d)
                pv_, pv_b = ps[4 + a], psb[4 + a]
                z_, z_b = ps[6 + a], psb[6 + a]
                S.mm([(pv_[:, 0:128], va[i][s_][:, blk, :], pt[n8][:, :]) for (n8, blk) in pts],
                     reads=[ptb[n8] for (n8, _) in pts] + [vab[i][s_]], writes=[pv_b])
                S.mm([(z_[:, 0:128], on[s_][:, :], pt[n8][:, :]) for (n8, blk) in pts],
                     reads=[ptb[n8] for (n8, _) in pts] + [onb[s_]], writes=[z_b])
                S.op("act", lambda e, pv_=pv_, rows=rows, qsl=qsl, i=i: e.activation(
                    out=pvb[i][rows, qsl], in_=pv_[rows, 0:128], func=AF.Copy),
                    reads=[pv_b], pwrites=[pvbb[i]])
                S.op("dve", lambda e, z_=z_, rows=rows, qsl=qsl, i=i: e.tensor_copy(
                    out=zb[i][rows, qsl], in_=z_[rows, 0:128]),
                    reads=[z_b], pwrites=[zbb[i]])

            st = stage1(groups[0])
            for gi, g in enumerate(groups):
                nxt = stage1(groups[gi + 1]) if gi + 1 < len(groups) else None
                stage2(g, st)
                st = nxt
            S.dma("sp", PVun[c * 128:(c + 1) * 128, :], pvb[i][:, :], reads=[pvbb[i]], pwrites=[PVb_d])
            S.dma("sp", Zbc[2 * c:2 * c + 1, :], zb[i][0:1, :], reads=[zbb[i]], pwrites=[Zb_d])
            S.dma("sp", Zbc[2 * c + 1:2 * c + 2, :], zb[i][64:65, :], reads=[zbb[i]], pwrites=[Zb_d])
        S.flush()


def phase_comb_a(nc, S, T, PVun, PVb_d, Zc, Zb_d, OT, OTb, asel_d, bsel_d, esel_d):
    ntile = T // TT
    with ExitStack() as st:
        sb = lambda name, shape, dt: st.enter_context(nc.sbuf_tensor(_uid(name), shape, dt))
        zt = [sb(f"zt{i}", [16, TT], F32) for i in range(2)]
        pv = [sb(f"pv{i}", [128, NCH, TT], F32) for i in range(2)]
        asel = sb("asel", [16, 3], F32)
        bsel = sb("bsel", [3, 16], F32)
        esel = sb("esel", [16, NCH * 128], F32)
        o33 = sb("o33", [3, 3], F32)
        sg = sb("sg", [3, TT], F32)
        rt = sb("rt", [3, TT], F32)
        al = sb("al", [3, TT], F32)
        rz = sb("rz", [16, TT], F32)
        f16 = sb("f16", [16, TT], F32)
        ost = [sb(f"ost{i}", [128, NCH, TT], BF16) for i in range(2)]
        ps = [st.enter_context(nc.psum_tensor(_uid(f"ps{i}"), [128, TT], F32)) for i in range(8)]
        B = Buf
        ztb = [B("a"), B("b")]
        pvb = [B("a"), B("b")]
        aselb, bselb, eselb, o33b, sgb, rtb, alb, rzb, f16b = (B("a"), B("b"), B("e"), B("c"), B("d"), B("e"),
                                                               B("f"), B("g"), B("h"))
        ostb = [B("a"), B("b")]
        psb = [B(f"ps{i}") for i in range(8)]
        S.dma("sp", asel[:, :], asel_d, writes=[aselb])
        S.dma("sp", bsel[:, :], bsel_d, writes=[bselb])
        S.dma("sp", esel[:, :], esel_d, writes=[eselb])
        S.op(PMS, lambda e: e.memset(o33[:, :], 1.0), writes=[o33b])

        def load(t):
            S.dma("sp", zt[t % 2][:, :], Zc[:, t * TT:(t + 1) * TT], reads=[Zb_d], writes=[ztb[t % 2]])
            S.dma("sp", pv[t % 2][:, :, :], PVun[:, t * TT:(t + 1) * TT].rearrange("(k p) n -> p k n", p=128),
                  reads=[PVb_d], writes=[pvb[t % 2]])
        load(0)
        for t in range(ntile):
            if t + 1 < ntile:
                load(t + 1)
            z_, z_b = zt[t % 2], ztb[t % 2]
            p_, p_b = pv[t % 2], pvb[t % 2]
            S.mm([(ps[0][0:3, :], asel[:, :], z_[:, :])], reads=[z_b, aselb], writes=[psb[0]])
            S.op("dve", lambda e: e.tensor_copy(out=sg[:, :], in_=ps[0][0:3, :]), reads=[psb[0]], writes=[sgb])
            S.mm([(ps[1][0:3, :], o33[:, :], sg[:, :])], reads=[sgb, o33b], writes=[psb[1]])
            S.op("dve", lambda e: e.reciprocal(out=rt[:, :], in_=ps[1][0:3, :]), reads=[psb[1]], writes=[rtb])
            S.op("dve", lambda e: e.scalar_tensor_tensor(out=al[:, :], in0=sg[:, :], scalar=3.0, in1=rt[:, :],
                                                         op0=ALU.mult, op1=ALU.mult),
                 reads=[sgb, rtb], writes=[alb])
            S.mm([(ps[2][0:16, :], bsel[:, :], al[:, :])], reads=[alb, bselb], writes=[psb[2]])
            S.op("dve", lambda e, z_=z_: e.reciprocal(out=rz[:, :], in_=z_[:, :]), reads=[z_b], writes=[rzb])
            S.op("dve", lambda e: e.tensor_tensor(out=f16[:, :], in0=ps[2][0:16, :], in1=rz[:, :], op=ALU.mult),
                 reads=[psb[2], rzb], writes=[f16b])
            o_, o_b = ost[t % 2], ostb[t % 2]
            for c in range(NCH):
                pa, pab = ps[3 + c % 4], psb[3 + c % 4]
                S.mm([(pa[:, :], esel[:, c * 128:(c + 1) * 128], f16[:, :])], reads=[f16b, eselb], writes=[pab])
                S.op("dve", lambda e, pa=pa, p_=p_, c=c, o_=o_: e.tensor_tensor(
                    out=o_[:, c, :], in0=pa[:, :], in1=p_[:, c, :], op=ALU.mult),
                    reads=[pab, p_b], pwrites=[o_b])
            S.dma("sp", OT[:, t * TT:(t + 1) * TT].rearrange("(c p) n -> p c n", p=128), o_[:, :, :],
                  reads=[o_b], pwrites=[OTb])
        S.flush()


def lambda_init_fn(layer_idx):
    return 0.8 - 0.6 * math.exp(-0.3 * layer_idx)


def build(T, layers=(0, 1, 2, 3), final=True):
    nc = bass.Bass("TRN2", target_bir_lowering=False)
    ntile = T // TT
    import os
    limit = int(os.environ.get("K_STOP", "1000"))
    count = [0]

    def RUN(fn, *a):
        count[0] += 1
        if count[0] <= limit:
            fn(*a)
    with ExitStack() as st:
        S = Sched(nc, st)

        def ext(name, shape, dtype=F32):
            return nc.dram_tensor(name, list(shape), dtype, kind="ExternalInput").ap()

        def internal(name, shape, dtype):
            return nc.dram_tensor(name, list(shape), dtype, kind="Internal").ap()

        xT = ext("xT", [D, T])
        outT = nc.dram_tensor("outT", [D, T], F32, kind="ExternalOutput").ap()
        kinds = {li % 3 for li in layers}
        W = {"mlp_w_in": ext("mlp_w_in", [4, D, DFF]), "mlp_w_out": ext("mlp_w_out", [4, DFF, D])}
        if 0 in kinds:
            W["a_w_qkv"] = ext("a_w_qkv", [2, D, 3072])
            W["a_w_o"] = ext("a_w_o", [2, D, D])
        if 1 in kinds:
            W["b_w_qkv"] = ext("b_w_qkv", [1, D, 3072])
            W["b_w_o"] = ext("b_w_o", [1, D, D])
        if 2 in kinds:
            W["c_w_qkv"] = ext("c_w_qkv", [1, D, 1536])
            W["c_w_o"] = ext("c_w_o", [1, D, D])
        gmix = ext("gmix", [128, 32])
        gmlp = ext("gmlp", [128, 32])
        gfin = ext("gfin", [128, 8])
        gtab = ext("gtab", [16, 128, GW])
        bconst = ext("bconst", [128, 32])
        lam = ext("lam", [128, 256])
        gsub = ext("gsub", [128, 1])
        ctab = ext("ctab", [128, T])
        stab = ext("stab", [128, T])
        pmat = ext("pmat", [128, 128])
        qkg = ext("qkg", [128, 2])
        bm = ext("bm", [8, 128, 1024])
        asel = ext("asel", [16, 3])
        bsel = ext("bsel", [3, 16])
        esel = ext("esel", [16, 1024])

        hT = internal("hT", [D, T], F32)
        QT = internal("QT", [D, T], BF16)
        KT = internal("KT", [D, T + 2 * PAD], BF16)
        V = internal("V", [T + 2 * PAD, D], BF16)
        OT = internal("OT", [D, T], BF16)
        PVun = internal("PVun", [D, T], F32)
        Zbc = internal("Zbc", [16, T], F32)
        hTb = [Buf(f"hT{t}") for t in range(ntile)]
        QTb, KTb, Vb, OTb, PVb, Zb, outb = (Buf("QT"), Buf("KT"), Buf("V"), Buf("OT"), Buf("PV"),
                                            Buf("Z"), Buf("out"))

        wbf = {}
        castchain = Buf("castchain")
        pending_casts = []

        def emit_casts():
            while pending_casts:
                pending_casts.pop(0)()

        def cast(name, idx, rows, cols):
            src = W[name][idx]
            dst = internal(f"{name}{idx}_bf", [rows, cols], BF16)
            sv, dv = src, dst
            if cols > 2048:
                c = 2048 if cols % 2048 == 0 else 1024
                sv = sv.rearrange("a (b c) -> (a b) c", c=c)
                dv = dv.rearrange("a (b c) -> (a b) c", c=c)
            elif cols == 1024:
                sv = sv.rearrange("(a b) c -> a (b c)", b=2)
                dv = dv.rearrange("(a b) c -> a (b c)", b=2)
            b = Buf(name)
            pending_casts.append(lambda dv=dv, sv=sv, b=b: S.dma("pool", dv, sv, writes=[b, castchain]))
            wbf[(name, idx)] = (dst, [b] * 8)

        first_f32 = (layers[0] % 3) in (0, 1)
        for n_, li in enumerate(layers):
            kind, j = li % 3, li // 3
            pre = "abc"[kind]
            if n_ == 0 and first_f32:
                wbf[(f"{pre}_w_qkv", j)] = (None, None)
            else:
                cast(f"{pre}_w_qkv", j, D, 1536 if kind == 2 else 3072)
            cast(f"{pre}_w_o", j, D, D)
            cast("mlp_w_in", li, D, DFF)
            cast("mlp_w_out", li, DFF, D)

        if not first_f32:
            emit_casts()

        with ExitStack() as st2:
            zt = st2.enter_context(nc.sbuf_tensor(_uid("zt"), [128, NCH, PAD], BF16))
            ztb = Buf("zt")
            S.op(PMS, lambda e: e.memset(zt[:, :, :], 0.0), writes=[ztb])
            S.dma("sp", KT[:, 0:PAD].rearrange("(c p) n -> p c n", p=128), zt[:, :, :], reads=[ztb], pwrites=[KTb])
            S.dma("sp", KT[:, PAD + T:].rearrange("(c p) n -> p c n", p=128), zt[:, :, :], reads=[ztb], pwrites=[KTb])
            S.dma("sp", V[0:PAD, :].rearrange("(b p) f -> p b f", p=128), zt[:, :, :], reads=[ztb], pwrites=[Vb])
            S.dma("sp", V[PAD + T:, :].rearrange("(b p) f -> p b f", p=128), zt[:, :, :], reads=[ztb], pwrites=[Vb])
            S.flush(wait_bufs=[KTb, Vb])
        xTb = [Buf(f"xT{t}") for t in range(ntile)]

        for n_, li in enumerate(layers):
            kind, j = li % 3, li // 3
            pre = "abc"[kind]
            wq, wqb = wbf[(f"{pre}_w_qkv", j)]
            wf32 = W[f"{pre}_w_qkv"][j] if (n_ == 0 and first_f32) else None
            hin = (xT, xTb) if n_ == 0 else (None, None)
            wo, wob = wbf[(f"{pre}_w_o", j)]
            wi, wib = wbf[("mlp_w_in", li)]
            wo2, wo2b = wbf[("mlp_w_out", li)]
            g1 = gmix[:, 8 * li:8 * li + 8]
            g2 = gmlp[:, 8 * li:8 * li + 8]
            if kind == 0:
                RUN(phase_qkv_ab, nc, S, T, hT, hTb, wq, wqb, g1, QT, QTb, KT, KTb, V, Vb, wf32, *hin)
                emit_casts()
                RUN(phase_attn_a, nc, S, T, QT, QTb, KT, KTb, V, Vb, PVun, PVb, Zbc, Zb, bm)
                RUN(phase_comb_a, nc, S, T, PVun, PVb, Zbc, Zb, OT, OTb, asel, bsel, esel)
            elif kind == 1:
                RUN(phase_qkv_ab, nc, S, T, hT, hTb, wq, wqb, g1, QT, QTb, KT, KTb, V, Vb, wf32, *hin)
                emit_casts()
                RUN(phase_attn_b, nc, S, T, QT, QTb, KT, KTb, V, Vb, OT, OTb, gtab, bconst, lam, gsub,
                             lambda_init_fn(li))
            else:
                RUN(phase_qkv_c, nc, S, T, hT, hTb, wq, wqb, g1, QT, QTb, KT, KTb, V, Vb, ctab, stab, pmat, qkg, *hin)
                RUN(phase_attn_c, nc, S, T, QT, QTb, KT, KTb, V, Vb, OT, OTb)
            RUN(phase_wo, nc, S, T, OT, OTb, wo, wob, hT, hTb, *hin)
            RUN(phase_mlp, nc, S, T, hT, hTb, wi, wo2, [wib, wo2b], g2)
        if final:
            RUN(phase_final, nc, S, T, hT, hTb, gfin, outT, outb)
        else:
            with ExitStack() as st2:
                for t in range(ntile):
                    S.dma("sp", outT[:, t * TT:(t + 1) * TT], hT[:, t * TT:(t + 1) * TT], reads=[hTb[t]],
                          pwrites=[outb])
                S.flush(wait_bufs=[outb], wait_all=True)
    return nc


def t5_bucket_np(rel):
    rel = np.asarray(rel, dtype=np.int64)
    nb = 16
    max_exact = 8
    side = np.where(rel > 0, nb, 0)
    n = np.abs(rel)
    nf = np.maximum(n, 1).astype(np.float32)
    large = max_exact + (np.log(nf / np.float32(max_exact)) / np.float32(math.log(1024 / max_exact))
                         * np.float32(nb - max_exact)).astype(np.int32)
    large = np.minimum(large, nb - 1)
    return (side + np.where(n < max_exact, n, large)).astype(np.int64)


def host_tables(T, inputs):
    f32 = np.float32
    rb = np.asarray(inputs["rel_bias"], f32)
    tabs = {}
    i = np.arange(128)[:, None]
    col = np.arange(GW)[None, :]
    bk = t5_bucket_np(i - col + GC)
    tabs["gtab"] = np.ascontiguousarray(np.transpose(rb[bk], (2, 0, 1)))
    bc = np.concatenate([rb[15], rb[31]])[None, :]
    tabs["bconst"] = np.ascontiguousarray(np.repeat(bc, 128, axis=0))
    tabs["lam"] = np.ascontiguousarray(np.repeat(np.asarray(inputs["b_lambda"], f32).reshape(1, 256), 128, 0))
    tabs["gsub"] = np.ascontiguousarray(np.asarray(inputs["b_subln_g"], f32).reshape(128, 1))
    NEG = f32(-30000.0)
    bm = np.zeros((16, 4, 128, 128), f32)
    ii = np.arange(128)[:, None]
    jj = np.arange(128)[None, :]
    for h in range(16):
        d = A_DIL[h]
        o0 = ii - 64 - jj
        o1 = ii + 64 - jj
        b0 = rb[t5_bucket_np(o0 * d), h]
        b1 = rb[t5_bucket_np(o1 * d), h]
        v0 = ii >= jj
        v1 = ii <= jj
        bm[h, 0] = np.where(v0, b0, NEG)
        bm[h, 1] = np.where(v0 & (ii >= 64), b0, NEG)
        bm[h, 2] = np.where(v1, b1, NEG)
        bm[h, 3] = np.where(v1 & (ii < 64), b1, NEG)
    bm = bm.reshape(8, 2, 4, 128, 128).transpose(0, 3, 1, 2, 4).reshape(8, 128, 1024)
    tabs["bm"] = np.ascontiguousarray(bm)
    asel = np.zeros((16, 3), f32)
    bsel = np.zeros((3, 16), f32)
    esel = np.zeros((16, 1024), f32)
    for h in range(16):
        g = A_GRP[h]
        asel[h, g] = 1.0 / A_NH[g]
        bsel[g, h] = 1.0
        esel[h, h * 64:(h + 1) * 64] = 1.0
    tabs["asel"], tabs["bsel"], tabs["esel"] = asel, bsel, esel
    pos = np.arange(T)
    row = (pos // 64).astype(f32)
    colp = (pos % 64).astype(f32)
    inv = (f32(10000.0) ** (-np.arange(16, dtype=f32) / f32(16))).astype(f32)
    ang = np.concatenate([row[:, None] * inv, colp[:, None] * inv], axis=-1).astype(f32)
    cos, sin = np.cos(ang).astype(f32), np.sin(ang).astype(f32)
    ct = np.zeros((64, T), f32)
    stt = np.zeros((64, T), f32)
    pm = np.zeros((128, 128), f32)
    for a in range(2):
        for jh in range(2):
            for f in range(16):
                dd = a * 32 + jh * 16 + f
                ct[dd] = cos[:, a * 16 + f]
                stt[dd] = (-sin[:, a * 16 + f]) if jh == 0 else sin[:, a * 16 + f]
                other = a * 32 + (1 - jh) * 16 + f
                for hh in range(2):
                    pm[hh * 64 + other, hh * 64 + dd] = 1.0
    tabs["ctab"] = np.ascontiguousarray(np.concatenate([ct, ct], 0))
    tabs["stab"] = np.ascontiguousarray(np.concatenate([stt, stt], 0))
    tabs["pmat"] = pm
    qg = np.asarray(inputs["c_q_norm_g"], f32).reshape(64)
    kg = np.asarray(inputs["c_k_norm_g"], f32).reshape(64)
    tabs["qkg"] = np.ascontiguousarray(np.stack([np.tile(qg, 2), np.tile(kg, 2)], axis=1))
    def gl(a):
        a = np.asarray(a, f32).reshape(-1, 8, 128)
        return np.ascontiguousarray(a.transpose(2, 0, 1).reshape(128, -1))
    tabs["gmix"] = gl(inputs["norm_mix_g"])
    tabs["gmlp"] = gl(inputs["norm_mlp_g"])
    tabs["gfin"] = gl(inputs["norm_final_g"])
    for k in ("a_w_qkv", "a_w_o", "b_w_qkv", "b_w_o", "c_w_qkv", "c_w_o", "mlp_w_in", "mlp_w_out"):
        tabs[k] = np.ascontiguousarray(np.asarray(inputs[k], f32))
    return tabs


_PROGRAM_CACHE = {}


def kernel(**inputs):
    x = np.asarray(inputs["x"], np.float32)
    Bn, T, _ = x.shape
    key = (T,)
    if key not in _PROGRAM_CACHE:
        _PROGRAM_CACHE[key] = build(T)
    nc = _PROGRAM_CACHE[key]
    tabs = host_tables(T, inputs)
    in_maps = []
    for c in range(8):
        m = dict(tabs)
        m["xT"] = np.ascontiguousarray(x[c % Bn].T)
        in_maps.append(m)
    res = run_bass_kernel_spmd(nc, in_maps, core_ids=list(range(8)))
    out = np.stack([np.ascontiguousarray(res.results[b]["outT"].T) for b in range(Bn)], axis=0)
    return out.astype(np.float32)
```

```python
import math
from contextlib import ExitStack

import numpy as np
import concourse.bass as bass
import concourse.mybir as mybir
from concourse.bass_utils import run_bass_kernel_spmd

F32 = mybir.dt.float32
BF16 = mybir.dt.bfloat16
AF = mybir.ActivationFunctionType
ALU = mybir.AluOpType

D = 1024
NCH = 8
DFF = 4096
NFC = 32
EPS = 1e-6
TT = 512


import os as _os
PTT = "dve" if _os.environ.get("K_NOPTT", "0") == "1" else "pool"
PMS = "pool" if _os.environ.get("K_POOLMS", "0") == "1" else "dve"
_UID = [0]


def _uid(name):
    _UID[0] += 1
    return f"{name}_u{_UID[0]}"


class Buf:
    def __init__(self, name):
        self.name = name
        self.w = {}
        self.r = {}


class Sched:
    ENGS = ("pe", "act", "dve", "pool", "sp")

    def __init__(self, nc, stack, n_dma_sems=60):
        self.nc = nc
        self.sem = {e: stack.enter_context(nc.semaphore(f"sem_{e}")) for e in self.ENGS}
        self.cnt = {e: 0 for e in self.ENGS}
        self.dsem = [stack.enter_context(nc.semaphore(f"sem_dma{i}")) for i in range(n_dma_sems)]
        self.dcnt = [0] * n_dma_sems
        self.dnext = 0
        self.n_sw = 16
        self.dnext_sw = 0
        self.waited = {}
        self.q = {e: [] for e in self.ENGS}
        self.semobj = {}
        self.nblocks = 0

    def _key(self, sem):
        k = id(sem)
        self.semobj[k] = sem
        return k

    def _wait(self, e, toks):
        for k, val in toks.items():
            if self.waited.get((e, k), 0) >= val:
                continue
            self.waited[(e, k)] = val
            sem = self.semobj[k]
            self.q[e].append(lambda eng, sem=sem, val=val: eng.wait_ge(sem, val))

    def _deps(self, e, reads, writes, pwrites):
        toks = {}

        def add(d):
            for k, v in d.items():
                if v > toks.get(k, 0):
                    toks[k] = v
        for b in reads:
            add(b.w)
        for b in writes:
            add(b.w)
            add(b.r)
        for b in pwrites:
            add(b.r)
        if e == "pe":
            toks.pop(self._key(self.sem[e]), None)
        return toks

    def _commit(self, tok, reads, writes, pwrites=()):
        k = self._key(tok[0])
        for b in reads:
            if b.r.get(k, 0) < tok[1]:
                b.r[k] = tok[1]
        for b in writes:
            b.w = {k: tok[1]}
            b.r = {}
        for b in pwrites:
            if b.w.get(k, 0) < tok[1]:
                b.w[k] = tok[1]

    def op(self, e, fn, reads=(), writes=(), pwrites=()):
        self._wait(e, self._deps(e, reads, writes, pwrites))
        self.cnt[e] += 1
        sem = self.sem[e]
        self.q[e].append(lambda eng, fn=fn, sem=sem: fn(eng).then_inc(sem, 1))
        self._commit((sem, self.cnt[e]), reads, writes, pwrites)

    def mm(self, mms, reads=(), writes=(), pwrites=(), start=True, stop=True):
        e = "pe"
        self._wait(e, self._deps(e, reads, writes, pwrites))
        self.cnt[e] += 1
        sem = self.sem[e]
        n = len(mms)

        def run(eng, mms=mms, sem=sem, n=n, start=start, stop=stop):
            for i, (o, l, r) in enumerate(mms):
                ins = eng.matmul(o, l, r, start=(start and i == 0), stop=(stop and i == n - 1))
            ins.then_inc(sem, 1)
        self.q[e].append(run)
        self._commit((sem, self.cnt[e]), reads, writes, pwrites)

    def dma(self, e, out, in_, reads=(), writes=(), pwrites=()):
        toks = self._deps(e, reads, writes, pwrites)
        if e == "pool":
            i = self.dnext_sw
            self.dnext_sw = (self.dnext_sw + 1) % self.n_sw
        else:
            i = self.n_sw + self.dnext
            self.dnext = (self.dnext + 1) % (len(self.dsem) - self.n_sw)
        sem = self.dsem[i]
        k = self._key(sem)
        if self.dcnt[i] > toks.get(k, 0):
            toks[k] = self.dcnt[i]
        self._wait(e, toks)
        self.dcnt[i] += 16
        self.q[e].append(lambda eng, out=out, in_=in_, sem=sem:
                         eng.dma_start(out=out, in_=in_).then_inc(sem, 16))
        self._commit((sem, self.dcnt[i]), reads, writes, pwrites)

    def flush(self, wait_bufs=(), wait_all=False):
        toks = {}
        for i, sem in enumerate(self.dsem):
            if self.dcnt[i] > 0 and (wait_all or i >= self.n_sw):
                toks[self._key(sem)] = self.dcnt[i]
        for b in wait_bufs:
            for k, v in b.w.items():
                if v > toks.get(k, 0):
                    toks[k] = v
        if toks:
            self._wait("sp", toks)
        q = self.q
        self.q = {e: [] for e in self.ENGS}
        self.nblocks += 1
        with self.nc.Block() as block:
            @block.tensor
            def _(eng):
                for f in q["pe"]:
                    f(eng)

            @block.scalar
            def _(eng):
                for f in q["act"]:
                    f(eng)

            @block.vector
            def _(eng):
                for f in q["dve"]:
                    f(eng)

            @block.gpsimd
            def _(eng):
                for f in q["pool"]:
                    f(eng)

            @block.sync
            def _(eng):
                for f in q["sp"]:
                    f(eng)


def emit_norm(S, T0, hx, hxb, sq, sqb, hn, hnb, gcol, ones, ps, psb, rstd, rstdb, gcolb, onesb):
    S.op("act", lambda e: e.activation(out=sq[:, :, :], in_=hx[:, :, :], func=AF.Square),
         reads=[hxb], writes=[sqb])
    S.mm([(ps[:, :], ones[:, :], sq[:, k, :]) for k in range(NCH)], reads=[sqb, onesb], writes=[psb])
    S.op("act", lambda e: e.activation(out=rstd[:, :], in_=ps[:, :], func=AF.Sqrt,
                                       scale=1.0 / D, bias=EPS),
         reads=[psb], writes=[rstdb])
    S.op("dve", lambda e: e.reciprocal(out=rstd[:, :], in_=rstd[:, :]),
         reads=[rstdb], writes=[rstdb])
    for k in range(NCH):
        S.op("dve", lambda e, k=k: e.scalar_tensor_tensor(
            out=hn[:, k, :], in0=hx[:, k, :], scalar=gcol[:, k:k + 1], in1=rstd[:, :],
            op0=ALU.mult, op1=ALU.mult), reads=[hxb, rstdb, gcolb], pwrites=[hnb])


def phase_mlp(nc, S, T, hT, hTb, w_in_bf, w_out_bf, wb, gnorm_dram):
    ntile = T // TT
    with ExitStack() as st:
        sb = lambda name, shape, dt: st.enter_context(nc.sbuf_tensor(_uid(name), shape, dt))
        win = sb("win", [128, NCH, DFF], BF16)
        wout = sb("wout", [128, NFC, D], BF16)
        hx = [sb(f"hx{i}", [128, NCH, TT], F32) for i in range(2)]
        hn = sb("hn", [128, NCH, TT], BF16)
        u = sb("u", [128, NFC, TT], BF16)
        r = [sb(f"r{i}", [128, TT], BF16) for i in range(2)]
        rstd = sb("rstd", [128, TT], F32)
        gcol = sb("gcol", [128, NCH], F32)
        ones = sb("ones", [128, 128], BF16)
        ps = [st.enter_context(nc.psum_tensor(_uid(f"ps{i}"), [128, TT], F32)) for i in range(8)]
        B = lambda n: Buf(n)
        winb, woutb, hnb, ub, rstdb, gcolb, onesb = (B("win"), B("wout"), B("hn"), B("u"),
                                                     B("rstd"), B("gcol"), B("ones"))
        hxb = [B("hx0"), B("hx1")]
        rb = [B("r0"), B("r1")]
        psb = [B(f"ps{i}") for i in range(8)]
        ucb = [B(f"u{c}") for c in range(NFC)]

        S.op(PMS, lambda e: e.memset(ones[:, :], 1.0), writes=[onesb])
        S.dma("sp", gcol[:, :], gnorm_dram, writes=[gcolb])
        def load(t):
            S.dma("sp", hx[t % 2][:, :, :],
                  hT[:, t * TT:(t + 1) * TT].rearrange("(k p) n -> p k n", p=128),
                  reads=[hTb[t]], writes=[hxb[t % 2]])

        load(0)
        for k in range(NCH):
            S.dma("sp", win[:, k, :], w_in_bf[k * 128:(k + 1) * 128, :], reads=[wb[0][k]], pwrites=[winb])
        for c4 in range(0, NFC, 4):
            S.dma("sp", wout[:, c4:c4 + 4, :],
                  w_out_bf[c4 * 128:(c4 + 4) * 128, :].rearrange("(c p) n -> p c n", p=128),
                  reads=[wb[1][c4 // 4]], pwrites=[woutb])

        for t in range(ntile):
            if t + 1 < ntile:
                load(t + 1)
            x = hx[t % 2]
            xb = hxb[t % 2]
            emit_norm(S, t, x, xb, u[:, 0:NCH, :], ub, hn, hnb, gcol, ones, ps[7], psb[7],
                      rstd, rstdb, gcolb, onesb)
            for c in range(NFC):
                p = ps[c % 4]
                pb = psb[c % 4]
                S.mm([(p[:, :], win[:, k, c * 128:(c + 1) * 128], hn[:, k, :]) for k in range(NCH)],
                     reads=[hnb, winb], writes=[pb])
                rr, rrb = r[c % 2], rb[c % 2]
                S.op("act", lambda e, p=p, rr=rr: e.activation(out=rr[:, :], in_=p[:, :], func=AF.Relu),
                     reads=[pb], writes=[rrb])
                S.op("dve", lambda e, p=p, rr=rr, c=c: e.scalar_tensor_tensor(
                    out=u[:, c, :], in0=p[:, :], scalar=0.0, in1=rr[:, :],
                    op0=ALU.max, op1=ALU.mult), reads=[pb, rrb], writes=[ucb[c]])
            for o in range(NCH):
                p = ps[4 + o % 3]
                pb = psb[4 + o % 3]
                S.mm([(p[:, :], wout[:, c, o * 128:(o + 1) * 128], u[:, c, :]) for c in range(NFC)],
                     reads=ucb + [woutb], writes=[pb])
                S.op("dve", lambda e, p=p, x=x, o=o: e.tensor_tensor(
                    out=x[:, o, :], in0=p[:, :], in1=x[:, o, :], op=ALU.add),
                    reads=[pb, xb], pwrites=[xb])
            S._commit((S.sem["pe"], S.cnt["pe"]), ucb + [ub], [])
            S.dma("sp", hT[:, t * TT:(t + 1) * TT].rearrange("(k p) n -> p k n", p=128),
                  x[:, :, :], reads=[xb], writes=[hTb[t]])
        S.flush()


PAD = 1024
GW = 2266
GC = 1069
NEAR_LO, NEAR_HI = -686, 1070


def phase_qkv_ab(nc, S, T, hT, hTb, w_bf, wb, g_dram, QT, QTb, KT, KTb, V, Vb, w_f32=None, h_in=None, h_inb=None):
    ntile = T // TT
    with ExitStack() as st:
        sb = lambda name, shape, dt: st.enter_context(nc.sbuf_tensor(_uid(name), shape, dt))
        w = sb("wqkv", [128, NCH, 3072], BF16)
        hx = [sb(f"hx{i}", [128, NCH, TT], F32) for i in range(2)]
        sq = [sb(f"sq{i}", [128, NCH, TT], BF16) for i in range(2)]
        hn = [sb(f"hn{i}", [128, NCH, TT], BF16) for i in range(2)]
        rstd = sb("rstd", [128, TT], F32)
        gcol = sb("gcol", [128, NCH], F32)
        ones = sb("ones", [128, 128], BF16)
        qk = [sb(f"qk{i}", [128, 16, TT], BF16) for i in range(2)]
        vs = [sb(f"vs{i}", [128, 4, D], BF16) for i in range(2)]
        ps = [st.enter_context(nc.psum_tensor(_uid(f"ps{i}"), [128, TT], F32)) for i in range(8)]
        B = Buf
        wbuf, sqb, hnb, rstdb, gcolb, onesb = B("w"), B("sq"), B("hn"), B("rstd"), B("gcol"), B("ones")
        hxb = [B("hx0"), B("hx1")]
        sqb = [B("sq0"), B("sq1")]
        hnb = [B("hn0"), B("hn1")]
        qkb = [B("qk0"), B("qk1")]
        vsb = [B("vs0"), B("vs1")]
        psb = [B(f"ps{i}") for i in range(8)]
        S.op(PMS, lambda e: e.memset(ones[:, :], 1.0), writes=[onesb])
        S.dma("sp", gcol[:, :], g_dram, writes=[gcolb])
        def load(t):
            S.dma("sp", hx[t % 2][:, :, :],
                  (hT if h_in is None else h_in)[:, t * TT:(t + 1) * TT].rearrange("(k p) n -> p k n", p=128),
                  reads=[(hTb if h_in is None else h_inb)[t]], writes=[hxb[t % 2]])
        load(0)
        if w_f32 is None:
            for k in range(NCH):
                S.dma("sp", w[:, k, :], w_bf[k * 128:(k + 1) * 128, :], reads=[wb[k]], pwrites=[wbuf])
        else:
            wst = [sb(f"wst{i}", [128, 3072], F32) for i in range(2)]
            wstb = [B("wst0"), B("wst1")]
            for k in range(NCH):
                S.dma("sp", wst[k % 2][:, :], w_f32[k * 128:(k + 1) * 128, :], writes=[wstb[k % 2]])
                S.op("act", lambda e, k=k: e.activation(out=w[:, k, :], in_=wst[k % 2][:, :], func=AF.Copy),
                     reads=[wstb[k % 2]], pwrites=[wbuf])

        def norm(t):
            emit_norm(S, t, hx[t % 2], hxb[t % 2], sq[t % 2], sqb[t % 2], hn[t % 2], hnb[t % 2], gcol, ones,
                      ps[7], psb[7], rstd, rstdb, gcolb, onesb)
        for t in range(ntile):
            if t + 1 < ntile:
                load(t + 1)
            x, xb = hx[t % 2], hxb[t % 2]
            if t == 0:
                norm(0)
            hn_, hn_b = hn[t % 2], hnb[t % 2]
            q_, q_b = qk[t % 2], qkb[t % 2]
            for c in range(16):
                if c == 6 and t + 1 < ntile:
                    norm(t + 1)
                p, pb = ps[c % 4], psb[c % 4]
                S.mm([(p[:, :], w[:, k, c * 128:(c + 1) * 128], hn_[:, k, :]) for k in range(NCH)],
                     reads=[hn_b, wbuf], writes=[pb])
                if c < 8:
                    S.op("act", lambda e, p=p, q_=q_, c=c: e.activation(
                        out=q_[:, c, :], in_=p[:, :], func=AF.Copy, scale=0.125),
                        reads=[pb], pwrites=[q_b])
                else:
                    S.op("dve", lambda e, p=p, q_=q_, c=c: e.tensor_copy(out=q_[:, c, :], in_=p[:, :]),
                         reads=[pb], pwrites=[q_b])
            S.dma("sp", QT[:, t * TT:(t + 1) * TT].rearrange("(c p) n -> p c n", p=128),
                  q_[:, 0:8, :], reads=[q_b], pwrites=[QTb])
            S.dma("sp", KT[:, PAD + t * TT:PAD + (t + 1) * TT].rearrange("(c p) n -> p c n", p=128),
                  q_[:, 8:16, :], reads=[q_b], pwrites=[KTb])
            v_, v_b = vs[t % 2], vsb[t % 2]
            for tb in range(4):
                for half in range(2):
                    i = tb * 2 + half
                    p, pb = ps[4 + i % 3], psb[4 + i % 3]
                    S.mm([(p[:, :], hn_[:, k, tb * 128:(tb + 1) * 128],
                           w[:, k, 2048 + half * 512:2048 + (half + 1) * 512]) for k in range(NCH)],
                         reads=[hn_b, wbuf], writes=[pb])
                    if i % 2 == 0:
                        S.op("act", lambda e, p=p, v_=v_, tb=tb, half=half: e.activation(
                            out=v_[:, tb, half * 512:(half + 1) * 512], in_=p[:, :], func=AF.Copy),
                            reads=[pb], pwrites=[v_b])
                    else:
                        S.op("dve", lambda e, p=p, v_=v_, tb=tb, half=half: e.tensor_copy(
                            out=v_[:, tb, half * 512:(half + 1) * 512], in_=p[:, :]),
                            reads=[pb], pwrites=[v_b])
            S.dma("sp", V[PAD + t * TT:PAD + (t + 1) * TT, :].rearrange("(b p) f -> p b f", p=128),
                  v_[:, :, :], reads=[v_b], pwrites=[Vb])
            S._commit((S.sem["pe"], S.cnt["pe"]), [hn_b, sqb[t % 2]], [])
        S.flush()


def phase_wo(nc, S, T, OT, OTb, w_bf, wb, hT, hTb, h_in=None, h_inb=None):
    ntile = T // TT
    with ExitStack() as st:
        sb = lambda name, shape, dt: st.enter_context(nc.sbuf_tensor(_uid(name), shape, dt))
        w = sb("wo", [128, NCH, D], BF16)
        hx = [sb(f"hx{i}", [128, NCH, TT], F32) for i in range(2)]
        ot = [sb(f"ot{i}", [128, NCH, TT], BF16) for i in range(2)]
        ps = [st.enter_context(nc.psum_tensor(_uid(f"ps{i}"), [128, TT], F32)) for i in range(8)]
        B = Buf
        wbuf = B("w")
        hxb = [B("hx0"), B("hx1")]
        otb = [B("ot0"), B("ot1")]
        psb = [B(f"ps{i}") for i in range(8)]
        def load(t):
            S.dma("sp", hx[t % 2][:, :, :],
                  (hT if h_in is None else h_in)[:, t * TT:(t + 1) * TT].rearrange("(k p) n -> p k n", p=128),
                  reads=[(hTb if h_in is None else h_inb)[t]], writes=[hxb[t % 2]])
            S.dma("sp", ot[t % 2][:, :, :],
                  OT[:, t * TT:(t + 1) * TT].rearrange("(k p) n -> p k n", p=128),
                  reads=[OTb], writes=[otb[t % 2]])
        load(0)
        for k in range(NCH):
            S.dma("sp", w[:, k, :], w_bf[k * 128:(k + 1) * 128, :], reads=[wb[k]], pwrites=[wbuf])

        for t in range(ntile):
            if t + 1 < ntile:
                load(t + 1)
            x, xb = hx[t % 2], hxb[t % 2]
            o_, o_b = ot[t % 2], otb[t % 2]
            for o in range(NCH):
                p, pb = ps[o % 8], psb[o % 8]
                S.mm([(p[:, :], w[:, k, o * 128:(o + 1) * 128], o_[:, k, :]) for k in range(NCH)],
                     reads=[o_b, wbuf], writes=[pb])
                S.op("dve", lambda e, p=p, x=x, o=o: e.tensor_tensor(
                    out=x[:, o, :], in0=p[:, :], in1=x[:, o, :], op=ALU.add),
                    reads=[pb, xb], pwrites=[xb])
            S.dma("sp", hT[:, t * TT:(t + 1) * TT].rearrange("(k p) n -> p k n", p=128),
                  x[:, :, :], reads=[xb], writes=[hTb[t]])
        S.flush()


def phase_final(nc, S, T, hT, hTb, g_dram, outT, outb):
    ntile = T // TT
    with ExitStack() as st:
        sb = lambda name, shape, dt: st.enter_context(nc.sbuf_tensor(_uid(name), shape, dt))
        hx = [sb(f"hx{i}", [128, NCH, TT], F32) for i in range(2)]
        ho = [sb(f"ho{i}", [128, NCH, TT], F32) for i in range(2)]
        sq = sb("sq", [128, NCH, TT], BF16)
        rstd = sb("rstd", [128, TT], F32)
        gcol = sb("gcol", [128, NCH], F32)
        ones = sb("ones", [128, 128], BF16)
        ps = [st.enter_context(nc.psum_tensor(_uid(f"ps{i}"), [128, TT], F32)) for i in range(2)]
        B = Buf
        sqb, rstdb, gcolb, onesb = B("sq"), B("rstd"), B("gcol"), B("ones")
        hxb = [B("hx0"), B("hx1")]
        hob = [B("ho0"), B("ho1")]
        psb = [B("ps0"), B("ps1")]
        S.op(PMS, lambda e: e.memset(ones[:, :], 1.0), writes=[onesb])
        S.dma("sp", gcol[:, :], g_dram, writes=[gcolb])

        def load(t):
            S.dma("sp", hx[t % 2][:, :, :],
                  hT[:, t * TT:(t + 1) * TT].rearrange("(k p) n -> p k n", p=128),
                  reads=[hTb[t]], writes=[hxb[t % 2]])
        load(0)
        for t in range(ntile):
            if t + 1 < ntile:
                load(t + 1)
            emit_norm(S, t, hx[t % 2], hxb[t % 2], sq, sqb, ho[t % 2], hob[t % 2], gcol, ones,
                      ps[t % 2], psb[t % 2], rstd, rstdb, gcolb, onesb)
            S.dma("sp", outT[:, t * TT:(t + 1) * TT].rearrange("(k p) n -> p k n", p=128),
                  ho[t % 2][:, :, :], reads=[hob[t % 2]], pwrites=[outb])
        S.flush(wait_bufs=[outb], wait_all=True)


def phase_attn_b(nc, S, T, QT, QTb, KT, KTb, V, Vb, OT, OTb, gtab, bconst_d, lam_d, gsub_d, lam_init):
    NKB, NQT = T // 128, T // TT
    with ExitStack() as st:
        sb = lambda name, shape, dt: st.enter_context(nc.sbuf_tensor(_uid(name), shape, dt))
        qt_ = [sb(f"qt{i}", [128, T], BF16) for i in range(2)]
        kt_ = [sb(f"kt{i}", [128, T], BF16) for i in range(2)]
        vv = [sb(f"vv{i}", [128, NKB, 128], BF16) for i in range(2)]
        gt = [sb(f"gt{i}", [128, 2, GW], F32) for i in range(2)]
        sbi = [sb(f"sbi{i}", [128, TT], F32) for i in range(4)]
        pt = [sb(f"pt{i}", [128, TT], BF16) for i in range(8)]
        ones = sb("ones", [128, 128], BF16)
        bconst = sb("bconst", [128, 32], F32)
        lam = sb("lam", [128, 256], F32)
        ltmp = sb("ltmp", [128, 64], F32)
        lsc = sb("lsc", [128, 8], F32)
        gsub = sb("gsub", [128, 2], F32)
        ef = [sb(f"ef{i}", [128, TT], F32) for i in range(5)]
        osq = sb("osq", [128, TT], BF16)
        ost = [sb(f"ost{i}", [128, TT], BF16) for i in range(2)]
        zs = [sb(f"zs{i}", [128, TT], F32) for i in range(2)]
        pvs = [sb(f"pvs{i}", [128, TT], F32) for i in range(2)]
        zsb = [Buf("zs0"), Buf("zs1")]
        pvsb = [Buf("pvs0"), Buf("pvs1")]
        deferred = []
        ps = [st.enter_context(nc.psum_tensor(_uid(f"ps{i}"), [128, TT], F32)) for i in range(8)]
        B = Buf
        qtb = [B("qt0"), B("qt1")]
        ktb = [B("kt0"), B("kt1")]
        vvb = [B("vv0"), B("vv1")]
        gtb = [B("gt0"), B("gt1")]
        sbib = [B(f"sbi{i}") for i in range(4)]
        ptb = [B(f"pt{i}") for i in range(8)]
        efb = [B(f"ef{i}") for i in range(5)]
        onesb, bcb, lamb, ltb, lscb, gsubb, osqb = (B("ones"), B("bc"), B("lam"), B("lt"), B("lsc"),
                                                    B("gsub"), B("osq"))
        ostb = [B("ost0"), B("ost1")]
        psb = [B(f"ps{i}") for i in range(8)]
        psS = [[ps[0], ps[1]], [ps[2], ps[3]]]
        psSb = [[psb[0], psb[1]], [psb[2], psb[3]]]
        PV, PVb = [ps[4], ps[5]], [psb[4], psb[5]]
        Z, Zb = [ps[6], ps[7]], [psb[6], psb[7]]

        S.op(PMS, lambda e: e.memset(ones[:, :], 1.0), writes=[onesb])
        S.dma("sp", bconst[:, :], bconst_d, writes=[bcb])
        S.dma("sp", lam[:, :], lam_d, writes=[lamb])
        S.dma("sp", gsub[:, 0:1], gsub_d, writes=[gsubb])
        S.op("dve", lambda e: e.scalar_tensor_tensor(out=ltmp[:, :], in0=lam[:, 0:64], scalar=1.0,
                                                     in1=lam[:, 64:128], op0=ALU.mult, op1=ALU.mult,
                                                     accum_out=lsc[:, 0:1]),
             reads=[lamb], writes=[ltb, lscb])
        S.op("dve", lambda e: e.scalar_tensor_tensor(out=ltmp[:, :], in0=lam[:, 128:192], scalar=1.0,
                                                     in1=lam[:, 192:256], op0=ALU.mult, op1=ALU.mult,
                                                     accum_out=lsc[:, 1:2]),
             reads=[lamb, lscb], writes=[ltb, lscb])
        S.op("act", lambda e: e.activation(out=lsc[:, 2:4], in_=lsc[:, 0:2], func=AF.Exp),
             reads=[lscb], writes=[lscb])
        S.op("dve", lambda e: e.tensor_tensor(out=lsc[:, 4:5], in0=lsc[:, 3:4], in1=lsc[:, 2:3],
                                              op=ALU.subtract), reads=[lscb], writes=[lscb])
        S.op("dve", lambda e: e.tensor_scalar(out=lsc[:, 5:6], in0=lsc[:, 4:5], scalar1=-float(lam_init),
                                              scalar2=None, op0=ALU.add), reads=[lscb], writes=[lscb])
        S.op("dve", lambda e: e.tensor_scalar(out=gsub[:, 1:2], in0=gsub[:, 0:1],
                                              scalar1=float(1.0 - lam_init), scalar2=None, op0=ALU.mult),
             reads=[gsubb], writes=[gsubb])
        neglam = lsc[:, 5:6]

        def load(h):
            i = h % 2
            S.dma("sp", qt_[i][:, :], QT[h * 128:(h + 1) * 128, :], reads=[QTb], writes=[qtb[i]])
            S.dma("sp", kt_[i][:, :], KT[h * 128:(h + 1) * 128, PAD:PAD + T], reads=[KTb], writes=[ktb[i]])
            S.dma("sp", vv[i][:, :, :],
                  V[PAD:PAD + T, h * 128:(h + 1) * 128].rearrange("(b p) f -> p b f", p=128),
                  reads=[Vb], writes=[vvb[i]])
            for j in range(2):
                S.dma("sp", gt[i][:, j, :], gtab[2 * h + j, :, :], pwrites=[gtb[i]],
                      reads=[])
        load(0)
        npt = 0
        nsb = 0
        for h in range(8):
            if h + 1 < 8:
                load(h + 1)
            i = h % 2
            q_, k_, v_, g_ = qt_[i], kt_[i], vv[i], gt[i]
            for qt in range(NQT):
                q0 = qt * TT

                def issue_S(kb):
                    for j in range(2):
                        S.mm([(psS[j][kb % 2][:, :], k_[j * 64:(j + 1) * 64, kb * 128:(kb + 1) * 128],
                               q_[j * 64:(j + 1) * 64, q0:q0 + TT])],
                             reads=[qtb[i], ktb[i]], writes=[psSb[j][kb % 2]])
                issue_S(0)
                for kb in range(NKB):
                    if kb + 1 < NKB:
                        issue_S(kb + 1)
                    d = kb * 128 - q0
                    pts = []
                    for j in range(2):
                        sp_, spb = psS[j][kb % 2], psSb[j][kb % 2]
                        p_, p_b = pt[npt % 8], ptb[npt % 8]
                        npt += 1
                        if NEAR_LO < d < NEAR_HI:
                            m0 = GC - d
                            s_, s_b = sbi[nsb % 4], sbib[nsb % 4]
                            nsb += 1
                            S.op("dve", lambda e, s_=s_, sp_=sp_, g_=g_, j=j, m0=m0: e.tensor_tensor(
                                out=s_[:, :], in0=sp_[:, :], in1=g_[:, j, m0:m0 + TT], op=ALU.add),
                                reads=[spb, gtb[i]], writes=[s_b])
                            S.op("act", lambda e, p_=p_, s_=s_: e.activation(
                                out=p_[:, :], in_=s_[:, :], func=AF.Exp), reads=[s_b], writes=[p_b])
                        else:
                            col = (16 if d > 0 else 0) + 2 * h + j
                            S.op("act", lambda e, p_=p_, sp_=sp_, col=col: e.activation(
                                out=p_[:, :], in_=sp_[:, :], func=AF.Exp, bias=bconst[:, col:col + 1]),
                                reads=[spb, bcb], writes=[p_b])
                        pts.append((p_, p_b))
                    for j in range(2):
                        p_, p_b = pts[j]
                        S.mm([(PV[j][:, :], v_[:, kb, :], p_[:, :])], reads=[p_b, vvb[i]],
                             writes=[PVb[j]] if kb == 0 else [], pwrites=[] if kb == 0 else [PVb[j]],
                             start=(kb == 0), stop=(kb == NKB - 1))
                    for j in range(2):
                        p_, p_b = pts[j]
                        S.mm([(Z[j][:, :], ones[:, :], p_[:, :])], reads=[p_b, onesb],
                             writes=[Zb[j]] if kb == 0 else [], pwrites=[] if kb == 0 else [Zb[j]],
                             start=(kb == 0), stop=(kb == NKB - 1))
                    if deferred and kb >= 1:
                        deferred.pop(0)(kb)
                while deferred:
                    deferred.pop(0)(NKB - 1)
                S.op("act", lambda e: e.activation(out=zs[0][:, :], in_=Z[0][:, :], func=AF.Copy),
                     reads=[Zb[0]], writes=[zsb[0]])
                S.op("dve", lambda e: e.tensor_copy(out=pvs[0][:, :], in_=PV[0][:, :]),
                     reads=[PVb[0]], writes=[pvsb[0]])
                S.op("act", lambda e: e.activation(out=zs[1][:, :], in_=Z[1][:, :], func=AF.Copy),
                     reads=[Zb[1]], writes=[zsb[1]])
                S.op("dve", lambda e: e.tensor_copy(out=pvs[1][:, :], in_=PV[1][:, :]),
                     reads=[PVb[1]], writes=[pvsb[1]])
                r0, r1, o0, t1, o = ef
                os_, os_b = ost[qt % 2], ostb[qt % 2]
                dst = OT[h * 128:(h + 1) * 128, q0:q0 + TT]

                def mk(os_=os_, os_b=os_b, dst=dst):
                    ops = []
                    ops.append(lambda kb: S.op("dve", lambda e: e.reciprocal(out=r0[:, :], in_=zs[0][:, :]),
                                               reads=[zsb[0]], writes=[efb[0]]))
                    ops.append(lambda kb: S.op("dve", lambda e: e.reciprocal(out=r1[:, :], in_=zs[1][:, :]),
                                               reads=[zsb[1]], writes=[efb[1]]))
                    ops.append(lambda kb: S.op("dve", lambda e: e.tensor_tensor(
                        out=o0[:, :], in0=pvs[0][:, :], in1=r0[:, :], op=ALU.mult),
                        reads=[pvsb[0], efb[0]], writes=[efb[2]]))
                    ops.append(lambda kb: S.op("dve", lambda e: e.tensor_tensor(
                        out=t1[:, :], in0=pvs[1][:, :], in1=r1[:, :], op=ALU.mult),
                        reads=[pvsb[1], efb[1]], writes=[efb[3]]))
                    ops.append(lambda kb: S.op("dve", lambda e: e.scalar_tensor_tensor(
                        out=o[:, :], in0=t1[:, :], scalar=neglam, in1=o0[:, :], op0=ALU.mult, op1=ALU.add),
                        reads=[efb[2], efb[3], lscb], writes=[efb[4]]))
                    ops.append(lambda kb: S.op("act", lambda e: e.activation(
                        out=osq[:, :], in_=o[:, :], func=AF.Square), reads=[efb[4]], writes=[osqb]))

                    def ones_mm(kb):
                        pm, pmb = psS[0][kb % 2], psSb[0][kb % 2]
                        S.mm([(pm[:, :], ones[:, :], osq[:, :])], reads=[osqb, onesb], writes=[pmb])
                        S.op("act", lambda e, pm=pm: e.activation(out=r0[:, :], in_=pm[:, :], func=AF.Sqrt,
                                                                  scale=1.0 / 128, bias=EPS),
                             reads=[pmb], writes=[efb[0]])
                    ops.append(ones_mm)
                    ops.append(lambda kb: S.op("dve", lambda e: e.reciprocal(out=r0[:, :], in_=r0[:, :]),
                                               reads=[efb[0]], writes=[efb[0]]))
                    ops.append(lambda kb: S.op("dve", lambda e: e.scalar_tensor_tensor(
                        out=os_[:, :], in0=o[:, :], scalar=gsub[:, 1:2], in1=r0[:, :],
                        op0=ALU.mult, op1=ALU.mult), reads=[efb[4], efb[0], gsubb], writes=[os_b]))
                    ops.append(lambda kb: S.dma("sp", dst, os_[:, :], reads=[os_b], pwrites=[OTb]))
                    return ops
                deferred = mk()
        while deferred:
            deferred.pop(0)(NKB - 1)
        S.flush()


def phase_qkv_c(nc, S, T, hT, hTb, w_bf, wb, g_dram, QT, QTb, KT, KTb, V, Vb,
                ctab_d, stab_d, pmat_d, qkg_d, h_in=None, h_inb=None):
    ntile = T // TT
    with ExitStack() as st:
        sb = lambda name, shape, dt: st.enter_context(nc.sbuf_tensor(_uid(name), shape, dt))
        w = sb("wqkv", [128, NCH, 1536], BF16)
        hx = [sb(f"hx{i}", [128, NCH, TT], F32) for i in range(2)]
        sq = [sb(f"sq{i}", [128, NCH, TT], BF16) for i in range(2)]
        hn = [sb(f"hn{i}", [128, NCH, TT], BF16) for i in range(2)]
        rstd = sb("rstd", [128, TT], F32)
        gcol = sb("gcol", [128, NCH], F32)
        ones = sb("ones", [128, 128], BF16)
        oblk = sb("oblk", [128, 128], BF16)
        pm32 = sb("pm32", [128, 128], F32)
        pmat = sb("pmat", [128, 128], BF16)
        qkg = sb("qkg", [128, 2], F32)
        ctab = sb("ctab", [128, T], F32)
        stab = sb("stab", [128, T], F32)
        sq2 = [sb(f"sq2{i}", [128, TT], BF16) for i in range(2)]
        qg = [sb(f"qg{i}", [128, TT], BF16) for i in range(2)]
        qgf = [sb(f"qgf{i}", [128, TT], F32) for i in range(2)]
        qgfb = [Buf("qgf0"), Buf("qgf1")]
        rs = [sb(f"rs{i}", [128, TT], F32) for i in range(2)]
        t1 = [sb(f"t1{i}", [128, TT], F32) for i in range(2)]
        t2 = [sb(f"t2{i}", [128, TT], F32) for i in range(2)]
        qk = [sb(f"qk{i}", [128, 10, TT], BF16) for i in range(2)]
        vs = [sb(f"vs{i}", [128, 4, 256], BF16) for i in range(2)]
        ps = [st.enter_context(nc.psum_tensor(_uid(f"ps{i}"), [128, TT], F32)) for i in range(8)]
        B = Buf
        wbuf, sqb, hnb, rstdb, gcolb, onesb, oblkb, pm32b, pmatb, qkgb, ctb, stb = (
            B("w"), B("sq"), B("hn"), B("rstd"), B("gcol"), B("ones"), B("oblk"), B("pm32"), B("pmat"),
            B("qkg"), B("ct"), B("st"))
        hxb = [B("hx0"), B("hx1")]
        sqb = [B("sq0"), B("sq1")]
        hnb = [B("hn0"), B("hn1")]
        sq2b = [B("a"), B("b")]
        qgb = [B("a"), B("b")]
        rsb = [B("a"), B("b")]
        t1b = [B("a"), B("b")]
        t2b = [B("a"), B("b")]
        qkb = [B("qk0"), B("qk1")]
        vsb = [B("vs0"), B("vs1")]
        psb = [B(f"ps{i}") for i in range(8)]
        S.op(PMS, lambda e: e.memset(ones[:, :], 1.0), writes=[onesb])
        S.op(PMS, lambda e: e.memset(oblk[:, :], 0.0), writes=[oblkb])
        if _os.environ.get("K_C2", "0") != "1":
            S.op(PMS, lambda e: e.memset(oblk[0:64, 0:64], 1.0), writes=[oblkb])
            S.op(PMS, lambda e: e.memset(oblk[64:128, 64:128], 1.0), writes=[oblkb])
        S.dma("sp", gcol[:, :], g_dram, writes=[gcolb])
        if _os.environ.get("K_C3", "0") == "2":
            S.dma("sp", pm32[:, :], pmat_d, writes=[pm32b])
            S.op("dve", lambda e: e.memset(pmat[:, :], 1.0), writes=[pmatb])
        elif _os.environ.get("K_C3", "0") != "1":
            S.dma("sp", pm32[:, :], pmat_d, writes=[pm32b])
            S.op("act", lambda e: e.activation(out=pmat[:, :], in_=pm32[:, :], func=AF.Copy),
                 reads=[pm32b], writes=[pmatb])
        else:
            S.op("dve", lambda e: e.memset(pmat[:, :], 1.0), writes=[pmatb])
        if _os.environ.get("K_C4", "0") != "1":
            S.dma("sp", qkg[:, :], qkg_d, writes=[qkgb])
        if _os.environ.get("K_C5", "0") != "1":
            S.dma("sp", ctab[:, :], ctab_d, writes=[ctb])
            S.dma("sp", stab[:, :], stab_d, writes=[stb])
        def load(t):
            S.dma("sp", hx[t % 2][:, :, :],
                  (hT if h_in is None else h_in)[:, t * TT:(t + 1) * TT].rearrange("(k p) n -> p k n", p=128),
                  reads=[(hTb if h_in is None else h_inb)[t]], writes=[hxb[t % 2]])
        load(0)
        for k in range(NCH):
            S.dma("sp", w[:, k, :], w_bf[k * 128:(k + 1) * 128, :], reads=[wb[k]], pwrites=[wbuf])

        n = 0
        def norm(t):
            emit_norm(S, t, hx[t % 2], hxb[t % 2], sq[t % 2], sqb[t % 2], hn[t % 2], hnb[t % 2], gcol, ones,
                      ps[7], psb[7], rstd, rstdb, gcolb, onesb)
        for t in range(ntile):
            if t + 1 < ntile:
                load(t + 1)
            x, xb = hx[t % 2], hxb[t % 2]
            if t == 0:
                norm(0)
            hn_, hn_b = hn[t % 2], hnb[t % 2]
            q_, q_b = qk[t % 2], qkb[t % 2]
            t0 = t * TT
            for c in range(10):
                if c == 3 and t + 1 < ntile:
                    norm(t + 1)
                isq = c < 8
                pA, pAb = ps[(2 * c) % 4], psb[(2 * c) % 4]
                pB, pBb = ps[(2 * c) % 4 + 1], psb[(2 * c) % 4 + 1]
                pC, pCb = ps[4 + c % 2], psb[4 + c % 2]
                i = n % 2
                n += 1
                gq = qkg[:, 0:1] if isq else qkg[:, 1:2]
                _cn = int(_os.environ.get('K_CN', '99'))
                if _cn > 0:
                    S.mm([(pA[:, :], w[:, k, c * 128:(c + 1) * 128], hn_[:, k, :]) for k in range(NCH)],
                         reads=[hn_b, wbuf], writes=[pAb])
                if _cn > 1:
                    S.op("act", lambda e, pA=pA, i=i: e.activation(out=sq2[i][:, :], in_=pA[:, :], func=AF.Square),
                         reads=[pAb], writes=[sq2b[i]])
                if _cn > 2:
                    S.op("act", lambda e, pA=pA, i=i, gq=gq: e.activation(
                        out=qgf[i][:, :], in_=pA[:, :], func=AF.Copy, scale=gq),
                        reads=[pAb, qkgb], writes=[qgfb[i]])
                    S.op("pool", lambda e, i=i: e.tensor_copy(out=qg[i][:, :], in_=qgf[i][:, :]),
                         reads=[qgfb[i]], writes=[qgb[i]])
                if _cn > 3:
                    S.mm([(pB[:, :], oblk[:, :], sq2[i][:, :])], reads=[sq2b[i], oblkb], writes=[pBb])
                if _cn > 4:
                    S.mm([(pC[:, :], pmat[:, :], qg[i][:, :])], reads=[qgb[i], pmatb], writes=[pCb])
                if _cn > 5:
                    S.op("act", lambda e, pB=pB, i=i: e.activation(out=rs[i][:, :], in_=pB[:, :], func=AF.Sqrt,
                                                                  scale=1.0 / 64, bias=EPS),
                         reads=[pBb], writes=[rsb[i]])
                if _cn > 6:
                    S.op("dve", lambda e, i=i: e.reciprocal(out=rs[i][:, :], in_=rs[i][:, :]),
                         reads=[rsb[i]], writes=[rsb[i]])
                if _cn > 7:
                    S.op("pool", lambda e, i=i, t0=t0: e.tensor_tensor(
                        out=t1[i][:, :], in0=qgf[i][:, :], in1=ctab[:, t0:t0 + TT], op=ALU.mult),
                        reads=[qgfb[i], ctb], writes=[t1b[i]])
                if _cn > 8:
                    S.op("dve", lambda e, pC=pC, i=i, t0=t0: e.tensor_tensor(
                        out=t2[i][:, :], in0=pC[:, :], in1=stab[:, t0:t0 + TT], op=ALU.mult),
                        reads=[pCb, stb], writes=[t2b[i]])
                if _cn > 9:
                    S.op(PTT, lambda e, i=i: e.tensor_tensor(out=t1[i][:, :], in0=t1[i][:, :], in1=t2[i][:, :],
                                                               op=ALU.add),
                         reads=[t1b[i], t2b[i]], writes=[t1b[i]])
                if _cn > 10:
                    S.op("dve", lambda e, i=i, q_=q_, c=c, isq=isq: e.scalar_tensor_tensor(
                        out=q_[:, c, :], in0=t1[i][:, :], scalar=(0.125 if isq else 1.0), in1=rs[i][:, :],
                        op0=ALU.mult, op1=ALU.mult), reads=[t1b[i], rsb[i]], pwrites=[q_b])
            if _os.environ.get("K_C6", "0") != "1":
                S.dma("sp", QT[:, t0:t0 + TT].rearrange("(c p) n -> p c n", p=128),
                      q_[:, 0:8, :], reads=[q_b], pwrites=[QTb])
                S.dma("sp", KT[0:256, PAD + t0:PAD + t0 + TT].rearrange("(c p) n -> p c n", p=128),
                      q_[:, 8:10, :], reads=[q_b], pwrites=[KTb])
            v_, v_b = vs[t % 2], vsb[t % 2]
            for tb in range(4 if _os.environ.get("K_C1", "0") != "1" else 0):
                p, pb = ps[6], psb[6]
                S.mm([(p[:, 0:256], hn_[:, k, tb * 128:(tb + 1) * 128], w[:, k, 1280:1536])
                      for k in range(NCH)], reads=[hn_b, wbuf], writes=[pb])
                S.op("act", lambda e, p=p, v_=v_, tb=tb: e.activation(
                    out=v_[:, tb, :], in_=p[:, 0:256], func=AF.Copy), reads=[pb], pwrites=[v_b])
            if _os.environ.get("K_C6", "0") != "1":
                S.dma("sp", V[PAD + t0:PAD + t0 + TT, 0:256].rearrange("(b p) f -> p b f", p=128),
                      v_[:, :, :], reads=[v_b], pwrites=[Vb])
            S._commit((S.sem["pe"], S.cnt["pe"]), [hn_b, sqb[t % 2]], [])
        S.flush()


def phase_attn_c(nc, S, T, QT, QTb, KT, KTb, V, Vb, OT, OTb):
    NKB, NQT = T // 128, T // TT
    with ExitStack() as st:
        sb = lambda name, shape, dt: st.enter_context(nc.sbuf_tensor(_uid(name), shape, dt))
        qt_ = [sb(f"qt{i}", [128, T], BF16) for i in range(2)]
        k2 = [sb(f"k2{i}", [128, T], BF16) for i in range(2)]
        ve = [sb(f"ve{i}", [128, NKB, 128], BF16) for i in range(2)]
        vo = [sb(f"vo{i}", [128, NKB, 128], BF16) for i in range(2)]
        pt = [sb(f"pt{i}", [128, 2, TT], BF16) for i in range(4)]
        zz = [sb(f"zz{i}", [128, TT], F32) for i in range(2)]
        zzs = [sb(f"zzs{i}", [128, TT], F32) for i in range(2)]
        rzs = [sb(f"rzs{i}", [128, TT], F32) for i in range(2)]
        ost = [sb(f"ost{i}", [128, TT], BF16) for i in range(2)]
        sS = [st.enter_context(nc.psum_tensor(_uid(f"sS{i}"), [128, 2, TT], F32)) for i in range(2)]
        E = [st.enter_context(nc.psum_tensor(_uid(f"E{i}"), [128, TT], F32)) for i in range(2)]
        O = [st.enter_context(nc.psum_tensor(_uid(f"O{i}"), [128, TT], F32)) for i in range(2)]
        B = Buf
        qtb = [B("qt0"), B("qt1")]
        k2b = [B("k0"), B("k1")]
        veb = [B("a"), B("b")]
        vob = [B("a"), B("b")]
        ptb = [B(f"pt{i}") for i in range(4)]
        zzb = [B("a"), B("b")]
        zzsb = [B("a"), B("b")]
        rzsb = [B("a"), B("b")]
        ostb = [B("a"), B("b")]
        sSb = [B("s0"), B("s1")]
        Eb = [B("e0"), B("e1")]
        Ob = [B("o0"), B("o1")]
        for i in range(2):
            S.op(PMS, lambda e, i=i: e.memset(ve[i][:, :, 64:128], 1.0), writes=[veb[i]])
            S.op(PMS, lambda e, i=i: e.memset(vo[i][:, :, 0:64], 1.0), writes=[vob[i]])

        def load(c):
            i = c % 2
            g = c // 2
            S.dma("sp", qt_[i][:, :], QT[c * 128:(c + 1) * 128, :], reads=[QTb], writes=[qtb[i]])
            S.dma("sp", k2[i][0:64, :], KT[g * 64:(g + 1) * 64, PAD:PAD + T], reads=[KTb], writes=[k2b[i]])
            S.dma("sp", k2[i][64:128, :], KT[g * 64:(g + 1) * 64, PAD:PAD + T], reads=[KTb], pwrites=[k2b[i]])
            vsrc = V[PAD:PAD + T, g * 64:(g + 1) * 64].rearrange("(b p) f -> p b f", p=128)
            S.dma("sp", ve[i][:, :, 0:64], vsrc, reads=[Vb, veb[i]], pwrites=[veb[i]])
            S.dma("sp", vo[i][:, :, 64:128], vsrc, reads=[Vb, vob[i]], pwrites=[vob[i]])
        load(0)
        npt = 0
        nq = 0
        for c in range(8):
            if c + 1 < 8:
                load(c + 1)
            i = c % 2
            q_, k_ = qt_[i], k2[i]
            for qt in range(NQT):
                q0 = qt * TT
                a = nq % 2
                nq += 1

                def issue_S(kb):
                    x = kb % 2
                    for j in range(2):
                        S.mm([(sS[x][:, j, :], k_[j * 64:(j + 1) * 64, kb * 128:(kb + 1) * 128],
                               q_[j * 64:(j + 1) * 64, q0:q0 + TT])],
                             reads=[qtb[i], k2b[i]],
                             writes=[sSb[x]] if j == 0 else [], pwrites=[] if j == 0 else [sSb[x]])
                issue_S(0)
                for kb in range(NKB):
                    if kb + 1 < NKB:
                        issue_S(kb + 1)
                    x = kb % 2
                    n = npt % 4
                    npt += 1
                    S.op("act", lambda e, n=n, x=x: e.activation(
                        out=pt[n][:, :, :], in_=sS[x][:, :, :], func=AF.Exp), reads=[sSb[x]], writes=[ptb[n]])
                    first, last = (kb == 0), (kb == NKB - 1)
                    S.mm([(E[a][:, :], ve[i][:, kb, :], pt[n][:, 0, :])], reads=[ptb[n], veb[i]],
                         writes=[Eb[a]] if first else [], pwrites=[] if first else [Eb[a]],
                         start=first, stop=last)
                    S.mm([(O[a][:, :], vo[i][:, kb, :], pt[n][:, 1, :])], reads=[ptb[n], vob[i]],
                         writes=[Ob[a]] if first else [], pwrites=[] if first else [Ob[a]],
                         start=first, stop=last)
                S.op("dve", lambda e, a=a: e.tensor_copy(out=zz[a][0:64, :], in_=O[a][0:64, :]),
                     reads=[Ob[a]], writes=[zzb[a]])
                S.op("dve", lambda e, a=a: e.tensor_copy(out=zz[a][64:128, :], in_=E[a][64:128, :]),
                     reads=[Eb[a]], pwrites=[zzb[a]])
                S.dma("sp", zzs[a][0:64, :], zz[a][64:128, :], reads=[zzb[a]], writes=[zzsb[a]])
                S.dma("sp", zzs[a][64:128, :], zz[a][0:64, :], reads=[zzb[a]], pwrites=[zzsb[a]])
                S.op("dve", lambda e, a=a: e.reciprocal(out=rzs[a][:, :], in_=zzs[a][:, :]),
                     reads=[zzsb[a]], writes=[rzsb[a]])
                S.op("dve", lambda e, a=a: e.tensor_tensor(out=ost[a][0:64, :], in0=E[a][0:64, :],
                                                          in1=rzs[a][0:64, :], op=ALU.mult),
                     reads=[Eb[a], rzsb[a]], writes=[ostb[a]])
                S.op("dve", lambda e, a=a: e.tensor_tensor(out=ost[a][64:128, :], in0=O[a][64:128, :],
                                                          in1=rzs[a][64:128, :], op=ALU.mult),
                     reads=[Ob[a], rzsb[a]], pwrites=[ostb[a]])
                S.dma("sp", OT[c * 128:(c + 1) * 128, q0:q0 + TT], ost[a][:, :], reads=[ostb[a]], pwrites=[OTb])
        S.flush()


A_DIL = [1] * 6 + [4] * 5 + [16] * 5
A_GRP = [0] * 6 + [1] * 5 + [2] * 5
A_NH = [6, 5, 5]


def phase_attn_a(nc, S, T, QT, QTb, KT, KTb, V, Vb, PVun, PVb_d, Zbc, Zb_d, bm_d):
    with ExitStack() as st:
        sb = lambda name, shape, dt: st.enter_context(nc.sbuf_tensor(_uid(name), shape, dt))
        NBMAX = T // 128 + 16
        qa = [sb(f"qa{i}", [128, T], BF16) for i in range(2)]
        ka = [sb(f"ka{i}", [128, T + 2 * PAD], BF16) for i in range(2)]
        va = [[sb(f"va{i}{j}", [128, NBMAX, 128], BF16) for j in range(2)] for i in range(2)]
        bm = [sb(f"bm{i}", [128, 2, 4, 128], F32) for i in range(2)]
        pvb = [sb(f"pvb{i}", [128, T], F32) for i in range(2)]
        zb = [sb(f"zb{i}", [128, T], F32) for i in range(2)]
        sbi = [sb(f"sbi{i}", [128, 128], F32) for i in range(4)]
        pt = [sb(f"pt{i}", [128, 128], BF16) for i in range(8)]
        olo = sb("olo", [128, 128], BF16)
        ohi = sb("ohi", [128, 128], BF16)
        ps = [st.enter_context(nc.psum_tensor(_uid(f"ps{i}"), [128, TT], F32)) for i in range(8)]
        B = Buf
        qab = [B("a"), B("b")]
        kab = [B("a"), B("b")]
        vab = [[B("a"), B("b")], [B("c"), B("d")]]
        bmb = [B("a"), B("b")]
        pvbb = [B("a"), B("b")]
        zbb = [B("a"), B("b")]
        sbib = [B("s") for _ in range(4)]
        ptb = [B("p") for _ in range(8)]
        olob, ohib = B("olo"), B("ohi")
        psb = [B(f"ps{i}") for i in range(8)]
        S.op(PMS, lambda e: e.memset(olo[:, :], 0.0), writes=[olob])
        S.op(PMS, lambda e: e.memset(olo[:, 0:64], 1.0), writes=[olob])
        S.op(PMS, lambda e: e.memset(ohi[:, :], 0.0), writes=[ohib])
        S.op(PMS, lambda e: e.memset(ohi[:, 64:128], 1.0), writes=[ohib])
        for i in range(2):
            for j in range(2):
                S.op(PMS, lambda e, i=i, j=j: e.memset(va[i][j][:, :, :], 0.0), writes=[vab[i][j]])
        on = [olo, ohi]
        onb = [olob, ohib]

        def load(c):
            i = c % 2
            S.dma("sp", qa[i][:, :], QT[c * 128:(c + 1) * 128, :], reads=[QTb], writes=[qab[i]])
            S.dma("sp", ka[i][:, :], KT[c * 128:(c + 1) * 128, :], reads=[KTb], writes=[kab[i]])
            S.dma("sp", bm[i][:, :, :, :], bm_d[c, :, :].rearrange("p (s v j) -> p s v j", s=2, v=4),
                  writes=[bmb[i]])
            for s_ in range(2):
                h = 2 * c + s_
                d = A_DIL[h]
                nbl = T // (128 * d) + 1
                for r in range(d):
                    start = PAD + r - 64 * d
                    src = V[start:start + d * (128 * nbl - 1) + 1:d, h * 64:(h + 1) * 64].rearrange(
                        "(m i) f -> i m f", i=128)
                    S.dma("sp", va[i][s_][:, r * nbl:(r + 1) * nbl, s_ * 64:(s_ + 1) * 64], src,
                          reads=[Vb, vab[i][s_]], pwrites=[vab[i][s_]])
        load(0)
        nn = 0
        for c in range(8):
            if c + 1 < 8:
                load(c + 1)
            i = c % 2
            groups = []
            for s_ in range(2):
                h = 2 * c + s_
                d = A_DIL[h]
                nq = (T // d) // 128
                for r in range(d):
                    for b in range(nq):
                        groups.append((s_, d, nq, r, b))

            def stage1(g):
                nonlocal nn
                s_, d, nq, r, b = g
                nbl = nq + 1
                rows = slice(s_ * 64, (s_ + 1) * 64)
                qsl = slice(r + 128 * d * b, r + 128 * d * b + 127 * d + 1, d)
                pts = []
                for mi in range(2):
                    m = b + mi
                    k0 = PAD + r - 64 * d + 128 * d * m
                    ksl = slice(k0, k0 + 127 * d + 1, d)
                    if mi == 0:
                        var = 1 if b == 0 else 0
                    else:
                        var = 3 if b == nq - 1 else 2
                    n4, n8 = nn % 4, nn % 8
                    nn += 1
                    sp_, spb = ps[n4], psb[n4]
                    S.mm([(sp_[:, 0:128], ka[i][rows, ksl], qa[i][rows, qsl])],
                         reads=[kab[i], qab[i]], writes=[spb])
                    S.op("dve", lambda e, n4=n4, sp_=sp_, s_=s_, var=var, i=i: e.tensor_tensor(
                        out=sbi[n4][:, :], in0=sp_[:, 0:128], in1=bm[i][:, s_, var, :], op=ALU.add),
                        reads=[spb, bmb[i]], writes=[sbib[n4]])
                    S.op("act", lambda e, n4=n4, n8=n8: e.activation(
                        out=pt[n8][:, :], in_=sbi[n4][:, :], func=AF.Exp),
                        reads=[sbib[n4]], writes=[ptb[n8]])
                    pts.append((n8, r * nbl + m))
                return pts, (nn // 2) % 2

            def stage2(g, st1):
                s_, d, nq, r, b = g
                pts, a = st1
                rows = slice(s_ * 64, (s_ + 1) * 64)
                qsl = slice(r + 128 * d * b, r + 128 * d * b + 127 * d + 1, d)
                pv_, pv_b = ps[4 + a], psb[4 + a]
                z_, z_b = ps[6 + a], psb[6 + a]
                S.mm([(pv_[:, 0:128], va[i][s_][:, blk, :], pt[n8][:, :]) for (n8, blk) in pts],
                     reads=[ptb[n8] for (n8, _) in pts] + [vab[i][s_]], writes=[pv_b])
                S.mm([(z_[:, 0:128], on[s_][:, :], pt[n8][:, :]) for (n8, blk) in pts],
                     reads=[ptb[n8] for (n8, _) in pts] + [onb[s_]], writes=[z_b])
                S.op("act", lambda e, pv_=pv_, rows=rows, qsl=qsl, i=i: e.activation(
                    out=pvb[i][rows, qsl], in_=pv_[rows, 0:128], func=AF.Copy),
                    reads=[pv_b], pwrites=[pvbb[i]])
                S.op("dve", lambda e, z_=z_, rows=rows, qsl=qsl, i=i: e.tensor_copy(
                    out=zb[i][rows, qsl], in_=z_[rows, 0:128]),
                    reads=[z_b], pwrites=[zbb[i]])

            st = stage1(groups[0])
            for gi, g in enumerate(groups):
                nxt = stage1(groups[gi + 1]) if gi + 1 < len(groups) else None
                stage2(g, st)
                st = nxt
            S.dma("sp", PVun[c * 128:(c + 1) * 128, :], pvb[i][:, :], reads=[pvbb[i]], pwrites=[PVb_d])
            S.dma("sp", Zbc[2 * c:2 * c + 1, :], zb[i][0:1, :], reads=[zbb[i]], pwrites=[Zb_d])
            S.dma("sp", Zbc[2 * c + 1:2 * c + 2, :], zb[i][64:65, :], reads=[zbb[i]], pwrites=[Zb_d])
        S.flush()


def phase_comb_a(nc, S, T, PVun, PVb_d, Zc, Zb_d, OT, OTb, asel_d, bsel_d, esel_d):
    ntile = T // TT
    with ExitStack() as st:
        sb = lambda name, shape, dt: st.enter_context(nc.sbuf_tensor(_uid(name), shape, dt))
        zt = [sb(f"zt{i}", [16, TT], F32) for i in range(2)]
        pv = [sb(f"pv{i}", [128, NCH, TT], F32) for i in range(2)]
        asel = sb("asel", [16, 3], F32)
        bsel = sb("bsel", [3, 16], F32)
        esel = sb("esel", [16, NCH * 128], F32)
        o33 = sb("o33", [3, 3], F32)
        sg = sb("sg", [3, TT], F32)
        rt = sb("rt", [3, TT], F32)
        al = sb("al", [3, TT], F32)
        rz = sb("rz", [16, TT], F32)
        f16 = sb("f16", [16, TT], F32)
        ost = [sb(f"ost{i}", [128, NCH, TT], BF16) for i in range(2)]
        ps = [st.enter_context(nc.psum_tensor(_uid(f"ps{i}"), [128, TT], F32)) for i in range(8)]
        B = Buf
        ztb = [B("a"), B("b")]
        pvb = [B("a"), B("b")]
        aselb, bselb, eselb, o33b, sgb, rtb, alb, rzb, f16b = (B("a"), B("b"), B("e"), B("c"), B("d"), B("e"),
                                                               B("f"), B("g"), B("h"))
        ostb = [B("a"), B("b")]
        psb = [B(f"ps{i}") for i in range(8)]
        S.dma("sp", asel[:, :], asel_d, writes=[aselb])
        S.dma("sp", bsel[:, :], bsel_d, writes=[bselb])
        S.dma("sp", esel[:, :], esel_d, writes=[eselb])
        S.op(PMS, lambda e: e.memset(o33[:, :], 1.0), writes=[o33b])

        def load(t):
            S.dma("sp", zt[t % 2][:, :], Zc[:, t * TT:(t + 1) * TT], reads=[Zb_d], writes=[ztb[t % 2]])
            S.dma("sp", pv[t % 2][:, :, :], PVun[:, t * TT:(t + 1) * TT].rearrange("(k p) n -> p k n", p=128),
                  reads=[PVb_d], writes=[pvb[t % 2]])
        load(0)
        for t in range(ntile):
            if t + 1 < ntile:
                load(t + 1)
            z_, z_b = zt[t % 2], ztb[t % 2]
            p_, p_b = pv[t % 2], pvb[t % 2]
            S.mm([(ps[0][0:3, :], asel[:, :], z_[:, :])], reads=[z_b, aselb], writes=[psb[0]])
            S.op("dve", lambda e: e.tensor_copy(out=sg[:, :], in_=ps[0][0:3, :]), reads=[psb[0]], writes=[sgb])
            S.mm([(ps[1][0:3, :], o33[:, :], sg[:, :])], reads=[sgb, o33b], writes=[psb[1]])
            S.op("dve", lambda e: e.reciprocal(out=rt[:, :], in_=ps[1][0:3, :]), reads=[psb[1]], writes=[rtb])
            S.op("dve", lambda e: e.scalar_tensor_tensor(out=al[:, :], in0=sg[:, :], scalar=3.0, in1=rt[:, :],
                                                         op0=ALU.mult, op1=ALU.mult),
                 reads=[sgb, rtb], writes=[alb])
            S.mm([(ps[2][0:16, :], bsel[:, :], al[:, :])], reads=[alb, bselb], writes=[psb[2]])
            S.op("dve", lambda e, z_=z_: e.reciprocal(out=rz[:, :], in_=z_[:, :]), reads=[z_b], writes=[rzb])
            S.op("dve", lambda e: e.tensor_tensor(out=f16[:, :], in0=ps[2][0:16, :], in1=rz[:, :], op=ALU.mult),
                 reads=[psb[2], rzb], writes=[f16b])
            o_, o_b = ost[t % 2], ostb[t % 2]
            for c in range(NCH):
                pa, pab = ps[3 + c % 4], psb[3 + c % 4]
                S.mm([(pa[:, :], esel[:, c * 128:(c + 1) * 128], f16[:, :])], reads=[f16b, eselb], writes=[pab])
                S.op("dve", lambda e, pa=pa, p_=p_, c=c, o_=o_: e.tensor_tensor(
                    out=o_[:, c, :], in0=pa[:, :], in1=p_[:, c, :], op=ALU.mult),
                    reads=[pab, p_b], pwrites=[o_b])
            S.dma("sp", OT[:, t * TT:(t + 1) * TT].rearrange("(c p) n -> p c n", p=128), o_[:, :, :],
                  reads=[o_b], pwrites=[OTb])
        S.flush()


def lambda_init_fn(layer_idx):
    return 0.8 - 0.6 * math.exp(-0.3 * layer_idx)


def build(T, layers=(0, 1, 2, 3), final=True):
    nc = bass.Bass("TRN2", target_bir_lowering=False)
    ntile = T // TT
    import os
    limit = int(os.environ.get("K_STOP", "1000"))
    count = [0]

    def RUN(fn, *a):
        count[0] += 1
        if count[0] <= limit:
            fn(*a)
    with ExitStack() as st:
        S = Sched(nc, st)

        def ext(name, shape, dtype=F32):
            return nc.dram_tensor(name, list(shape), dtype, kind="ExternalInput").ap()

        def internal(name, shape, dtype):
            return nc.dram_tensor(name, list(shape), dtype, kind="Internal").ap()

        xT = ext("xT", [D, T])
        outT = nc.dram_tensor("outT", [D, T], F32, kind="ExternalOutput").ap()
        kinds = {li % 3 for li in layers}
        W = {"mlp_w_in": ext("mlp_w_in", [4, D, DFF]), "mlp_w_out": ext("mlp_w_out", [4, DFF, D])}
        if 0 in kinds:
            W["a_w_qkv"] = ext("a_w_qkv", [2, D, 3072])
            W["a_w_o"] = ext("a_w_o", [2, D, D])
        if 1 in kinds:
            W["b_w_qkv"] = ext("b_w_qkv", [1, D, 3072])
            W["b_w_o"] = ext("b_w_o", [1, D, D])
        if 2 in kinds:
            W["c_w_qkv"] = ext("c_w_qkv", [1, D, 1536])
            W["c_w_o"] = ext("c_w_o", [1, D, D])
        gmix = ext("gmix", [128, 32])
        gmlp = ext("gmlp", [128, 32])
        gfin = ext("gfin", [128, 8])
        gtab = ext("gtab", [16, 128, GW])
        bconst = ext("bconst", [128, 32])
        lam = ext("lam", [128, 256])
        gsub = ext("gsub", [128, 1])
        ctab = ext("ctab", [128, T])
        stab = ext("stab", [128, T])
        pmat = ext("pmat", [128, 128])
        qkg = ext("qkg", [128, 2])
        bm = ext("bm", [8, 128, 1024])
        asel = ext("asel", [16, 3])
        bsel = ext("bsel", [3, 16])
        esel = ext("esel", [16, 1024])

        hT = internal("hT", [D, T], F32)
        QT = internal("QT", [D, T], BF16)
        KT = internal("KT", [D, T + 2 * PAD], BF16)
        V = internal("V", [T + 2 * PAD, D], BF16)
        OT = internal("OT", [D, T], BF16)
        PVun = internal("PVun", [D, T], F32)
        Zbc = internal("Zbc", [16, T], F32)
        hTb = [Buf(f"hT{t}") for t in range(ntile)]
        QTb, KTb, Vb, OTb, PVb, Zb, outb = (Buf("QT"), Buf("KT"), Buf("V"), Buf("OT"), Buf("PV"),
                                            Buf("Z"), Buf("out"))

        wbf = {}
        castchain = Buf("castchain")
        pending_casts = []

        def emit_casts():
            while pending_casts:
                pending_casts.pop(0)()

        def cast(name, idx, rows, cols):
            src = W[name][idx]
            dst = internal(f"{name}{idx}_bf", [rows, cols], BF16)
            sv, dv = src, dst
            if cols > 2048:
                c = 2048 if cols % 2048 == 0 else 1024
                sv = sv.rearrange("a (b c) -> (a b) c", c=c)
                dv = dv.rearrange("a (b c) -> (a b) c", c=c)
            elif cols == 1024:
                sv = sv.rearrange("(a b) c -> a (b c)", b=2)
                dv = dv.rearrange("(a b) c -> a (b c)", b=2)
            b = Buf(name)
            pending_casts.append(lambda dv=dv, sv=sv, b=b: S.dma("pool", dv, sv, writes=[b, castchain]))
            wbf[(name, idx)] = (dst, [b] * 8)

        first_f32 = (layers[0] % 3) in (0, 1)
        for n_, li in enumerate(layers):
            kind, j = li % 3, li // 3
            pre = "abc"[kind]
            if n_ == 0 and first_f32:
                wbf[(f"{pre}_w_qkv", j)] = (None, None)
            else:
                cast(f"{pre}_w_qkv", j, D, 1536 if kind == 2 else 3072)
            cast(f"{pre}_w_o", j, D, D)
            cast("mlp_w_in", li, D, DFF)
            cast("mlp_w_out", li, DFF, D)

        if not first_f32:
            emit_casts()

        with ExitStack() as st2:
            zt = st2.enter_context(nc.sbuf_tensor(_uid("zt"), [128, NCH, PAD], BF16))
            ztb = Buf("zt")
            S.op(PMS, lambda e: e.memset(zt[:, :, :], 0.0), writes=[ztb])
            S.dma("sp", KT[:, 0:PAD].rearrange("(c p) n -> p c n", p=128), zt[:, :, :], reads=[ztb], pwrites=[KTb])
            S.dma("sp", KT[:, PAD + T:].rearrange("(c p) n -> p c n", p=128), zt[:, :, :], reads=[ztb], pwrites=[KTb])
            S.dma("sp", V[0:PAD, :].rearrange("(b p) f -> p b f", p=128), zt[:, :, :], reads=[ztb], pwrites=[Vb])
            S.dma("sp", V[PAD + T:, :].rearrange("(b p) f -> p b f", p=128), zt[:, :, :], reads=[ztb], pwrites=[Vb])
            S.flush(wait_bufs=[KTb, Vb])
        xTb = [Buf(f"xT{t}") for t in range(ntile)]

        for n_, li in enumerate(layers):
            kind, j = li % 3, li // 3
            pre = "abc"[kind]
            wq, wqb = wbf[(f"{pre}_w_qkv", j)]
            wf32 = W[f"{pre}_w_qkv"][j] if (n_ == 0 and first_f32) else None
            hin = (xT, xTb) if n_ == 0 else (None, None)
            wo, wob = wbf[(f"{pre}_w_o", j)]
            wi, wib = wbf[("mlp_w_in", li)]
            wo2, wo2b = wbf[("mlp_w_out", li)]
            g1 = gmix[:, 8 * li:8 * li + 8]
            g2 = gmlp[:, 8 * li:8 * li + 8]
            if kind == 0:
                RUN(phase_qkv_ab, nc, S, T, hT, hTb, wq, wqb, g1, QT, QTb, KT, KTb, V, Vb, wf32, *hin)
                emit_casts()
                RUN(phase_attn_a, nc, S, T, QT, QTb, KT, KTb, V, Vb, PVun, PVb, Zbc, Zb, bm)
                RUN(phase_comb_a, nc, S, T, PVun, PVb, Zbc, Zb, OT, OTb, asel, bsel, esel)
            elif kind == 1:
                RUN(phase_qkv_ab, nc, S, T, hT, hTb, wq, wqb, g1, QT, QTb, KT, KTb, V, Vb, wf32, *hin)
                emit_casts()
                RUN(phase_attn_b, nc, S, T, QT, QTb, KT, KTb, V, Vb, OT, OTb, gtab, bconst, lam, gsub,
                             lambda_init_fn(li))
            else:
                RUN(phase_qkv_c, nc, S, T, hT, hTb, wq, wqb, g1, QT, QTb, KT, KTb, V, Vb, ctab, stab, pmat, qkg, *hin)
                RUN(phase_attn_c, nc, S, T, QT, QTb, KT, KTb, V, Vb, OT, OTb)
            RUN(phase_wo, nc, S, T, OT, OTb, wo, wob, hT, hTb, *hin)
            RUN(phase_mlp, nc, S, T, hT, hTb, wi, wo2, [wib, wo2b], g2)
        if final:
            RUN(phase_final, nc, S, T, hT, hTb, gfin, outT, outb)
        else:
            with ExitStack() as st2:
                for t in range(ntile):
                    S.dma("sp", outT[:, t * TT:(t + 1) * TT], hT[:, t * TT:(t + 1) * TT], reads=[hTb[t]],
                          pwrites=[outb])
                S.flush(wait_bufs=[outb], wait_all=True)
    return nc


def t5_bucket_np(rel):
    rel = np.asarray(rel, dtype=np.int64)
    nb = 16
    max_exact = 8
    side = np.where(rel > 0, nb, 0)
    n = np.abs(rel)
    nf = np.maximum(n, 1).astype(np.float32)
    large = max_exact + (np.log(nf / np.float32(max_exact)) / np.float32(math.log(1024 / max_exact))
                         * np.float32(nb - max_exact)).astype(np.int32)
    large = np.minimum(large, nb - 1)
    return (side + np.where(n < max_exact, n, large)).astype(np.int64)


def host_tables(T, inputs):
    f32 = np.float32
    rb = np.asarray(inputs["rel_bias"], f32)
    tabs = {}
    i = np.arange(128)[:, None]
    col = np.arange(GW)[None, :]
    bk = t5_bucket_np(i - col + GC)
    tabs["gtab"] = np.ascontiguousarray(np.transpose(rb[bk], (2, 0, 1)))
    bc = np.concatenate([rb[15], rb[31]])[None, :]
    tabs["bconst"] = np.ascontiguousarray(np.repeat(bc, 128, axis=0))
    tabs["lam"] = np.ascontiguousarray(np.repeat(np.asarray(inputs["b_lambda"], f32).reshape(1, 256), 128, 0))
    tabs["gsub"] = np.ascontiguousarray(np.asarray(inputs["b_subln_g"], f32).reshape(128, 1))
    NEG = f32(-30000.0)
    bm = np.zeros((16, 4, 128, 128), f32)
    ii = np.arange(128)[:, None]
    jj = np.arange(128)[None, :]
    for h in range(16):
        d = A_DIL[h]
        o0 = ii - 64 - jj
        o1 = ii + 64 - jj
        b0 = rb[t5_bucket_np(o0 * d), h]
        b1 = rb[t5_bucket_np(o1 * d), h]
        v0 = ii >= jj
        v1 = ii <= jj
        bm[h, 0] = np.where(v0, b0, NEG)
        bm[h, 1] = np.where(v0 & (ii >= 64), b0, NEG)
        bm[h, 2] = np.where(v1, b1, NEG)
        bm[h, 3] = np.where(v1 & (ii < 64), b1, NEG)
    bm = bm.reshape(8, 2, 4, 128, 128).transpose(0, 3, 1, 2, 4).reshape(8, 128, 1024)
    tabs["bm"] = np.ascontiguousarray(bm)
    asel = np.zeros((16, 3), f32)
    bsel = np.zeros((3, 16), f32)
    esel = np.zeros((16, 1024), f32)
    for h in range(16):
        g = A_GRP[h]
        asel[h, g] = 1.0 / A_NH[g]
        bsel[g, h] = 1.0
        esel[h, h * 64:(h + 1) * 64] = 1.0
    tabs["asel"], tabs["bsel"], tabs["esel"] = asel, bsel, esel
    pos = np.arange(T)
    row = (pos // 64).astype(f32)
    colp = (pos % 64).astype(f32)
    inv = (f32(10000.0) ** (-np.arange(16, dtype=f32) / f32(16))).astype(f32)
    ang = np.concatenate([row[:, None] * inv, colp[:, None] * inv], axis=-1).astype(f32)
    cos, sin = np.cos(ang).astype(f32), np.sin(ang).astype(f32)
    ct = np.zeros((64, T), f32)
    stt = np.zeros((64, T), f32)
    pm = np.zeros((128, 128), f32)
    for a in range(2):
        for jh in range(2):
            for f in range(16):
                dd = a * 32 + jh * 16 + f
                ct[dd] = cos[:, a * 16 + f]
                stt[dd] = (-sin[:, a * 16 + f]) if jh == 0 else sin[:, a * 16 + f]
                other = a * 32 + (1 - jh) * 16 + f
                for hh in range(2):
                    pm[hh * 64 + other, hh * 64 + dd] = 1.0
    tabs["ctab"] = np.ascontiguousarray(np.concatenate([ct, ct], 0))
    tabs["stab"] = np.ascontiguousarray(np.concatenate([stt, stt], 0))
    tabs["pmat"] = pm
    qg = np.asarray(inputs["c_q_norm_g"], f32).reshape(64)
    kg = np.asarray(inputs["c_k_norm_g"], f32).reshape(64)
    tabs["qkg"] = np.ascontiguousarray(np.stack([np.tile(qg, 2), np.tile(kg, 2)], axis=1))
    def gl(a):
        a = np.asarray(a, f32).reshape(-1, 8, 128)
        return np.ascontiguousarray(a.transpose(2, 0, 1).reshape(128, -1))
    tabs["gmix"] = gl(inputs["norm_mix_g"])
    tabs["gmlp"] = gl(inputs["norm_mlp_g"])
    tabs["gfin"] = gl(inputs["norm_final_g"])
    for k in ("a_w_qkv", "a_w_o", "b_w_qkv", "b_w_o", "c_w_qkv", "c_w_o", "mlp_w_in", "mlp_w_out"):
        tabs[k] = np.ascontiguousarray(np.asarray(inputs[k], f32))
    return tabs


_PROGRAM_CACHE = {}


def kernel(**inputs):
    x = np.asarray(inputs["x"], np.float32)
    Bn, T, _ = x.shape
    key = (T,)
    if key not in _PROGRAM_CACHE:
        _PROGRAM_CACHE[key] = build(T)
    nc = _PROGRAM_CACHE[key]
    tabs = host_tables(T, inputs)
    in_maps = []
    for c in range(8):
        m = dict(tabs)
        m["xT"] = np.ascontiguousarray(x[c % Bn].T)
        in_maps.append(m)
    res = run_bass_kernel_spmd(nc, in_maps, core_ids=list(range(8)))
    out = np.stack([np.ascontiguousarray(res.results[b]["outT"].T) for b in range(Bn)], axis=0)
    return out.astype(np.float32)
```

```python
import math
from contextlib import ExitStack

import numpy as np
import concourse.bass as bass
import concourse.mybir as mybir
from concourse.bass_utils import run_bass_kernel_spmd

F32 = mybir.dt.float32
BF16 = mybir.dt.bfloat16
AF = mybir.ActivationFunctionType
ALU = mybir.AluOpType

D = 1024
NCH = 8
DFF = 4096
NFC = 32
EPS = 1e-6
TT = 512


import os as _os
PTT = "dve" if _os.environ.get("K_NOPTT", "0") == "1" else "pool"
PMS = "pool" if _os.environ.get("K_POOLMS", "0") == "1" else "dve"
_UID = [0]


def _uid(name):
    _UID[0] += 1
    return f"{name}_u{_UID[0]}"


class Buf:
    def __init__(self, name):
        self.name = name
        self.w = {}
        self.r = {}


class Sched:
    ENGS = ("pe", "act", "dve", "pool", "sp")

    def __init__(self, nc, stack, n_dma_sems=60):
        self.nc = nc
        self.sem = {e: stack.enter_context(nc.semaphore(f"sem_{e}")) for e in self.ENGS}
        self.cnt = {e: 0 for e in self.ENGS}
        self.dsem = [stack.enter_context(nc.semaphore(f"sem_dma{i}")) for i in range(n_dma_sems)]
        self.dcnt = [0] * n_dma_sems
        self.dnext = 0
        self.n_sw = 16
        self.dnext_sw = 0
        self.waited = {}
        self.q = {e: [] for e in self.ENGS}
        self.semobj = {}
        self.nblocks = 0

    def _key(self, sem):
        k = id(sem)
        self.semobj[k] = sem
        return k

    def _wait(self, e, toks):
        for k, val in toks.items():
            if self.waited.get((e, k), 0) >= val:
                continue
            self.waited[(e, k)] = val
            sem = self.semobj[k]
            self.q[e].append(lambda eng, sem=sem, val=val: eng.wait_ge(sem, val))

    def _deps(self, e, reads, writes, pwrites):
        toks = {}

        def add(d):
            for k, v in d.items():
                if v > toks.get(k, 0):
                    toks[k] = v
        for b in reads:
            add(b.w)
        for b in writes:
            add(b.w)
            add(b.r)
        for b in pwrites:
            add(b.r)
        if e == "pe":
            toks.pop(self._key(self.sem[e]), None)
        return toks

    def _commit(self, tok, reads, writes, pwrites=()):
        k = self._key(tok[0])
        for b in reads:
            if b.r.get(k, 0) < tok[1]:
                b.r[k] = tok[1]
        for b in writes:
            b.w = {k: tok[1]}
            b.r = {}
        for b in pwrites:
            if b.w.get(k, 0) < tok[1]:
                b.w[k] = tok[1]

    def op(self, e, fn, reads=(), writes=(), pwrites=()):
        self._wait(e, self._deps(e, reads, writes, pwrites))
        self.cnt[e] += 1
        sem = self.sem[e]
        self.q[e].append(lambda eng, fn=fn, sem=sem: fn(eng).then_inc(sem, 1))
        self._commit((sem, self.cnt[e]), reads, writes, pwrites)

    def mm(self, mms, reads=(), writes=(), pwrites=(), start=True, stop=True):
        e = "pe"
        self._wait(e, self._deps(e, reads, writes, pwrites))
        self.cnt[e] += 1
        sem = self.sem[e]
        n = len(mms)

        def run(eng, mms=mms, sem=sem, n=n, start=start, stop=stop):
            for i, (o, l, r) in enumerate(mms):
                ins = eng.matmul(o, l, r, start=(start and i == 0), stop=(stop and i == n - 1))
            ins.then_inc(sem, 1)
        self.q[e].append(run)
        self._commit((sem, self.cnt[e]), reads, writes, pwrites)

    def dma(self, e, out, in_, reads=(), writes=(), pwrites=()):
        toks = self._deps(e, reads, writes, pwrites)
        if e == "pool":
            i = self.dnext_sw
            self.dnext_sw = (self.dnext_sw + 1) % self.n_sw
        else:
            i = self.n_sw + self.dnext
            self.dnext = (self.dnext + 1) % (len(self.dsem) - self.n_sw)
        sem = self.dsem[i]
        k = self._key(sem)
        if self.dcnt[i] > toks.get(k, 0):
            toks[k] = self.dcnt[i]
        self._wait(e, toks)
        self.dcnt[i] += 16
        self.q[e].append(lambda eng, out=out, in_=in_, sem=sem:
                         eng.dma_start(out=out, in_=in_).then_inc(sem, 16))
        self._commit((sem, self.dcnt[i]), reads, writes, pwrites)

    def flush(self, wait_bufs=(), wait_all=False):
        toks = {}
        for i, sem in enumerate(self.dsem):
            if self.dcnt[i] > 0 and (wait_all or i >= self.n_sw):
                toks[self._key(sem)] = self.dcnt[i]
        for b in wait_bufs:
            for k, v in b.w.items():
                if v > toks.get(k, 0):
                    toks[k] = v
        if toks:
            self._wait("sp", toks)
        q = self.q
        self.q = {e: [] for e in self.ENGS}
        self.nblocks += 1
        with self.nc.Block() as block:
            @block.tensor
            def _(eng):
                for f in q["pe"]:
                    f(eng)

            @block.scalar
            def _(eng):
                for f in q["act"]:
                    f(eng)

            @block.vector
            def _(eng):
                for f in q["dve"]:
                    f(eng)

            @block.gpsimd
            def _(eng):
                for f in q["pool"]:
                    f(eng)

            @block.sync
            def _(eng):
                for f in q["sp"]:
                    f(eng)


def emit_norm(S, T0, hx, hxb, sq, sqb, hn, hnb, gcol, ones, ps, psb, rstd, rstdb, gcolb, onesb):
    S.op("act", lambda e: e.activation(out=sq[:, :, :], in_=hx[:, :, :], func=AF.Square),
         reads=[hxb], writes=[sqb])
    S.mm([(ps[:, :], ones[:, :], sq[:, k, :]) for k in range(NCH)], reads=[sqb, onesb], writes=[psb])
    S.op("act", lambda e: e.activation(out=rstd[:, :], in_=ps[:, :], func=AF.Sqrt,
                                       scale=1.0 / D, bias=EPS),
         reads=[psb], writes=[rstdb])
    S.op("dve", lambda e: e.reciprocal(out=rstd[:, :], in_=rstd[:, :]),
         reads=[rstdb], writes=[rstdb])
    for k in range(NCH):
        S.op("dve", lambda e, k=k: e.scalar_tensor_tensor(
            out=hn[:, k, :], in0=hx[:, k, :], scalar=gcol[:, k:k + 1], in1=rstd[:, :],
            op0=ALU.mult, op1=ALU.mult), reads=[hxb, rstdb, gcolb], pwrites=[hnb])


def phase_mlp(nc, S, T, hT, hTb, w_in_bf, w_out_bf, wb, gnorm_dram):
    ntile = T // TT
    with ExitStack() as st:
        sb = lambda name, shape, dt: st.enter_context(nc.sbuf_tensor(_uid(name), shape, dt))
        win = sb("win", [128, NCH, DFF], BF16)
        wout = sb("wout", [128, NFC, D], BF16)
        hx = [sb(f"hx{i}", [128, NCH, TT], F32) for i in range(2)]
        hn = sb("hn", [128, NCH, TT], BF16)
        u = sb("u", [128, NFC, TT], BF16)
        r = [sb(f"r{i}", [128, TT], BF16) for i in range(2)]
        rstd = sb("rstd", [128, TT], F32)
        gcol = sb("gcol", [128, NCH], F32)
        ones = sb("ones", [128, 128], BF16)
        ps = [st.enter_context(nc.psum_tensor(_uid(f"ps{i}"), [128, TT], F32)) for i in range(8)]
        B = lambda n: Buf(n)
        winb, woutb, hnb, ub, rstdb, gcolb, onesb = (B("win"), B("wout"), B("hn"), B("u"),
                                                     B("rstd"), B("gcol"), B("ones"))
        hxb = [B("hx0"), B("hx1")]
        rb = [B("r0"), B("r1")]
        psb = [B(f"ps{i}") for i in range(8)]
        ucb = [B(f"u{c}") for c in range(NFC)]

        S.op(PMS, lambda e: e.memset(ones[:, :], 1.0), writes=[onesb])
        S.dma("sp", gcol[:, :], gnorm_dram, writes=[gcolb])
        def load(t):
            S.dma("sp", hx[t % 2][:, :, :],
                  hT[:, t * TT:(t + 1) * TT].rearrange("(k p) n -> p k n", p=128),
                  reads=[hTb[t]], writes=[hxb[t % 2]])

        load(0)
        for k in range(NCH):
            S.dma("sp", win[:, k, :], w_in_bf[k * 128:(k + 1) * 128, :], reads=[wb[0][k]], pwrites=[winb])
        for c4 in range(0, NFC, 4):
            S.dma("sp", wout[:, c4:c4 + 4, :],
                  w_out_bf[c4 * 128:(c4 + 4) * 128, :].rearrange("(c p) n -> p c n", p=128),
                  reads=[wb[1][c4 // 4]], pwrites=[woutb])

        sqc = [sb(f"sqc{i}", [128, TT], BF16) for i in range(2)]
        sqcb = [B("sqc0"), B("sqc1")]

        def norm_a_step(t, k):
            xx, xxb = hx[t % 2], hxb[t % 2]
            i = k % 2
            S.op("act", lambda e, xx=xx, i=i, k=k: e.activation(out=sqc[i][:, :], in_=xx[:, k, :], func=AF.Square),
                 reads=[xxb], writes=[sqcb[i]])
            S.mm([(ps[7][:, :], ones[:, :], sqc[i][:, :])], reads=[sqcb[i], onesb],
                 writes=[psb[7]] if k == 0 else [], pwrites=[] if k == 0 else [psb[7]],
                 start=(k == 0), stop=(k == NCH - 1))

        def norm_a_fin(t):
            S.op("act", lambda e: e.activation(out=rstd[:, :], in_=ps[7][:, :], func=AF.Sqrt,
                                               scale=1.0 / D, bias=EPS), reads=[psb[7]], writes=[rstdb])
            S.op("dve", lambda e: e.reciprocal(out=rstd[:, :], in_=rstd[:, :]), reads=[rstdb], writes=[rstdb])

        def norm_b(t):
            xx, xxb = hx[t % 2], hxb[t % 2]
            for k in range(NCH):
                S.op("dve", lambda e, xx=xx, k=k: e.scalar_tensor_tensor(
                    out=hn[:, k, :], in0=xx[:, k, :], scalar=gcol[:, k:k + 1], in1=rstd[:, :],
                    op0=ALU.mult, op1=ALU.mult), reads=[xxb, rstdb, gcolb], pwrites=[hnb])

        for t in range(ntile):
            if t + 1 < ntile:
                load(t + 1)
            x = hx[t % 2]
            xb = hxb[t % 2]
            if t == 0:
                for k in range(NCH):
                    norm_a_step(0, k)
                norm_a_fin(0)
                norm_b(0)
            for c in range(NFC):
                p = ps[c % 4]
                pb = psb[c % 4]
                S.mm([(p[:, :], win[:, k, c * 128:(c + 1) * 128], hn[:, k, :]) for k in range(NCH)],
                     reads=[hnb, winb], writes=[pb])
                rr, rrb = r[c % 2], rb[c % 2]
                S.op("act", lambda e, p=p, rr=rr: e.activation(out=rr[:, :], in_=p[:, :], func=AF.Relu),
                     reads=[pb], writes=[rrb])
                S.op("dve", lambda e, p=p, rr=rr, c=c: e.scalar_tensor_tensor(
                    out=u[:, c, :], in0=p[:, :], scalar=0.0, in1=rr[:, :],
                    op0=ALU.max, op1=ALU.mult), reads=[pb, rrb], writes=[ucb[c]])
                if t + 1 < ntile and 8 <= c < 16:
                    norm_a_step(t + 1, c - 8)
                if t + 1 < ntile and c == 16:
                    norm_a_fin(t + 1)
            S._commit((S.sem["pe"], S.cnt["pe"]), [hnb], [])
            if t + 1 < ntile:
                norm_b(t + 1)
            for o in range(NCH):
                p = ps[4 + o % 3]
                pb = psb[4 + o % 3]
                S.mm([(p[:, :], wout[:, c, o * 128:(o + 1) * 128], u[:, c, :]) for c in range(NFC)],
                     reads=ucb + [woutb], writes=[pb])
                S.op("dve", lambda e, p=p, x=x, o=o: e.tensor_tensor(
                    out=x[:, o, :], in0=p[:, :], in1=x[:, o, :], op=ALU.add),
                    reads=[pb, xb], pwrites=[xb])
            S._commit((S.sem["pe"], S.cnt["pe"]), ucb + [ub], [])
            S.dma("sp", hT[:, t * TT:(t + 1) * TT].rearrange("(k p) n -> p k n", p=128),
                  x[:, :, :], reads=[xb], writes=[hTb[t]])
        S.flush()


PAD = 1024
GW = 2266
GC = 1069
NEAR_LO, NEAR_HI = -686, 1070


def phase_qkv_ab(nc, S, T, hT, hTb, w_bf, wb, g_dram, QT, QTb, KT, KTb, V, Vb, w_f32=None, h_in=None, h_inb=None):
    ntile = T // TT
    with ExitStack() as st:
        sb = lambda name, shape, dt: st.enter_context(nc.sbuf_tensor(_uid(name), shape, dt))
        w = sb("wqkv", [128, NCH, 3072], BF16)
        hx = [sb(f"hx{i}", [128, NCH, TT], F32) for i in range(2)]
        sq = [sb(f"sq{i}", [128, NCH, TT], BF16) for i in range(2)]
        hn = [sb(f"hn{i}", [128, NCH, TT], BF16) for i in range(2)]
        rstd = sb("rstd", [128, TT], F32)
        gcol = sb("gcol", [128, NCH], F32)
        ones = sb("ones", [128, 128], BF16)
        qk = [sb(f"qk{i}", [128, 16, TT], BF16) for i in range(2)]
        vs = [sb(f"vs{i}", [128, 4, D], BF16) for i in range(2)]
        ps = [st.enter_context(nc.psum_tensor(_uid(f"ps{i}"), [128, TT], F32)) for i in range(8)]
        B = Buf
        wbuf, sqb, hnb, rstdb, gcolb, onesb = B("w"), B("sq"), B("hn"), B("rstd"), B("gcol"), B("ones")
        hxb = [B("hx0"), B("hx1")]
        sqb = [B("sq0"), B("sq1")]
        hnb = [B("hn0"), B("hn1")]
        qkb = [B("qk0"), B("qk1")]
        vsb = [B("vs0"), B("vs1")]
        psb = [B(f"ps{i}") for i in range(8)]
        S.op(PMS, lambda e: e.memset(ones[:, :], 1.0), writes=[onesb])
        S.dma("sp", gcol[:, :], g_dram, writes=[gcolb])
        def load(t):
            S.dma("sp", hx[t % 2][:, :, :],
                  (hT if h_in is None else h_in)[:, t * TT:(t + 1) * TT].rearrange("(k p) n -> p k n", p=128),
                  reads=[(hTb if h_in is None else h_inb)[t]], writes=[hxb[t % 2]])
        load(0)
        if w_f32 is None:
            for k in range(NCH):
                S.dma("sp", w[:, k, :], w_bf[k * 128:(k + 1) * 128, :], reads=[wb[k]], pwrites=[wbuf])
        else:
            wst = [sb(f"wst{i}", [128, 3072], F32) for i in range(2)]
            wstb = [B("wst0"), B("wst1")]
            for k in range(NCH):
                S.dma("sp", wst[k % 2][:, :], w_f32[k * 128:(k + 1) * 128, :], writes=[wstb[k % 2]])
                S.op("act", lambda e, k=k: e.activation(out=w[:, k, :], in_=wst[k % 2][:, :], func=AF.Copy),
                     reads=[wstb[k % 2]], pwrites=[wbuf])

        def norm(t):
            emit_norm(S, t, hx[t % 2], hxb[t % 2], sq[t % 2], sqb[t % 2], hn[t % 2], hnb[t % 2], gcol, ones,
                      ps[7], psb[7], rstd, rstdb, gcolb, onesb)
        for t in range(ntile):
            if t + 1 < ntile:
                load(t + 1)
            x, xb = hx[t % 2], hxb[t % 2]
            if t == 0:
                norm(0)
            hn_, hn_b = hn[t % 2], hnb[t % 2]
            q_, q_b = qk[t % 2], qkb[t % 2]
            for c in range(16):
                if c == 6 and t + 1 < ntile:
                    norm(t + 1)
                p, pb = ps[c % 4], psb[c % 4]
                S.mm([(p[:, :], w[:, k, c * 128:(c + 1) * 128], hn_[:, k, :]) for k in range(NCH)],
                     reads=[hn_b, wbuf], writes=[pb])
                if c < 8:
                    S.op("act", lambda e, p=p, q_=q_, c=c: e.activation(
                        out=q_[:, c, :], in_=p[:, :], func=AF.Copy, scale=0.125),
                        reads=[pb], pwrites=[q_b])
                else:
                    S.op("dve", lambda e, p=p, q_=q_, c=c: e.tensor_copy(out=q_[:, c, :], in_=p[:, :]),
                         reads=[pb], pwrites=[q_b])
            S.dma("sp", QT[:, t * TT:(t + 1) * TT].rearrange("(c p) n -> p c n", p=128),
                  q_[:, 0:8, :], reads=[q_b], pwrites=[QTb])
            S.dma("sp", KT[:, PAD + t * TT:PAD + (t + 1) * TT].rearrange("(c p) n -> p c n", p=128),
                  q_[:, 8:16, :], reads=[q_b], pwrites=[KTb])
            v_, v_b = vs[t % 2], vsb[t % 2]
            for tb in range(4):
                for half in range(2):
                    i = tb * 2 + half
                    p, pb = ps[4 + i % 3], psb[4 + i % 3]
                    S.mm([(p[:, :], hn_[:, k, tb * 128:(tb + 1) * 128],
                           w[:, k, 2048 + half * 512:2048 + (half + 1) * 512]) for k in range(NCH)],
                         reads=[hn_b, wbuf], writes=[pb])
                    if i % 2 == 0:
                        S.op("act", lambda e, p=p, v_=v_, tb=tb, half=half: e.activation(
                            out=v_[:, tb, half * 512:(half + 1) * 512], in_=p[:, :], func=AF.Copy),
                            reads=[pb], pwrites=[v_b])
                    else:
                        S.op("dve", lambda e, p=p, v_=v_, tb=tb, half=half: e.tensor_copy(
                            out=v_[:, tb, half * 512:(half + 1) * 512], in_=p[:, :]),
                            reads=[pb], pwrites=[v_b])
            S.dma("sp", V[PAD + t * TT:PAD + (t + 1) * TT, :].rearrange("(b p) f -> p b f", p=128),
                  v_[:, :, :], reads=[v_b], pwrites=[Vb])
            S._commit((S.sem["pe"], S.cnt["pe"]), [hn_b, sqb[t % 2]], [])
        S.flush()


def phase_wo(nc, S, T, OT, OTb, w_bf, wb, hT, hTb, h_in=None, h_inb=None):
    ntile = T // TT
    with ExitStack() as st:
        sb = lambda name, shape, dt: st.enter_context(nc.sbuf_tensor(_uid(name), shape, dt))
        w = sb("wo", [128, NCH, D], BF16)
        hx = [sb(f"hx{i}", [128, NCH, TT], F32) for i in range(2)]
        ot = [sb(f"ot{i}", [128, NCH, TT], BF16) for i in range(2)]
        ps = [st.enter_context(nc.psum_tensor(_uid(f"ps{i}"), [128, TT], F32)) for i in range(8)]
        B = Buf
        wbuf = B("w")
        hxb = [B("hx0"), B("hx1")]
        otb = [B("ot0"), B("ot1")]
        psb = [B(f"ps{i}") for i in range(8)]
        def load(t):
            S.dma("sp", hx[t % 2][:, :, :],
                  (hT if h_in is None else h_in)[:, t * TT:(t + 1) * TT].rearrange("(k p) n -> p k n", p=128),
                  reads=[(hTb if h_in is None else h_inb)[t]], writes=[hxb[t % 2]])
            S.dma("sp", ot[t % 2][:, :, :],
                  OT[:, t * TT:(t + 1) * TT].rearrange("(k p) n -> p k n", p=128),
                  reads=[OTb], writes=[otb[t % 2]])
        load(0)
        for k in range(NCH):
            S.dma("sp", w[:, k, :], w_bf[k * 128:(k + 1) * 128, :], reads=[wb[k]], pwrites=[wbuf])

        for t in range(ntile):
            if t + 1 < ntile:
                load(t + 1)
            x, xb = hx[t % 2], hxb[t % 2]
            o_, o_b = ot[t % 2], otb[t % 2]
            for o in range(NCH):
                p, pb = ps[o % 8], psb[o % 8]
                S.mm([(p[:, :], w[:, k, o * 128:(o + 1) * 128], o_[:, k, :]) for k in range(NCH)],
                     reads=[o_b, wbuf], writes=[pb])
                S.op("dve", lambda e, p=p, x=x, o=o: e.tensor_tensor(
                    out=x[:, o, :], in0=p[:, :], in1=x[:, o, :], op=ALU.add),
                    reads=[pb, xb], pwrites=[xb])
            S.dma("sp", hT[:, t * TT:(t + 1) * TT].rearrange("(k p) n -> p k n", p=128),
                  x[:, :, :], reads=[xb], writes=[hTb[t]])
        S.flush()


def phase_final(nc, S, T, hT, hTb, g_dram, outT, outb):
    ntile = T // TT
    with ExitStack() as st:
        sb = lambda name, shape, dt: st.enter_context(nc.sbuf_tensor(_uid(name), shape, dt))
        hx = [sb(f"hx{i}", [128, NCH, TT], F32) for i in range(2)]
        ho = [sb(f"ho{i}", [128, NCH, TT], F32) for i in range(2)]
        sq = sb("sq", [128, NCH, TT], BF16)
        rstd = sb("rstd", [128, TT], F32)
        gcol = sb("gcol", [128, NCH], F32)
        ones = sb("ones", [128, 128], BF16)
        ps = [st.enter_context(nc.psum_tensor(_uid(f"ps{i}"), [128, TT], F32)) for i in range(2)]
        B = Buf
        sqb, rstdb, gcolb, onesb = B("sq"), B("rstd"), B("gcol"), B("ones")
        hxb = [B("hx0"), B("hx1")]
        hob = [B("ho0"), B("ho1")]
        psb = [B("ps0"), B("ps1")]
        S.op(PMS, lambda e: e.memset(ones[:, :], 1.0), writes=[onesb])
        S.dma("sp", gcol[:, :], g_dram, writes=[gcolb])

        def load(t):
            S.dma("sp", hx[t % 2][:, :, :],
                  hT[:, t * TT:(t + 1) * TT].rearrange("(k p) n -> p k n", p=128),
                  reads=[hTb[t]], writes=[hxb[t % 2]])
        load(0)
        for t in range(ntile):
            if t + 1 < ntile:
                load(t + 1)
            emit_norm(S, t, hx[t % 2], hxb[t % 2], sq, sqb, ho[t % 2], hob[t % 2], gcol, ones,
                      ps[t % 2], psb[t % 2], rstd, rstdb, gcolb, onesb)
            S.dma("sp", outT[:, t * TT:(t + 1) * TT].rearrange("(k p) n -> p k n", p=128),
                  ho[t % 2][:, :, :], reads=[hob[t % 2]], pwrites=[outb])
        S.flush(wait_bufs=[outb], wait_all=True)


def phase_attn_b(nc, S, T, QT, QTb, KT, KTb, V, Vb, OT, OTb, gtab, bconst_d, lam_d, gsub_d, lam_init):
    NKB, NQT = T // 128, T // TT
    with ExitStack() as st:
        sb = lambda name, shape, dt: st.enter_context(nc.sbuf_tensor(_uid(name), shape, dt))
        qt_ = [sb(f"qt{i}", [128, T], BF16) for i in range(2)]
        kt_ = [sb(f"kt{i}", [128, T], BF16) for i in range(2)]
        vv = [sb(f"vv{i}", [128, NKB, 128], BF16) for i in range(2)]
        gt = [sb(f"gt{i}", [128, 2, GW], F32) for i in range(2)]
        sbi = [sb(f"sbi{i}", [128, TT], F32) for i in range(4)]
        pt = [sb(f"pt{i}", [128, TT], BF16) for i in range(8)]
        ones = sb("ones", [128, 128], BF16)
        bconst = sb("bconst", [128, 32], F32)
        lam = sb("lam", [128, 256], F32)
        ltmp = sb("ltmp", [128, 64], F32)
        lsc = sb("lsc", [128, 8], F32)
        gsub = sb("gsub", [128, 2], F32)
        ef = [sb(f"ef{i}", [128, TT], F32) for i in range(5)]
        osq = sb("osq", [128, TT], BF16)
        ost = [sb(f"ost{i}", [128, TT], BF16) for i in range(2)]
        zs = [sb(f"zs{i}", [128, TT], F32) for i in range(2)]
        pvs = [sb(f"pvs{i}", [128, TT], F32) for i in range(2)]
        zsb = [Buf("zs0"), Buf("zs1")]
        pvsb = [Buf("pvs0"), Buf("pvs1")]
        deferred = []
        ps = [st.enter_context(nc.psum_tensor(_uid(f"ps{i}"), [128, TT], F32)) for i in range(8)]
        B = Buf
        qtb = [B("qt0"), B("qt1")]
        ktb = [B("kt0"), B("kt1")]
        vvb = [B("vv0"), B("vv1")]
        gtb = [B("gt0"), B("gt1")]
        sbib = [B(f"sbi{i}") for i in range(4)]
        ptb = [B(f"pt{i}") for i in range(8)]
        efb = [B(f"ef{i}") for i in range(5)]
        onesb, bcb, lamb, ltb, lscb, gsubb, osqb = (B("ones"), B("bc"), B("lam"), B("lt"), B("lsc"),
                                                    B("gsub"), B("osq"))
        ostb = [B("ost0"), B("ost1")]
        psb = [B(f"ps{i}") for i in range(8)]
        psS = [[ps[0], ps[1]], [ps[2], ps[3]]]
        psSb = [[psb[0], psb[1]], [psb[2], psb[3]]]
        PV, PVb = [ps[4], ps[5]], [psb[4], psb[5]]
        Z, Zb = [ps[6], ps[7]], [psb[6], psb[7]]

        S.op(PMS, lambda e: e.memset(ones[:, :], 1.0), writes=[onesb])
        S.dma("sp", bconst[:, :], bconst_d, writes=[bcb])
        S.dma("sp", lam[:, :], lam_d, writes=[lamb])
        S.dma("sp", gsub[:, 0:1], gsub_d, writes=[gsubb])
        S.op("dve", lambda e: e.scalar_tensor_tensor(out=ltmp[:, :], in0=lam[:, 0:64], scalar=1.0,
                                                     in1=lam[:, 64:128], op0=ALU.mult, op1=ALU.mult,
                                                     accum_out=lsc[:, 0:1]),
             reads=[lamb], writes=[ltb, lscb])
        S.op("dve", lambda e: e.scalar_tensor_tensor(out=ltmp[:, :], in0=lam[:, 128:192], scalar=1.0,
                                                     in1=lam[:, 192:256], op0=ALU.mult, op1=ALU.mult,
                                                     accum_out=lsc[:, 1:2]),
             reads=[lamb, lscb], writes=[ltb, lscb])
        S.op("act", lambda e: e.activation(out=lsc[:, 2:4], in_=lsc[:, 0:2], func=AF.Exp),
             reads=[lscb], writes=[lscb])
        S.op("dve", lambda e: e.tensor_tensor(out=lsc[:, 4:5], in0=lsc[:, 3:4], in1=lsc[:, 2:3],
                                              op=ALU.subtract), reads=[lscb], writes=[lscb])
        S.op("dve", lambda e: e.tensor_scalar(out=lsc[:, 5:6], in0=lsc[:, 4:5], scalar1=-float(lam_init),
                                              scalar2=None, op0=ALU.add), reads=[lscb], writes=[lscb])
        S.op("dve", lambda e: e.tensor_scalar(out=gsub[:, 1:2], in0=gsub[:, 0:1],
                                              scalar1=float(1.0 - lam_init), scalar2=None, op0=ALU.mult),
             reads=[gsubb], writes=[gsubb])
        neglam = lsc[:, 5:6]

        def load(h):
            i = h % 2
            S.dma("sp", qt_[i][:, :], QT[h * 128:(h + 1) * 128, :], reads=[QTb], writes=[qtb[i]])
            S.dma("sp", kt_[i][:, :], KT[h * 128:(h + 1) * 128, PAD:PAD + T], reads=[KTb], writes=[ktb[i]])
            S.dma("sp", vv[i][:, :, :],
                  V[PAD:PAD + T, h * 128:(h + 1) * 128].rearrange("(b p) f -> p b f", p=128),
                  reads=[Vb], writes=[vvb[i]])
            for j in range(2):
                S.dma("sp", gt[i][:, j, :], gtab[2 * h + j, :, :], pwrites=[gtb[i]],
                      reads=[])
        load(0)
        npt = 0
        nsb = 0
        for h in range(8):
            if h + 1 < 8:
                load(h + 1)
            i = h % 2
            q_, k_, v_, g_ = qt_[i], kt_[i], vv[i], gt[i]
            for qt in range(NQT):
                q0 = qt * TT

                def issue_S(kb):
                    for j in range(2):
                        S.mm([(psS[j][kb % 2][:, :], k_[j * 64:(j + 1) * 64, kb * 128:(kb + 1) * 128],
                               q_[j * 64:(j + 1) * 64, q0:q0 + TT])],
                             reads=[qtb[i], ktb[i]], writes=[psSb[j][kb % 2]])
                issue_S(0)
                for kb in range(NKB):
                    if kb + 1 < NKB:
                        issue_S(kb + 1)
                    d = kb * 128 - q0
                    pts = []
                    for j in range(2):
                        sp_, spb = psS[j][kb % 2], psSb[j][kb % 2]
                        p_, p_b = pt[npt % 8], ptb[npt % 8]
                        npt += 1
                        if NEAR_LO < d < NEAR_HI:
                            m0 = GC - d
                            s_, s_b = sbi[nsb % 4], sbib[nsb % 4]
                            nsb += 1
                            S.op("dve", lambda e, s_=s_, sp_=sp_, g_=g_, j=j, m0=m0: e.tensor_tensor(
                                out=s_[:, :], in0=sp_[:, :], in1=g_[:, j, m0:m0 + TT], op=ALU.add),
                                reads=[spb, gtb[i]], writes=[s_b])
                            S.op("act", lambda e, p_=p_, s_=s_: e.activation(
                                out=p_[:, :], in_=s_[:, :], func=AF.Exp), reads=[s_b], writes=[p_b])
                        else:
                            col = (16 if d > 0 else 0) + 2 * h + j
                            S.op("act", lambda e, p_=p_, sp_=sp_, col=col: e.activation(
                                out=p_[:, :], in_=sp_[:, :], func=AF.Exp, bias=bconst[:, col:col + 1]),
                                reads=[spb, bcb], writes=[p_b])
                        pts.append((p_, p_b))
                    for j in range(2):
                        p_, p_b = pts[j]
                        S.mm([(PV[j][:, :], v_[:, kb, :], p_[:, :])], reads=[p_b, vvb[i]],
                             writes=[PVb[j]] if kb == 0 else [], pwrites=[] if kb == 0 else [PVb[j]],
                             start=(kb == 0), stop=(kb == NKB - 1))
                    for j in range(2):
                        p_, p_b = pts[j]
                        S.mm([(Z[j][:, :], ones[:, :], p_[:, :])], reads=[p_b, onesb],
                             writes=[Zb[j]] if kb == 0 else [], pwrites=[] if kb == 0 else [Zb[j]],
                             start=(kb == 0), stop=(kb == NKB - 1))
                    if deferred and kb >= 1:
                        deferred.pop(0)(kb)
                while deferred:
                    deferred.pop(0)(NKB - 1)
                S.op("act", lambda e: e.activation(out=zs[0][:, :], in_=Z[0][:, :], func=AF.Copy),
                     reads=[Zb[0]], writes=[zsb[0]])
                S.op("dve", lambda e: e.tensor_copy(out=pvs[0][:, :], in_=PV[0][:, :]),
                     reads=[PVb[0]], writes=[pvsb[0]])
                S.op("act", lambda e: e.activation(out=zs[1][:, :], in_=Z[1][:, :], func=AF.Copy),
                     reads=[Zb[1]], writes=[zsb[1]])
                S.op("dve", lambda e: e.tensor_copy(out=pvs[1][:, :], in_=PV[1][:, :]),
                     reads=[PVb[1]], writes=[pvsb[1]])
                r0, r1, o0, t1, o = ef
                os_, os_b = ost[qt % 2], ostb[qt % 2]
                dst = OT[h * 128:(h + 1) * 128, q0:q0 + TT]

                def mk(os_=os_, os_b=os_b, dst=dst):
                    ops = []
                    ops.append(lambda kb: S.op("dve", lambda e: e.reciprocal(out=r0[:, :], in_=zs[0][:, :]),
                                               reads=[zsb[0]], writes=[efb[0]]))
                    ops.append(lambda kb: S.op("dve", lambda e: e.reciprocal(out=r1[:, :], in_=zs[1][:, :]),
                                               reads=[zsb[1]], writes=[efb[1]]))
                    ops.append(lambda kb: S.op("dve", lambda e: e.tensor_tensor(
                        out=o0[:, :], in0=pvs[0][:, :], in1=r0[:, :], op=ALU.mult),
                        reads=[pvsb[0], efb[0]], writes=[efb[2]]))
                    ops.append(lambda kb: S.op("dve", lambda e: e.tensor_tensor(
                        out=t1[:, :], in0=pvs[1][:, :], in1=r1[:, :], op=ALU.mult),
                        reads=[pvsb[1], efb[1]], writes=[efb[3]]))
                    ops.append(lambda kb: S.op("dve", lambda e: e.scalar_tensor_tensor(
                        out=o[:, :], in0=t1[:, :], scalar=neglam, in1=o0[:, :], op0=ALU.mult, op1=ALU.add),
                        reads=[efb[2], efb[3], lscb], writes=[efb[4]]))
                    ops.append(lambda kb: S.op("act", lambda e: e.activation(
                        out=osq[:, :], in_=o[:, :], func=AF.Square), reads=[efb[4]], writes=[osqb]))

                    def ones_mm(kb):
                        pm, pmb = psS[0][kb % 2], psSb[0][kb % 2]
                        S.mm([(pm[:, :], ones[:, :], osq[:, :])], reads=[osqb, onesb], writes=[pmb])
                        S.op("act", lambda e, pm=pm: e.activation(out=r0[:, :], in_=pm[:, :], func=AF.Sqrt,
                                                                  scale=1.0 / 128, bias=EPS),
                             reads=[pmb], writes=[efb[0]])
                    ops.append(ones_mm)
                    ops.append(lambda kb: S.op("dve", lambda e: e.reciprocal(out=r0[:, :], in_=r0[:, :]),
                                               reads=[efb[0]], writes=[efb[0]]))
                    ops.append(lambda kb: S.op("dve", lambda e: e.scalar_tensor_tensor(
                        out=os_[:, :], in0=o[:, :], scalar=gsub[:, 1:2], in1=r0[:, :],
                        op0=ALU.mult, op1=ALU.mult), reads=[efb[4], efb[0], gsubb], writes=[os_b]))
                    ops.append(lambda kb: S.dma("sp", dst, os_[:, :], reads=[os_b], pwrites=[OTb]))
                    return ops
                deferred = mk()
        while deferred:
            deferred.pop(0)(NKB - 1)
        S.flush()


def phase_qkv_c(nc, S, T, hT, hTb, w_bf, wb, g_dram, QT, QTb, KT, KTb, V, Vb,
                ctab_d, stab_d, pmat_d, qkg_d, h_in=None, h_inb=None):
    ntile = T // TT
    with ExitStack() as st:
        sb = lambda name, shape, dt: st.enter_context(nc.sbuf_tensor(_uid(name), shape, dt))
        w = sb("wqkv", [128, NCH, 1536], BF16)
        hx = [sb(f"hx{i}", [128, NCH, TT], F32) for i in range(2)]
        sq = [sb(f"sq{i}", [128, NCH, TT], BF16) for i in range(2)]
        hn = [sb(f"hn{i}", [128, NCH, TT], BF16) for i in range(2)]
        rstd = sb("rstd", [128, TT], F32)
        gcol = sb("gcol", [128, NCH], F32)
        ones = sb("ones", [128, 128], BF16)
        oblk = sb("oblk", [128, 128], BF16)
        pm32 = sb("pm32", [128, 128], F32)
        pmat = sb("pmat", [128, 128], BF16)
        qkg = sb("qkg", [128, 2], F32)
        ctab = sb("ctab", [128, T], F32)
        stab = sb("stab", [128, T], F32)
        sq2 = [sb(f"sq2{i}", [128, TT], BF16) for i in range(2)]
        qg = [sb(f"qg{i}", [128, TT], BF16) for i in range(2)]
        qgf = [sb(f"qgf{i}", [128, TT], F32) for i in range(2)]
        qgfb = [Buf("qgf0"), Buf("qgf1")]
        rs = [sb(f"rs{i}", [128, TT], F32) for i in range(2)]
        t1 = [sb(f"t1{i}", [128, TT], F32) for i in range(2)]
        t2 = [sb(f"t2{i}", [128, TT], F32) for i in range(2)]
        qk = [sb(f"qk{i}", [128, 10, TT], BF16) for i in range(2)]
        vs = [sb(f"vs{i}", [128, 4, 256], BF16) for i in range(2)]
        ps = [st.enter_context(nc.psum_tensor(_uid(f"ps{i}"), [128, TT], F32)) for i in range(8)]
        B = Buf
        wbuf, sqb, hnb, rstdb, gcolb, onesb, oblkb, pm32b, pmatb, qkgb, ctb, stb = (
            B("w"), B("sq"), B("hn"), B("rstd"), B("gcol"), B("ones"), B("oblk"), B("pm32"), B("pmat"),
            B("qkg"), B("ct"), B("st"))
        hxb = [B("hx0"), B("hx1")]
        sqb = [B("sq0"), B("sq1")]
        hnb = [B("hn0"), B("hn1")]
        sq2b = [B("a"), B("b")]
        qgb = [B("a"), B("b")]
        rsb = [B("a"), B("b")]
        t1b = [B("a"), B("b")]
        t2b = [B("a"), B("b")]
        qkb = [B("qk0"), B("qk1")]
        vsb = [B("vs0"), B("vs1")]
        psb = [B(f"ps{i}") for i in range(8)]
        S.op(PMS, lambda e: e.memset(ones[:, :], 1.0), writes=[onesb])
        S.op(PMS, lambda e: e.memset(oblk[:, :], 0.0), writes=[oblkb])
        if _os.environ.get("K_C2", "0") != "1":
            S.op(PMS, lambda e: e.memset(oblk[0:64, 0:64], 1.0), writes=[oblkb])
            S.op(PMS, lambda e: e.memset(oblk[64:128, 64:128], 1.0), writes=[oblkb])
        S.dma("sp", gcol[:, :], g_dram, writes=[gcolb])
        if _os.environ.get("K_C3", "0") == "2":
            S.dma("sp", pm32[:, :], pmat_d, writes=[pm32b])
            S.op("dve", lambda e: e.memset(pmat[:, :], 1.0), writes=[pmatb])
        elif _os.environ.get("K_C3", "0") != "1":
            S.dma("sp", pm32[:, :], pmat_d, writes=[pm32b])
            S.op("act", lambda e: e.activation(out=pmat[:, :], in_=pm32[:, :], func=AF.Copy),
                 reads=[pm32b], writes=[pmatb])
        else:
            S.op("dve", lambda e: e.memset(pmat[:, :], 1.0), writes=[pmatb])
        if _os.environ.get("K_C4", "0") != "1":
            S.dma("sp", qkg[:, :], qkg_d, writes=[qkgb])
        if _os.environ.get("K_C5", "0") != "1":
            S.dma("sp", ctab[:, :], ctab_d, writes=[ctb])
            S.dma("sp", stab[:, :], stab_d, writes=[stb])
        def load(t):
            S.dma("sp", hx[t % 2][:, :, :],
                  (hT if h_in is None else h_in)[:, t * TT:(t + 1) * TT].rearrange("(k p) n -> p k n", p=128),
                  reads=[(hTb if h_in is None else h_inb)[t]], writes=[hxb[t % 2]])
        load(0)
        for k in range(NCH):
            S.dma("sp", w[:, k, :], w_bf[k * 128:(k + 1) * 128, :], reads=[wb[k]], pwrites=[wbuf])

        n = 0
        def norm(t):
            emit_norm(S, t, hx[t % 2], hxb[t % 2], sq[t % 2], sqb[t % 2], hn[t % 2], hnb[t % 2], gcol, ones,
                      ps[7], psb[7], rstd, rstdb, gcolb, onesb)
        for t in range(ntile):
            if t + 1 < ntile:
                load(t + 1)
            x, xb = hx[t % 2], hxb[t % 2]
            if t == 0:
                norm(0)
            hn_, hn_b = hn[t % 2], hnb[t % 2]
            q_, q_b = qk[t % 2], qkb[t % 2]
            t0 = t * TT
            for c in range(10):
                if c == 3 and t + 1 < ntile:
                    norm(t + 1)
                isq = c < 8
                pA, pAb = ps[(2 * c) % 4], psb[(2 * c) % 4]
                pB, pBb = ps[(2 * c) % 4 + 1], psb[(2 * c) % 4 + 1]
                pC, pCb = ps[4 + c % 2], psb[4 + c % 2]
                i = n % 2
                n += 1
                gq = qkg[:, 0:1] if isq else qkg[:, 1:2]
                _cn = int(_os.environ.get('K_CN', '99'))
                if _cn > 0:
                    S.mm([(pA[:, :], w[:, k, c * 128:(c + 1) * 128], hn_[:, k, :]) for k in range(NCH)],
                         reads=[hn_b, wbuf], writes=[pAb])
                if _cn > 1:
                    S.op("act", lambda e, pA=pA, i=i: e.activation(out=sq2[i][:, :], in_=pA[:, :], func=AF.Square),
                         reads=[pAb], writes=[sq2b[i]])
                if _cn > 2:
                    S.op("act", lambda e, pA=pA, i=i, gq=gq: e.activation(
                        out=qgf[i][:, :], in_=pA[:, :], func=AF.Copy, scale=gq),
                        reads=[pAb, qkgb], writes=[qgfb[i]])
                    S.op("pool", lambda e, i=i: e.tensor_copy(out=qg[i][:, :], in_=qgf[i][:, :]),
                         reads=[qgfb[i]], writes=[qgb[i]])
                if _cn > 3:
                    S.mm([(pB[:, :], oblk[:, :], sq2[i][:, :])], reads=[sq2b[i], oblkb], writes=[pBb])
                if _cn > 4:
                    S.mm([(pC[:, :], pmat[:, :], qg[i][:, :])], reads=[qgb[i], pmatb], writes=[pCb])
                if _cn > 5:
                    S.op("act", lambda e, pB=pB, i=i: e.activation(out=rs[i][:, :], in_=pB[:, :], func=AF.Sqrt,
                                                                  scale=1.0 / 64, bias=EPS),
                         reads=[pBb], writes=[rsb[i]])
                if _cn > 6:
                    S.op("dve", lambda e, i=i: e.reciprocal(out=rs[i][:, :], in_=rs[i][:, :]),
                         reads=[rsb[i]], writes=[rsb[i]])
                if _cn > 7:
                    S.op("pool", lambda e, i=i, t0=t0: e.tensor_tensor(
                        out=t1[i][:, :], in0=qgf[i][:, :], in1=ctab[:, t0:t0 + TT], op=ALU.mult),
                        reads=[qgfb[i], ctb], writes=[t1b[i]])
                if _cn > 8:
                    S.op("dve", lambda e, pC=pC, i=i, t0=t0: e.tensor_tensor(
                        out=t2[i][:, :], in0=pC[:, :], in1=stab[:, t0:t0 + TT], op=ALU.mult),
                        reads=[pCb, stb], writes=[t2b[i]])
                if _cn > 9:
                    S.op(PTT, lambda e, i=i: e.tensor_tensor(out=t1[i][:, :], in0=t1[i][:, :], in1=t2[i][:, :],
                                                               op=ALU.add),
                         reads=[t1b[i], t2b[i]], writes=[t1b[i]])
                if _cn > 10:
                    S.op("dve", lambda e, i=i, q_=q_, c=c, isq=isq: e.scalar_tensor_tensor(
                        out=q_[:, c, :], in0=t1[i][:, :], scalar=(0.125 if isq else 1.0), in1=rs[i][:, :],
                        op0=ALU.mult, op1=ALU.mult), reads=[t1b[i], rsb[i]], pwrites=[q_b])
            if _os.environ.get("K_C6", "0") != "1":
                S.dma("sp", QT[:, t0:t0 + TT].rearrange("(c p) n -> p c n", p=128),
                      q_[:, 0:8, :], reads=[q_b], pwrites=[QTb])
                S.dma("sp", KT[0:256, PAD + t0:PAD + t0 + TT].rearrange("(c p) n -> p c n", p=128),
                      q_[:, 8:10, :], reads=[q_b], pwrites=[KTb])
            v_, v_b = vs[t % 2], vsb[t % 2]
            for tb in range(4 if _os.environ.get("K_C1", "0") != "1" else 0):
                p, pb = ps[6], psb[6]
                S.mm([(p[:, 0:256], hn_[:, k, tb * 128:(tb + 1) * 128], w[:, k, 1280:1536])
                      for k in range(NCH)], reads=[hn_b, wbuf], writes=[pb])
                S.op("act", lambda e, p=p, v_=v_, tb=tb: e.activation(
                    out=v_[:, tb, :], in_=p[:, 0:256], func=AF.Copy), reads=[pb], pwrites=[v_b])
            if _os.environ.get("K_C6", "0") != "1":
                S.dma("sp", V[PAD + t0:PAD + t0 + TT, 0:256].rearrange("(b p) f -> p b f", p=128),
                      v_[:, :, :], reads=[v_b], pwrites=[Vb])
            S._commit((S.sem["pe"], S.cnt["pe"]), [hn_b, sqb[t % 2]], [])
        S.flush()


def phase_attn_c(nc, S, T, QT, QTb, KT, KTb, V, Vb, OT, OTb):
    NKB, NQT = T // 128, T // TT
    with ExitStack() as st:
        sb = lambda name, shape, dt: st.enter_context(nc.sbuf_tensor(_uid(name), shape, dt))
        qt_ = [sb(f"qt{i}", [128, T], BF16) for i in range(2)]
        k2 = [sb(f"k2{i}", [128, T], BF16) for i in range(2)]
        ve = [sb(f"ve{i}", [128, NKB, 128], BF16) for i in range(2)]
        vo = [sb(f"vo{i}", [128, NKB, 128], BF16) for i in range(2)]
        pt = [sb(f"pt{i}", [128, 2, TT], BF16) for i in range(4)]
        zz = [sb(f"zz{i}", [128, TT], F32) for i in range(2)]
        zzs = [sb(f"zzs{i}", [128, TT], F32) for i in range(2)]
        rzs = [sb(f"rzs{i}", [128, TT], F32) for i in range(2)]
        ost = [sb(f"ost{i}", [128, TT], BF16) for i in range(2)]
        sS = [st.enter_context(nc.psum_tensor(_uid(f"sS{i}"), [128, 2, TT], F32)) for i in range(2)]
        E = [st.enter_context(nc.psum_tensor(_uid(f"E{i}"), [128, TT], F32)) for i in range(2)]
        O = [st.enter_context(nc.psum_tensor(_uid(f"O{i}"), [128, TT], F32)) for i in range(2)]
        B = Buf
        qtb = [B("qt0"), B("qt1")]
        k2b = [B("k0"), B("k1")]
        veb = [B("a"), B("b")]
        vob = [B("a"), B("b")]
        ptb = [B(f"pt{i}") for i in range(4)]
        zzb = [B("a"), B("b")]
        zzsb = [B("a"), B("b")]
        rzsb = [B("a"), B("b")]
        ostb = [B("a"), B("b")]
        sSb = [B("s0"), B("s1")]
        Eb = [B("e0"), B("e1")]
        Ob = [B("o0"), B("o1")]
        for i in range(2):
            S.op(PMS, lambda e, i=i: e.memset(ve[i][:, :, 64:128], 1.0), writes=[veb[i]])
            S.op(PMS, lambda e, i=i: e.memset(vo[i][:, :, 0:64], 1.0), writes=[vob[i]])

        def load(c):
            i = c % 2
            g = c // 2
            S.dma("sp", qt_[i][:, :], QT[c * 128:(c + 1) * 128, :], reads=[QTb], writes=[qtb[i]])
            S.dma("sp", k2[i][0:64, :], KT[g * 64:(g + 1) * 64, PAD:PAD + T], reads=[KTb], writes=[k2b[i]])
            S.dma("sp", k2[i][64:128, :], KT[g * 64:(g + 1) * 64, PAD:PAD + T], reads=[KTb], pwrites=[k2b[i]])
            vsrc = V[PAD:PAD + T, g * 64:(g + 1) * 64].rearrange("(b p) f -> p b f", p=128)
            S.dma("sp", ve[i][:, :, 0:64], vsrc, reads=[Vb, veb[i]], pwrites=[veb[i]])
            S.dma("sp", vo[i][:, :, 64:128], vsrc, reads=[Vb, vob[i]], pwrites=[vob[i]])
        load(0)
        npt = 0
        nq = 0
        for c in range(8):
            if c + 1 < 8:
                load(c + 1)
            i = c % 2
            q_, k_ = qt_[i], k2[i]
            for qt in range(NQT):
                q0 = qt * TT
                a = nq % 2
                nq += 1

                def issue_S(kb):
                    x = kb % 2
                    for j in range(2):
                        S.mm([(sS[x][:, j, :], k_[j * 64:(j + 1) * 64, kb * 128:(kb + 1) * 128],
                               q_[j * 64:(j + 1) * 64, q0:q0 + TT])],
                             reads=[qtb[i], k2b[i]],
                             writes=[sSb[x]] if j == 0 else [], pwrites=[] if j == 0 else [sSb[x]])
                issue_S(0)
                for kb in range(NKB):
                    if kb + 1 < NKB:
                        issue_S(kb + 1)
                    x = kb % 2
                    n = npt % 4
                    npt += 1
                    S.op("act", lambda e, n=n, x=x: e.activation(
                        out=pt[n][:, :, :], in_=sS[x][:, :, :], func=AF.Exp), reads=[sSb[x]], writes=[ptb[n]])
                    first, last = (kb == 0), (kb == NKB - 1)
                    S.mm([(E[a][:, :], ve[i][:, kb, :], pt[n][:, 0, :])], reads=[ptb[n], veb[i]],
                         writes=[Eb[a]] if first else [], pwrites=[] if first else [Eb[a]],
                         start=first, stop=last)
                    S.mm([(O[a][:, :], vo[i][:, kb, :], pt[n][:, 1, :])], reads=[ptb[n], vob[i]],
                         writes=[Ob[a]] if first else [], pwrites=[] if first else [Ob[a]],
                         start=first, stop=last)
                S.op("dve", lambda e, a=a: e.tensor_copy(out=zz[a][0:64, :], in_=O[a][0:64, :]),
                     reads=[Ob[a]], writes=[zzb[a]])
                S.op("dve", lambda e, a=a: e.tensor_copy(out=zz[a][64:128, :], in_=E[a][64:128, :]),
                     reads=[Eb[a]], pwrites=[zzb[a]])
                S.dma("sp", zzs[a][0:64, :], zz[a][64:128, :], reads=[zzb[a]], writes=[zzsb[a]])
                S.dma("sp", zzs[a][64:128, :], zz[a][0:64, :], reads=[zzb[a]], pwrites=[zzsb[a]])
                S.op("dve", lambda e, a=a: e.reciprocal(out=rzs[a][:, :], in_=zzs[a][:, :]),
                     reads=[zzsb[a]], writes=[rzsb[a]])
                S.op("dve", lambda e, a=a: e.tensor_tensor(out=ost[a][0:64, :], in0=E[a][0:64, :],
                                                          in1=rzs[a][0:64, :], op=ALU.mult),
                     reads=[Eb[a], rzsb[a]], writes=[ostb[a]])
                S.op("dve", lambda e, a=a: e.tensor_tensor(out=ost[a][64:128, :], in0=O[a][64:128, :],
                                                          in1=rzs[a][64:128, :], op=ALU.mult),
                     reads=[Ob[a], rzsb[a]], pwrites=[ostb[a]])
                S.dma("sp", OT[c * 128:(c + 1) * 128, q0:q0 + TT], ost[a][:, :], reads=[ostb[a]], pwrites=[OTb])
        S.flush()


A_DIL = [1] * 6 + [4] * 5 + [16] * 5
A_GRP = [0] * 6 + [1] * 5 + [2] * 5
A_NH = [6, 5, 5]


def phase_attn_a(nc, S, T, QT, QTb, KT, KTb, V, Vb, PVun, PVb_d, Zbc, Zb_d, bm_d):
    with ExitStack() as st:
        sb = lambda name, shape, dt: st.enter_context(nc.sbuf_tensor(_uid(name), shape, dt))
        NBMAX = T // 128 + 16
        qa = [sb(f"qa{i}", [128, T], BF16) for i in range(2)]
        ka = [sb(f"ka{i}", [128, T + 2 * PAD], BF16) for i in range(2)]
        va = [[sb(f"va{i}{j}", [128, NBMAX, 128], BF16) for j in range(2)] for i in range(2)]
        bm = [sb(f"bm{i}", [128, 2, 4, 128], F32) for i in range(2)]
        pvb = [sb(f"pvb{i}", [128, T], F32) for i in range(2)]
        zb = [sb(f"zb{i}", [128, T], F32) for i in range(2)]
        sbi = [sb(f"sbi{i}", [128, 128], F32) for i in range(4)]
        pt = [sb(f"pt{i}", [128, 128], BF16) for i in range(8)]
        olo = sb("olo", [128, 128], BF16)
        ohi = sb("ohi", [128, 128], BF16)
        ps = [st.enter_context(nc.psum_tensor(_uid(f"ps{i}"), [128, TT], F32)) for i in range(8)]
        B = Buf
        qab = [B("a"), B("b")]
        kab = [B("a"), B("b")]
        vab = [[B("a"), B("b")], [B("c"), B("d")]]
        bmb = [B("a"), B("b")]
        pvbb = [B("a"), B("b")]
        zbb = [B("a"), B("b")]
        sbib = [B("s") for _ in range(4)]
        ptb = [B("p") for _ in range(8)]
        olob, ohib = B("olo"), B("ohi")
        psb = [B(f"ps{i}") for i in range(8)]
        S.op(PMS, lambda e: e.memset(olo[:, :], 0.0), writes=[olob])
        S.op(PMS, lambda e: e.memset(olo[:, 0:64], 1.0), writes=[olob])
        S.op(PMS, lambda e: e.memset(ohi[:, :], 0.0), writes=[ohib])
        S.op(PMS, lambda e: e.memset(ohi[:, 64:128], 1.0), writes=[ohib])
        for i in range(2):
            for j in range(2):
                S.op(PMS, lambda e, i=i, j=j: e.memset(va[i][j][:, :, :], 0.0), writes=[vab[i][j]])
        on = [olo, ohi]
        onb = [olob, ohib]

        def load(c):
            i = c % 2
            S.dma("sp", qa[i][:, :], QT[c * 128:(c + 1) * 128, :], reads=[QTb], writes=[qab[i]])
            S.dma("sp", ka[i][:, :], KT[c * 128:(c + 1) * 128, :], reads=[KTb], writes=[kab[i]])
            S.dma("sp", bm[i][:, :, :, :], bm_d[c, :, :].rearrange("p (s v j) -> p s v j", s=2, v=4),
                  writes=[bmb[i]])
            for s_ in range(2):
                h = 2 * c + s_
                d = A_DIL[h]
                nbl = T // (128 * d) + 1
                for r in range(d):
                    start = PAD + r - 64 * d
                    src = V[start:start + d * (128 * nbl - 1) + 1:d, h * 64:(h + 1) * 64].rearrange(
                        "(m i) f -> i m f", i=128)
                    S.dma("sp", va[i][s_][:, r * nbl:(r + 1) * nbl, s_ * 64:(s_ + 1) * 64], src,
                          reads=[Vb, vab[i][s_]], pwrites=[vab[i][s_]])
        load(0)
        nn = 0
        for c in range(8):
            if c + 1 < 8:
                load(c + 1)
            i = c % 2
            groups = []
            for s_ in range(2):
                h = 2 * c + s_
                d = A_DIL[h]
                nq = (T // d) // 128
                for r in range(d):
                    for b in range(nq):
                        groups.append((s_, d, nq, r, b))

            def stage1(g):
                nonlocal nn
                s_, d, nq, r, b = g
                nbl = nq + 1
                rows = slice(s_ * 64, (s_ + 1) * 64)
                qsl = slice(r + 128 * d * b, r + 128 * d * b + 127 * d + 1, d)
                pts = []
                for mi in range(2):
                    m = b + mi
                    k0 = PAD + r - 64 * d + 128 * d * m
                    ksl = slice(k0, k0 + 127 * d + 1, d)
                    if mi == 0:
                        var = 1 if b == 0 else 0
                    else:
                        var = 3 if b == nq - 1 else 2
                    n4, n8 = nn % 4, nn % 8
                    nn += 1
                    sp_, spb = ps[n4], psb[n4]
                    S.mm([(sp_[:, 0:128], ka[i][rows, ksl], qa[i][rows, qsl])],
                         reads=[kab[i], qab[i]], writes=[spb])
                    S.op("dve", lambda e, n4=n4, sp_=sp_, s_=s_, var=var, i=i: e.tensor_tensor(
                        out=sbi[n4][:, :], in0=sp_[:, 0:128], in1=bm[i][:, s_, var, :], op=ALU.add),
                        reads=[spb, bmb[i]], writes=[sbib[n4]])
                    S.op("act", lambda e, n4=n4, n8=n8: e.activation(
                        out=pt[n8][:, :], in_=sbi[n4][:, :], func=AF.Exp),
                        reads=[sbib[n4]], writes=[ptb[n8]])
                    pts.append((n8, r * nbl + m))
                return pts, (nn // 2) % 2

            def stage2(g, st1):
                s_, d, nq, r, b = g
                pts, a = st1
                rows = slice(s_ * 64, (s_ + 1) * 64)
                qsl = slice(r + 128 * d * b, r + 128 * d * b + 127 * d + 1, d)
                pv_, pv_b = ps[4 + a], psb[4 + a]
                z_, z_b = ps[6 + a], psb[6 + a]
                S.mm([(pv_[:, 0:128], va[i][s_][:, blk, :], pt[n8][:, :]) for (n8, blk) in pts],
                     reads=[ptb[n8] for (n8, _) in pts] + [vab[i][s_]], writes=[pv_b])
                S.mm([(z_[:, 0:128], on[s_][:, :], pt[n8][:, :]) for (n8, blk) in pts],
                     reads=[ptb[n8] for (n8, _) in pts] + [onb[s_]], writes=[z_b])
                S.op("act", lambda e, pv_=pv_, rows=rows, qsl=qsl, i=i: e.activation(
                    out=pvb[i][rows, qsl], in_=pv_[rows, 0:128], func=AF.Copy),
                    reads=[pv_b], pwrites=[pvbb[i]])
                S.op("dve", lambda e, z_=z_, rows=rows, qsl=qsl, i=i: e.tensor_copy(
                    out=zb[i][rows, qsl], in_=z_[rows, 0:128]),
                    reads=[z_b], pwrites=[zbb[i]])

            st = stage1(groups[0])
            for gi, g in enumerate(groups):
                nxt = stage1(groups[gi + 1]) if gi + 1 < len(groups) else None
                stage2(g, st)
                st = nxt
            S.dma("sp", PVun[c * 128:(c + 1) * 128, :], pvb[i][:, :], reads=[pvbb[i]], pwrites=[PVb_d])
            S.dma("sp", Zbc[2 * c:2 * c + 1, :], zb[i][0:1, :], reads=[zbb[i]], pwrites=[Zb_d])
            S.dma("sp", Zbc[2 * c + 1:2 * c + 2, :], zb[i][64:65, :], reads=[zbb[i]], pwrites=[Zb_d])
        S.flush()


def phase_comb_a(nc, S, T, PVun, PVb_d, Zc, Zb_d, OT, OTb, asel_d, bsel_d, esel_d):
    ntile = T // TT
    with ExitStack() as st:
        sb = lambda name, shape, dt: st.enter_context(nc.sbuf_tensor(_uid(name), shape, dt))
        zt = [sb(f"zt{i}", [16, TT], F32) for i in range(2)]
        pv = [sb(f"pv{i}", [128, NCH, TT], F32) for i in range(2)]
        asel = sb("asel", [16, 3], F32)
        bsel = sb("bsel", [3, 16], F32)
        esel = sb("esel", [16, NCH * 128], F32)
        o33 = sb("o33", [3, 3], F32)
        sg = sb("sg", [3, TT], F32)
        rt = sb("rt", [3, TT], F32)
        al = sb("al", [3, TT], F32)
        rz = sb("rz", [16, TT], F32)
        f16 = sb("f16", [16, TT], F32)
        ost = [sb(f"ost{i}", [128, NCH, TT], BF16) for i in range(2)]
        ps = [st.enter_context(nc.psum_tensor(_uid(f"ps{i}"), [128, TT], F32)) for i in range(8)]
        B = Buf
        ztb = [B("a"), B("b")]
        pvb = [B("a"), B("b")]
        aselb, bselb, eselb, o33b, sgb, rtb, alb, rzb, f16b = (B("a"), B("b"), B("e"), B("c"), B("d"), B("e"),
                                                               B("f"), B("g"), B("h"))
        ostb = [B("a"), B("b")]
        psb = [B(f"ps{i}") for i in range(8)]
        S.dma("sp", asel[:, :], asel_d, writes=[aselb])
        S.dma("sp", bsel[:, :], bsel_d, writes=[bselb])
        S.dma("sp", esel[:, :], esel_d, writes=[eselb])
        S.op(PMS, lambda e: e.memset(o33[:, :], 1.0), writes=[o33b])

        def load(t):
            S.dma("sp", zt[t % 2][:, :], Zc[:, t * TT:(t + 1) * TT], reads=[Zb_d], writes=[ztb[t % 2]])
            S.dma("sp", pv[t % 2][:, :, :], PVun[:, t * TT:(t + 1) * TT].rearrange("(k p) n -> p k n", p=128),
                  reads=[PVb_d], writes=[pvb[t % 2]])
        load(0)
        for t in range(ntile):
            if t + 1 < ntile:
                load(t + 1)
            z_, z_b = zt[t % 2], ztb[t % 2]
            p_, p_b = pv[t % 2], pvb[t % 2]
            S.mm([(ps[0][0:3, :], asel[:, :], z_[:, :])], reads=[z_b, aselb], writes=[psb[0]])
            S.op("dve", lambda e: e.tensor_copy(out=sg[:, :], in_=ps[0][0:3, :]), reads=[psb[0]], writes=[sgb])
            S.mm([(ps[1][0:3, :], o33[:, :], sg[:, :])], reads=[sgb, o33b], writes=[psb[1]])
            S.op("dve", lambda e: e.reciprocal(out=rt[:, :], in_=ps[1][0:3, :]), reads=[psb[1]], writes=[rtb])
            S.op("dve", lambda e: e.scalar_tensor_tensor(out=al[:, :], in0=sg[:, :], scalar=3.0, in1=rt[:, :],
                                                         op0=ALU.mult, op1=ALU.mult),
                 reads=[sgb, rtb], writes=[alb])
            S.mm([(ps[2][0:16, :], bsel[:, :], al[:, :])], reads=[alb, bselb], writes=[psb[2]])
            S.op("dve", lambda e, z_=z_: e.reciprocal(out=rz[:, :], in_=z_[:, :]), reads=[z_b], writes=[rzb])
            S.op("dve", lambda e: e.tensor_tensor(out=f16[:, :], in0=ps[2][0:16, :], in1=rz[:, :], op=ALU.mult),
                 reads=[psb[2], rzb], writes=[f16b])
            o_, o_b = ost[t % 2], ostb[t % 2]
            for c in range(NCH):
                pa, pab = ps[3 + c % 4], psb[3 + c % 4]
                S.mm([(pa[:, :], esel[:, c * 128:(c + 1) * 128], f16[:, :])], reads=[f16b, eselb], writes=[pab])
                S.op("dve", lambda e, pa=pa, p_=p_, c=c, o_=o_: e.tensor_tensor(
                    out=o_[:, c, :], in0=pa[:, :], in1=p_[:, c, :], op=ALU.mult),
                    reads=[pab, p_b], pwrites=[o_b])
            S.dma("sp", OT[:, t * TT:(t + 1) * TT].rearrange("(c p) n -> p c n", p=128), o_[:, :, :],
                  reads=[o_b], pwrites=[OTb])
        S.flush()


def lambda_init_fn(layer_idx):
    return 0.8 - 0.6 * math.exp(-0.3 * layer_idx)


def build(T, layers=(0, 1, 2, 3), final=True):
    nc = bass.Bass("TRN2", target_bir_lowering=False)
    ntile = T // TT
    import os
    limit = int(os.environ.get("K_STOP", "1000"))
    count = [0]

    def RUN(fn, *a):
        count[0] += 1
        if count[0] <= limit:
            fn(*a)
    with ExitStack() as st:
        S = Sched(nc, st)

        def ext(name, shape, dtype=F32):
            return nc.dram_tensor(name, list(shape), dtype, kind="ExternalInput").ap()

        def internal(name, shape, dtype):
            return nc.dram_tensor(name, list(shape), dtype, kind="Internal").ap()

        xT = ext("xT", [D, T])
        outT = nc.dram_tensor("outT", [D, T], F32, kind="ExternalOutput").ap()
        kinds = {li % 3 for li in layers}
        W = {"mlp_w_in": ext("mlp_w_in", [4, D, DFF]), "mlp_w_out": ext("mlp_w_out", [4, DFF, D])}
        if 0 in kinds:
            W["a_w_qkv"] = ext("a_w_qkv", [2, D, 3072])
            W["a_w_o"] = ext("a_w_o", [2, D, D])
        if 1 in kinds:
            W["b_w_qkv"] = ext("b_w_qkv", [1, D, 3072])
            W["b_w_o"] = ext("b_w_o", [1, D, D])
        if 2 in kinds:
            W["c_w_qkv"] = ext("c_w_qkv", [1, D, 1536])
            W["c_w_o"] = ext("c_w_o", [1, D, D])
        gmix = ext("gmix", [128, 32])
        gmlp = ext("gmlp", [128, 32])
        gfin = ext("gfin", [128, 8])
        gtab = ext("gtab", [16, 128, GW])
        bconst = ext("bconst", [128, 32])
        lam = ext("lam", [128, 256])
        gsub = ext("gsub", [128, 1])
        ctab = ext("ctab", [128, T])
        stab = ext("stab", [128, T])
        pmat = ext("pmat", [128, 128])
        qkg = ext("qkg", [128, 2])
        bm = ext("bm", [8, 128, 1024])
        asel = ext("asel", [16, 3])
        bsel = ext("bsel", [3, 16])
        esel = ext("esel", [16, 1024])

        hT = internal("hT", [D, T], F32)
        QT = internal("QT", [D, T], BF16)
        KT = internal("KT", [D, T + 2 * PAD], BF16)
        V = internal("V", [T + 2 * PAD, D], BF16)
        OT = internal("OT", [D, T], BF16)
        PVun = internal("PVun", [D, T], F32)
        Zbc = internal("Zbc", [16, T], F32)
        hTb = [Buf(f"hT{t}") for t in range(ntile)]
        QTb, KTb, Vb, OTb, PVb, Zb, outb = (Buf("QT"), Buf("KT"), Buf("V"), Buf("OT"), Buf("PV"),
                                            Buf("Z"), Buf("out"))

        wbf = {}
        castchain = Buf("castchain")
        pending_casts = []

        def emit_casts():
            while pending_casts:
                pending_casts.pop(0)()

        def cast(name, idx, rows, cols):
            src = W[name][idx]
            dst = internal(f"{name}{idx}_bf", [rows, cols], BF16)
            sv, dv = src, dst
            if cols > 2048:
                c = 2048 if cols % 2048 == 0 else 1024
                sv = sv.rearrange("a (b c) -> (a b) c", c=c)
                dv = dv.rearrange("a (b c) -> (a b) c", c=c)
            elif cols == 1024:
                sv = sv.rearrange("(a b) c -> a (b c)", b=2)
                dv = dv.rearrange("(a b) c -> a (b c)", b=2)
            b = Buf(name)
            pending_casts.append(lambda dv=dv, sv=sv, b=b: S.dma("pool", dv, sv, writes=[b, castchain]))
            wbf[(name, idx)] = (dst, [b] * 8)

        first_f32 = (layers[0] % 3) in (0, 1)
        for n_, li in enumerate(layers):
            kind, j = li % 3, li // 3
            pre = "abc"[kind]
            if n_ == 0 and first_f32:
                wbf[(f"{pre}_w_qkv", j)] = (None, None)
            else:
                cast(f"{pre}_w_qkv", j, D, 1536 if kind == 2 else 3072)
            cast(f"{pre}_w_o", j, D, D)
            cast("mlp_w_in", li, D, DFF)
            cast("mlp_w_out", li, DFF, D)

        if not first_f32:
            emit_casts()

        with ExitStack() as st2:
            zt = st2.enter_context(nc.sbuf_tensor(_uid("zt"), [128, NCH, PAD], BF16))
            ztb = Buf("zt")
            S.op(PMS, lambda e: e.memset(zt[:, :, :], 0.0), writes=[ztb])
            S.dma("sp", KT[:, 0:PAD].rearrange("(c p) n -> p c n", p=128), zt[:, :, :], reads=[ztb], pwrites=[KTb])
            S.dma("sp", KT[:, PAD + T:].rearrange("(c p) n -> p c n", p=128), zt[:, :, :], reads=[ztb], pwrites=[KTb])
            S.dma("sp", V[0:PAD, :].rearrange("(b p) f -> p b f", p=128), zt[:, :, :], reads=[ztb], pwrites=[Vb])
            S.dma("sp", V[PAD + T:, :].rearrange("(b p) f -> p b f", p=128), zt[:, :, :], reads=[ztb], pwrites=[Vb])
            S.flush(wait_bufs=[KTb, Vb])
        xTb = [Buf(f"xT{t}") for t in range(ntile)]

        for n_, li in enumerate(layers):
            kind, j = li % 3, li // 3
            pre = "abc"[kind]
            wq, wqb = wbf[(f"{pre}_w_qkv", j)]
            wf32 = W[f"{pre}_w_qkv"][j] if (n_ == 0 and first_f32) else None
            hin = (xT, xTb) if n_ == 0 else (None, None)
            wo, wob = wbf[(f"{pre}_w_o", j)]
            wi, wib = wbf[("mlp_w_in", li)]
            wo2, wo2b = wbf[("mlp_w_out", li)]
            g1 = gmix[:, 8 * li:8 * li + 8]
            g2 = gmlp[:, 8 * li:8 * li + 8]
            if kind == 0:
                RUN(phase_qkv_ab, nc, S, T, hT, hTb, wq, wqb, g1, QT, QTb, KT, KTb, V, Vb, wf32, *hin)
                emit_casts()
                RUN(phase_attn_a, nc, S, T, QT, QTb, KT, KTb, V, Vb, PVun, PVb, Zbc, Zb, bm)
                RUN(phase_comb_a, nc, S, T, PVun, PVb, Zbc, Zb, OT, OTb, asel, bsel, esel)
            elif kind == 1:
                RUN(phase_qkv_ab, nc, S, T, hT, hTb, wq, wqb, g1, QT, QTb, KT, KTb, V, Vb, wf32, *hin)
                emit_casts()
                RUN(phase_attn_b, nc, S, T, QT, QTb, KT, KTb, V, Vb, OT, OTb, gtab, bconst, lam, gsub,
                             lambda_init_fn(li))
            else:
                RUN(phase_qkv_c, nc, S, T, hT, hTb, wq, wqb, g1, QT, QTb, KT, KTb, V, Vb, ctab, stab, pmat, qkg, *hin)
                RUN(phase_attn_c, nc, S, T, QT, QTb, KT, KTb, V, Vb, OT, OTb)
            RUN(phase_wo, nc, S, T, OT, OTb, wo, wob, hT, hTb, *hin)
            RUN(phase_mlp, nc, S, T, hT, hTb, wi, wo2, [wib, wo2b], g2)
        if final:
            RUN(phase_final, nc, S, T, hT, hTb, gfin, outT, outb)
        else:
            with ExitStack() as st2:
                for t in range(ntile):
                    S.dma("sp", outT[:, t * TT:(t + 1) * TT], hT[:, t * TT:(t + 1) * TT], reads=[hTb[t]],
                          pwrites=[outb])
                S.flush(wait_bufs=[outb], wait_all=True)
    return nc


def t5_bucket_np(rel):
    rel = np.asarray(rel, dtype=np.int64)
    nb = 16
    max_exact = 8
    side = np.where(rel > 0, nb, 0)
    n = np.abs(rel)
    nf = np.maximum(n, 1).astype(np.float32)
    large = max_exact + (np.log(nf / np.float32(max_exact)) / np.float32(math.log(1024 / max_exact))
                         * np.float32(nb - max_exact)).astype(np.int32)
    large = np.minimum(large, nb - 1)
    return (side + np.where(n < max_exact, n, large)).astype(np.int64)


def host_tables(T, inputs):
    f32 = np.float32
    rb = np.asarray(inputs["rel_bias"], f32)
    tabs = {}
    i = np.arange(128)[:, None]
    col = np.arange(GW)[None, :]
    bk = t5_bucket_np(i - col + GC)
    tabs["gtab"] = np.ascontiguousarray(np.transpose(rb[bk], (2, 0, 1)))
    bc = np.concatenate([rb[15], rb[31]])[None, :]
    tabs["bconst"] = np.ascontiguousarray(np.repeat(bc, 128, axis=0))
    tabs["lam"] = np.ascontiguousarray(np.repeat(np.asarray(inputs["b_lambda"], f32).reshape(1, 256), 128, 0))
    tabs["gsub"] = np.ascontiguousarray(np.asarray(inputs["b_subln_g"], f32).reshape(128, 1))
    NEG = f32(-30000.0)
    bm = np.zeros((16, 4, 128, 128), f32)
    ii = np.arange(128)[:, None]
    jj = np.arange(128)[None, :]
    for h in range(16):
        d = A_DIL[h]
        o0 = ii - 64 - jj
        o1 = ii + 64 - jj
        b0 = rb[t5_bucket_np(o0 * d), h]
        b1 = rb[t5_bucket_np(o1 * d), h]
        v0 = ii >= jj
        v1 = ii <= jj
        bm[h, 0] = np.where(v0, b0, NEG)
        bm[h, 1] = np.where(v0 & (ii >= 64), b0, NEG)
        bm[h, 2] = np.where(v1, b1, NEG)
        bm[h, 3] = np.where(v1 & (ii < 64), b1, NEG)
    bm = bm.reshape(8, 2, 4, 128, 128).transpose(0, 3, 1, 2, 4).reshape(8, 128, 1024)
    tabs["bm"] = np.ascontiguousarray(bm)
    asel = np.zeros((16, 3), f32)
    bsel = np.zeros((3, 16), f32)
    esel = np.zeros((16, 1024), f32)
    for h in range(16):
        g = A_GRP[h]
        asel[h, g] = 1.0 / A_NH[g]
        bsel[g, h] = 1.0
        esel[h, h * 64:(h + 1) * 64] = 1.0
    tabs["asel"], tabs["bsel"], tabs["esel"] = asel, bsel, esel
    pos = np.arange(T)
    row = (pos // 64).astype(f32)
    colp = (pos % 64).astype(f32)
    inv = (f32(10000.0) ** (-np.arange(16, dtype=f32) / f32(16))).astype(f32)
    ang = np.concatenate([row[:, None] * inv, colp[:, None] * inv], axis=-1).astype(f32)
    cos, sin = np.cos(ang).astype(f32), np.sin(ang).astype(f32)
    ct = np.zeros((64, T), f32)
    stt = np.zeros((64, T), f32)
    pm = np.zeros((128, 128), f32)
    for a in range(2):
        for jh in range(2):
            for f in range(16):
                dd = a * 32 + jh * 16 + f
                ct[dd] = cos[:, a * 16 + f]
                stt[dd] = (-sin[:, a * 16 + f]) if jh == 0 else sin[:, a * 16 + f]
                other = a * 32 + (1 - jh) * 16 + f
                for hh in range(2):
                    pm[hh * 64 + other, hh * 64 + dd] = 1.0
    tabs["ctab"] = np.ascontiguousarray(np.concatenate([ct, ct], 0))
    tabs["stab"] = np.ascontiguousarray(np.concatenate([stt, stt], 0))
    tabs["pmat"] = pm
    qg = np.asarray(inputs["c_q_norm_g"], f32).reshape(64)
    kg = np.asarray(inputs["c_k_norm_g"], f32).reshape(64)
    tabs["qkg"] = np.ascontiguousarray(np.stack([np.tile(qg, 2), np.tile(kg, 2)], axis=1))
    def gl(a):
        a = np.asarray(a, f32).reshape(-1, 8, 128)
        return np.ascontiguousarray(a.transpose(2, 0, 1).reshape(128, -1))
    tabs["gmix"] = gl(inputs["norm_mix_g"])
    tabs["gmlp"] = gl(inputs["norm_mlp_g"])
    tabs["gfin"] = gl(inputs["norm_final_g"])
    for k in ("a_w_qkv", "a_w_o", "b_w_qkv", "b_w_o", "c_w_qkv", "c_w_o", "mlp_w_in", "mlp_w_out"):
        tabs[k] = np.ascontiguousarray(np.asarray(inputs[k], f32))
    return tabs


_PROGRAM_CACHE = {}


def kernel(**inputs):
    x = np.asarray(inputs["x"], np.float32)
    Bn, T, _ = x.shape
    key = (T,)
    if key not in _PROGRAM_CACHE:
        _PROGRAM_CACHE[key] = build(T)
    nc = _PROGRAM_CACHE[key]
    tabs = host_tables(T, inputs)
    in_maps = []
    for c in range(8):
        m = dict(tabs)
        m["xT"] = np.ascontiguousarray(x[c % Bn].T)
        in_maps.append(m)
    res = run_bass_kernel_spmd(nc, in_maps, core_ids=list(range(8)))
    out = np.stack([np.ascontiguousarray(res.results[b]["outT"].T) for b in range(Bn)], axis=0)
    return out.astype(np.float32)
```
